# Optimizing a Trainium2 kernel written in Bass

```python
import math
import jax, jax.numpy as jnp
from jax import lax
import numpy as np

D_MODEL = 1024
BATCH = 8
SEQ = 2048
DEPTH = 1

HEAD_DIM = 64
ATT_WIDTH = D_MODEL // 2
RWKV_WIDTH = D_MODEL - ATT_WIDTH
ATT_HEADS = ATT_WIDTH // HEAD_DIM
RWKV_HEADS = RWKV_WIDTH // HEAD_DIM
MIX_WIDTH = ATT_WIDTH + RWKV_WIDTH
DILATED_PATTERNS = ((128, 1), (512, 4), (2048, 16))
DECAY_LORA = 64
AAA_LORA = 64
GATE_LORA = 128
SHIFT_WIDTH = 3 * RWKV_WIDTH + DECAY_LORA + AAA_LORA + GATE_LORA
IN_WIDTH = 3 * ATT_WIDTH + SHIFT_WIDTH
PEER_N_KEYS = 128
PEER_N_EXPERTS = PEER_N_KEYS * PEER_N_KEYS
PEER_HEADS = 8
PEER_KEY_DIM = 256
PEER_TOPK = 16
PEER_TOKEN_BLOCK = 128
NORM_EPS = 1e-6
LNX_EPS = 64e-5
MASK_VALUE = -1e30

kernel_name = "hybrid_dilated_attn_rwkv7_peer_block"


def rms_norm(x, g, eps=NORM_EPS):
    x32 = x.astype(jnp.float32)
    y = x32 * lax.rsqrt(jnp.mean(x32 * x32, axis=-1, keepdims=True) + eps)
    return y * g.astype(jnp.float32)


def alibi_slopes(n_heads):
    return jnp.exp2(-8.0 * (jnp.arange(n_heads, dtype=jnp.float32) + 1.0) / n_heads)


def dilated_window_attention(q, k, v, slopes, window, dilation):
    B, S, H, E = q.shape
    d = dilation
    n = window // (2 * d)
    L = S // d
    nb = -(-L // n)
    Lp = nb * n
    qs = jnp.pad(q.reshape(B, L, d, H, E), ((0, 0), (0, Lp - L), (0, 0), (0, 0), (0, 0)))
    qs = qs.reshape(B, nb, n, d, H, E)

    def neighbourhood(t):
        tp = jnp.pad(t.reshape(B, L, d, H, E), ((0, 0), (n, Lp - L + n), (0, 0), (0, 0), (0, 0)))
        tp = tp.reshape(B, nb + 2, n, d, H, E)
        return jnp.concatenate([tp[:, :-2], tp[:, 1:-1], tp[:, 2:]], axis=2)

    kb = neighbourhood(k)
    vb = neighbourhood(v)
    qi = jnp.arange(nb)[:, None] * n + jnp.arange(n)[None, :]
    kj = (jnp.arange(nb)[:, None] - 1) * n + jnp.arange(3 * n)[None, :]
    off = kj[:, None, :] - qi[:, :, None]
    valid = (jnp.abs(off) <= n) & (kj[:, None, :] >= 0) & (kj[:, None, :] < L)
    dist = (d * jnp.abs(off)).astype(jnp.float32)

    s = jnp.einsum('bnqrhe,bnkrhe->bhrnqk', qs, kb) * (E ** -0.5)
    s = s - slopes[None, :, None, None, None, None] * dist[None, None, None]
    s = jnp.where(valid[None, None, None], s, MASK_VALUE)
    m = jnp.max(s, axis=-1, keepdims=True)
    p = jnp.exp(s - m)
    den = jnp.sum(p, axis=-1)
    lse = m[..., 0] + jnp.log(den)
    out = jnp.einsum('bhrnqk,bnkrhe->bnqrhe', p, vb)
    out = out / jnp.transpose(den, (0, 3, 4, 2, 1))[..., None]
    out = out.reshape(B, Lp, d, H, E)[:, :L].reshape(B, S, H, E)
    lse = jnp.transpose(lse, (0, 3, 4, 2, 1)).reshape(B, Lp, d, H)[:, :L].reshape(B, S, H)
    return out, lse


def dilated_mixture_attention(q, k, v):
    slopes = alibi_slopes(q.shape[2])
    outs, lses = [], []
    for window, dilation in DILATED_PATTERNS:
        o, l = dilated_window_attention(q, k, v, slopes, window, dilation)
        outs.append(o)
        lses.append(l)
    wts = jax.nn.softmax(jnp.stack(lses, axis=0), axis=0)
    return jnp.einsum('pbsh,pbshe->bshe', wts, jnp.stack(outs, axis=0))


def token_shift(z, mu_prev, mu_next):
    z_prev = jnp.pad(z, ((0, 0), (1, 0), (0, 0)))[:, :-1]
    z_next = jnp.pad(z, ((0, 0), (0, 1), (0, 0)))[:, 1:]
    return z + mu_prev * (z_prev - z) + mu_next * (z_next - z)


def wkv7_scan(r, w, k, v, kk, a, reverse):
    B, S, H, N = r.shape
    xs = tuple(jnp.moveaxis(t, 1, 0) for t in (r, w, k, v, kk, a))
    state0 = jnp.zeros((B, H, N, N), jnp.float32)

    def step(state, inp):
        r_t, w_t, k_t, v_t, kk_t, a_t = inp
        sa = jnp.einsum('bhvk,bhk->bhv', state, -kk_t)
        state = (state * w_t[:, :, None, :]
                 + sa[..., :, None] * (kk_t * a_t)[..., None, :]
                 + v_t[..., :, None] * k_t[..., None, :])
        y = jnp.einsum('bhvk,bhk->bhv', state, r_t)
        return state, y

    _, ys = lax.scan(step, state0, xs, reverse=reverse)
    return jnp.moveaxis(ys, 0, 1)


def rwkv7_bidirectional(r, k, v, xw, xa, xg, w_decay0, w_decay_up, a_gate0, a_gate_up,
                        g_up, k_k, k_a, r_k, lnx_g, lnx_b):
    B, S, _ = r.shape

    def hs(t):
        return t.reshape(B, S, RWKV_HEADS, HEAD_DIM)

    g = jax.nn.sigmoid(xg) @ g_up
    kk = hs(k * k_k)
    kk = kk * lax.rsqrt(jnp.maximum(jnp.sum(kk * kk, axis=-1, keepdims=True), 1e-12))
    ys, a_dirs = [], []
    for direction in range(2):
        w = -jax.nn.softplus(-(w_decay0[direction] + jnp.tanh(xw) @ w_decay_up[direction])) - 0.5
        decay = jnp.exp(-jnp.exp(w))
        a = jax.nn.sigmoid(a_gate0[direction] + xa @ a_gate_up[direction])
        k_dir = k * (1.0 + (a - 1.0) * k_a)
        ys.append(wkv7_scan(hs(r), hs(decay), hs(k_dir), hs(v), kk, hs(a), reverse=(direction == 1)))
        a_dirs.append(a)
    y = ys[0] + ys[1]
    mean = jnp.mean(y, axis=-1, keepdims=True)
    var = jnp.mean(jnp.square(y - mean), axis=-1, keepdims=True)
    yn = ((y - mean) * lax.rsqrt(var + LNX_EPS)).reshape(B, S, RWKV_WIDTH) * lnx_g + lnx_b
    k_bonus = hs(k * (1.0 + (0.5 * (a_dirs[0] + a_dirs[1]) - 1.0) * k_a))
    bonus = jnp.sum(hs(r) * k_bonus * r_k, axis=-1, keepdims=True) * hs(v)
    return (yn + bonus.reshape(B, S, RWKV_WIDTH)) * g


def peer_ffn(h, w_query, sub_keys1, sub_keys2, expert_u, expert_v):
    B, S, D = h.shape
    T = B * S
    half = PEER_KEY_DIM // 2
    x = h.reshape(T, D)
    q = (x @ w_query).astype(jnp.float32).reshape(T, PEER_HEADS, PEER_KEY_DIM)
    s1 = jnp.einsum('thd,nd->thn', q[..., :half], sub_keys1.astype(jnp.float32))
    s2 = jnp.einsum('thd,nd->thn', q[..., half:], sub_keys2.astype(jnp.float32))
    v1, i1 = lax.top_k(s1, PEER_TOPK)
    v2, i2 = lax.top_k(s2, PEER_TOPK)
    cand = (v1[..., :, None] + v2[..., None, :]).reshape(T, PEER_HEADS, PEER_TOPK * PEER_TOPK)
    top_s, top_p = lax.top_k(cand, PEER_TOPK)
    e_idx = (jnp.take_along_axis(i1, top_p // PEER_TOPK, axis=-1) * PEER_N_KEYS
             + jnp.take_along_axis(i2, top_p % PEER_TOPK, axis=-1))
    gate = jax.nn.softmax(top_s, axis=-1)
    HK = PEER_HEADS * PEER_TOPK
    nblk = T // PEER_TOKEN_BLOCK
    xb = x.reshape(nblk, PEER_TOKEN_BLOCK, D)
    ib = e_idx.reshape(nblk, PEER_TOKEN_BLOCK, HK)
    gb = gate.reshape(nblk, PEER_TOKEN_BLOCK, HK)

    def block(args):
        xc, ic, gc = args
        u = jnp.take(expert_u, ic, axis=0)
        act = jax.nn.gelu(jnp.einsum('cd,ced->ce', xc, u).astype(jnp.float32), approximate=False)
        vv = jnp.take(expert_v, ic, axis=0)
        return jnp.einsum('ce,ced->cd', (gc * act).astype(vv.dtype), vv)

    y = lax.map(block, (xb, ib, gb))
    return y.reshape(B, S, D)


def setup_inputs(seed: int = 0) -> dict:
    key = jax.random.key(seed)
    ks = jax.random.split(key, 32)
    f32 = jnp.float32
    nrm = lambda k, shape, s: jax.random.normal(k, shape, f32) * s
    RW = RWKV_WIDTH
    return {
        "x": nrm(ks[0], (BATCH, SEQ, D_MODEL), 1.0),
        "c": nrm(ks[1], (BATCH, D_MODEL), 1.0),
        "ada_w": nrm(ks[2], (DEPTH, D_MODEL, 6 * D_MODEL), 0.5 * D_MODEL ** -0.5),
        "ada_b": nrm(ks[3], (DEPTH, 6 * D_MODEL), 0.1),
        "norm1_g": 1.0 + nrm(ks[4], (DEPTH, D_MODEL), 0.05),
        "w_in": nrm(ks[5], (DEPTH, D_MODEL, IN_WIDTH), D_MODEL ** -0.5),
        "mu_prev": jax.random.uniform(ks[6], (DEPTH, SHIFT_WIDTH), f32, 0.0, 0.5),
        "mu_next": jax.random.uniform(ks[7], (DEPTH, SHIFT_WIDTH), f32, 0.0, 0.5),
        "q_norm_g": 1.0 + nrm(ks[8], (DEPTH, HEAD_DIM), 0.05),
        "k_norm_g": 1.0 + nrm(ks[9], (DEPTH, HEAD_DIM), 0.05),
        "w_decay0": jax.random.uniform(ks[10], (DEPTH, 2, RW), f32, -5.0, -1.0),
        "w_decay_up": nrm(ks[11], (DEPTH, 2, DECAY_LORA, RW), 0.5 * DECAY_LORA ** -0.5),
        "a_gate0": nrm(ks[12], (DEPTH, 2, RW), 0.5),
        "a_gate_up": nrm(ks[13], (DEPTH, 2, AAA_LORA, RW), AAA_LORA ** -0.5),
        "g_up": nrm(ks[14], (DEPTH, GATE_LORA, RW), GATE_LORA ** -0.5),
        "k_k": 0.85 + nrm(ks[15], (DEPTH, RW), 0.05),
        "k_a": 1.0 + nrm(ks[16], (DEPTH, RW), 0.05),
        "r_k": nrm(ks[17], (DEPTH, RWKV_HEADS, HEAD_DIM), 0.1),
        "lnx_g": 1.0 + nrm(ks[18], (DEPTH, RW), 0.05),
        "lnx_b": nrm(ks[19], (DEPTH, RW), 0.01),
        "w_out": nrm(ks[20], (DEPTH, MIX_WIDTH, D_MODEL), MIX_WIDTH ** -0.5),
        "norm2_g": 1.0 + nrm(ks[21], (DEPTH, D_MODEL), 0.05),
        "peer_w_query": nrm(ks[22], (DEPTH, D_MODEL, PEER_HEADS * PEER_KEY_DIM), D_MODEL ** -0.5),
        "peer_sub_keys1": nrm(ks[23], (DEPTH, PEER_N_KEYS, PEER_KEY_DIM // 2), (PEER_KEY_DIM // 2) ** -0.5),
        "peer_sub_keys2": nrm(ks[24], (DEPTH, PEER_N_KEYS, PEER_KEY_DIM // 2), (PEER_KEY_DIM // 2) ** -0.5),
        "peer_u": nrm(ks[25], (DEPTH, PEER_N_EXPERTS, D_MODEL), D_MODEL ** -0.5),
        "peer_v": nrm(ks[26], (DEPTH, PEER_N_EXPERTS, D_MODEL), (PEER_HEADS * PEER_TOPK) ** -0.5),
    }


def reference(x, c, ada_w, ada_b, norm1_g, w_in, mu_prev, mu_next, q_norm_g, k_norm_g,
              w_decay0, w_decay_up, a_gate0, a_gate_up, g_up, k_k, k_a, r_k, lnx_g, lnx_b,
              w_out, norm2_g, peer_w_query, peer_sub_keys1, peer_sub_keys2, peer_u, peer_v):
    in_dtype = x.dtype
    B, S, _ = x.shape
    A = ATT_WIDTH
    RW = RWKV_WIDTH
    for l in range(DEPTH):
        mod = (jax.nn.silu(c.astype(jnp.float32)) @ ada_w[l].astype(jnp.float32)
               + ada_b[l].astype(jnp.float32))[:, None, :]
        shift1, scale1, gate1, shift2, scale2, gate2 = jnp.split(mod, 6, axis=-1)

        h = rms_norm(x, norm1_g[l]) * (1.0 + scale1) + shift1
        z = (h.astype(in_dtype) @ w_in[l]).astype(jnp.float32)
        qa = rms_norm(z[..., :A].reshape(B, S, ATT_HEADS, HEAD_DIM), q_norm_g[l])
        ka = rms_norm(z[..., A:2 * A].reshape(B, S, ATT_HEADS, HEAD_DIM), k_norm_g[l])
        va = z[..., 2 * A:3 * A].reshape(B, S, ATT_HEADS, HEAD_DIM)
        att = dilated_mixture_attention(qa, ka, va).reshape(B, S, A)

        zr = token_shift(z[..., 3 * A:], mu_prev[l].astype(jnp.float32), mu_next[l].astype(jnp.float32))
        o1, o2, o3 = RW, 2 * RW, 3 * RW
        o4, o5 = o3 + DECAY_LORA, o3 + DECAY_LORA + AAA_LORA
        rw = rwkv7_bidirectional(
            zr[..., :o1], zr[..., o1:o2], zr[..., o2:o3],
            zr[..., o3:o4], zr[..., o4:o5], zr[..., o5:],
            w_decay0[l].astype(jnp.float32), w_decay_up[l].astype(jnp.float32),
            a_gate0[l].astype(jnp.float32), a_gate_up[l].astype(jnp.float32),
            g_up[l].astype(jnp.float32), k_k[l].astype(jnp.float32), k_a[l].astype(jnp.float32),
            r_k[l].astype(jnp.float32), lnx_g[l].astype(jnp.float32), lnx_b[l].astype(jnp.float32))
        mixed = jnp.concatenate([att, rw], axis=-1).astype(in_dtype) @ w_out[l]
        x = (x.astype(jnp.float32) + gate1 * mixed.astype(jnp.float32)).astype(in_dtype)

        h2 = (rms_norm(x, norm2_g[l]) * (1.0 + scale2) + shift2).astype(in_dtype)
        ff = peer_ffn(h2, peer_w_query[l], peer_sub_keys1[l], peer_sub_keys2[l], peer_u[l], peer_v[l])
        x = (x.astype(jnp.float32) + gate2 * ff.astype(jnp.float32)).astype(in_dtype)
    return x
```

```python
import contextlib
import numpy as np
import ml_dtypes
import concourse.bass as bass
import concourse.mybir as mybir
from concourse.bass_utils import run_bass_kernel_spmd

F32 = mybir.dt.float32
BF16 = mybir.dt.bfloat16
I32 = mybir.dt.int32
U32 = mybir.dt.uint32
ALU = mybir.AluOpType
AF = mybir.ActivationFunctionType
AX = mybir.AxisListType

N_DMA_SEMS = 24
ENGS = ("pe", "dve", "act", "pool", "sp")

D = 1024
SEQ = 2048
NT = SEQ // 128
INW = 3328
CDEC = float(np.exp(-0.5))
import os
MAXOPS = int(os.environ.get("MK_MAXOPS", "100000000"))


class Op:
    __slots__ = ("eng", "fn", "deps", "done", "dma")

    def __init__(self, eng, fn, dma):
        self.eng = eng
        self.fn = fn
        self.dma = dma
        self.deps = []
        self.done = None


class Sched:
    def __init__(self, nc):
        self.nc = nc
        self.ops = []
        self.state = {}
        self.count = {e: 0 for e in ENGS}
        self.dma_uses = [0] * N_DMA_SEMS
        self.dma_last = [None] * N_DMA_SEMS
        self.dma_rr = 0
        self.fence_ops = []

    def _keys(self, buf, sub):
        d = self.state.setdefault(buf, {})
        if sub is None:
            keys = list(d.keys())
            if None not in d:
                keys.append(None)
        else:
            keys = [sub, None]
        return d, keys

    def fence(self):
        last = {}
        for o in self.ops:
            s, v = o.done
            if v > last.get(s, (0, None))[0]:
                last[s] = (v, o)
        self.fence_ops = [o for (_, o) in last.values()]
        self.state = {}

    def op(self, eng, fn, reads=(), writes=(), dma=False):
        if len(self.ops) >= MAXOPS:
            return None
        o = Op(eng, fn, dma)
        deps = list(self.fence_ops)
        for r in reads:
            buf, sub = (r[0], tuple(r[1:])) if isinstance(r, tuple) else (r, None)
            d, keys = self._keys(buf, sub)
            for k in keys:
                st = d.get(k)
                if st and st[0] is not None:
                    deps.append(st[0])
        for w in writes:
            buf, sub = (w[0], tuple(w[1:])) if isinstance(w, tuple) else (w, None)
            d, keys = self._keys(buf, sub)
            for k in keys:
                st = d.get(k)
                if st:
                    if st[0] is not None:
                        deps.append(st[0])
                    deps.extend(st[1])
        if dma:
            i = self.dma_rr
            self.dma_rr = (i + 1) % N_DMA_SEMS
            if self.dma_last[i] is not None:
                deps.append(self.dma_last[i])
            self.dma_uses[i] += 1
            o.done = ("dma%d" % i, 16 * self.dma_uses[i])
            self.dma_last[i] = o
        else:
            self.count[eng] += 1
            o.done = (eng, self.count[eng])
        for r in reads:
            buf, sub = (r[0], tuple(r[1:])) if isinstance(r, tuple) else (r, None)
            st = self.state[buf].setdefault(sub, [None, []])
            st[1].append(o)
        for w in writes:
            buf, sub = (w[0], tuple(w[1:])) if isinstance(w, tuple) else (w, None)
            d = self.state[buf]
            if sub is None:
                for k in list(d.keys()):
                    d[k] = [o, []]
                d[None] = [o, []]
            else:
                d[sub] = [o, []]
        seen = {}
        for p in deps:
            s, v = p.done
            if eng == "pe" and (not dma) and (not p.dma) and p.eng == "pe":
                continue
            if v > seen.get(s, 0):
                seen[s] = v
        o.deps = list(seen.items())
        self.ops.append(o)
        return o

    def emit(self):
        nc = self.nc
        with contextlib.ExitStack() as es:
            sems = {}
            for e in ENGS:
                sems[e] = es.enter_context(nc.semaphore("s_" + e))
            for i in range(N_DMA_SEMS):
                sems["dma%d" % i] = es.enter_context(nc.semaphore("s_dma%d" % i))
            block = es.enter_context(nc.Block())
            per = {e: [o for o in self.ops if o.eng == e] for e in ENGS}
            final = {}
            for o in self.ops:
                s, v = o.done
                final[s] = max(final.get(s, 0), v)

            def run(e, h):
                known = {}
                for o in per[e]:
                    for s, v in o.deps:
                        if v > known.get(s, 0):
                            h.wait_ge(sems[s], v)
                            known[s] = v
                    inst = o.fn(h)
                    s, v = o.done
                    inst.then_inc(sems[s], 16 if o.dma else 1)
                if e == "sp":
                    for s, v in final.items():
                        if v > known.get(s, 0):
                            h.wait_ge(sems[s], v)

            @block.tensor
            def _(h):
                run("pe", h)

            @block.vector
            def _(h):
                run("dve", h)

            @block.scalar
            def _(h):
                run("act", h)

            @block.gpsimd
            def _(h):
                run("pool", h)

            @block.sync
            def _(h):
                run("sp", h)


SB_BASE = 16512
SB_END = 229376


class Arena:
    def __init__(self, nc):
        self.nc = nc
        self.off = SB_BASE
        self.n = 0

    def alloc(self, name, shape, dt):
        esz = 2 if dt == BF16 else 4
        sz = int(np.prod(shape[1:])) * esz
        sz = (sz + 63) // 64 * 64
        assert self.off + sz <= SB_END, ("SBUF overflow", name, self.off, sz)
        self.n += 1
        t = self.nc.alloc_sbuf_tensor_at("%s_%d" % (name, self.n), list(shape), dt, offset=self.off)
        self.off += sz
        return t

    def mark(self):
        return self.off

    def release(self, m):
        self.peak = max(getattr(self, "peak", 0), self.off)
        if os.environ.get("MK_VERBOSE"):
            print("arena peak", self.peak - SB_BASE, "of", SB_END - SB_BASE)
        self.off = m


def build(nc, dbg=()):
    S = Sched(nc)
    A = Arena(nc)
    dbg = set(dbg)
    taps = {}

    def din(name, shape, dt=F32):
        return nc.dram_tensor(name, list(shape), dt, kind="ExternalInput").ap()

    def dout(name, shape, dt=F32):
        return nc.dram_tensor(name, list(shape), dt, kind="ExternalOutput").ap()

    x_d = din("x", [SEQ, D])
    c_d = din("c_fp", [128, 8])
    adaw_d = din("ada_w", [D, 6 * D])
    adab_d = din("ada_b", [1, 6 * D])
    g1_d = din("g1_fp", [128, 8])
    g2_d = din("g2_fp", [128, 8])
    g2b_d = din("g2_b", [128, D])
    win_d = din("w_in", [D, INW])
    out_d = dout("out", [SEQ, D])
    qkg_d = din("qkg", [128, 2])
    eb_d = din("eb", [4, 128, 7 * 2 * 2 * 128], BF16)
    up_d = din("up", [128, 2, 512])
    gup_d = din("g_up", [128, 512])
    lnx_d = din("lnx", [128, 2, 512])
    mu_d = din("mu", [128, 2, 14])
    w0a0_d = din("w0a0", [128, 2, 2, 4])
    kkr_d = din("kkr", [128, 3, 4])
    cpk_d = din("cpk", [128, 1282], BF16)
    wout_d = din("w_out", [D, D])
    wqry_d = din("w_query", [D, 2048])
    skT_d = din("skT", [128, 2, 128])
    iota_d = din("iota16", [128, 16])
    peeru_d = din("peer_u", [16384, D])
    peerv_d = din("peer_v", [16384, D])
    bc_d = nc.dram_tensor("bc_scratch", [4, 128, D], F32, kind="Internal").ap()

    def mm(out, lhsT, rhs, start, stop, reads, writes):
        S.op("pe", lambda h: h.matmul(out, lhsT=lhsT, rhs=rhs, start=start, stop=stop), reads=reads, writes=writes)

    def tr(out, in_, ident, reads, writes):
        S.op("pe", lambda h: h.transpose(out=out, in_=in_, identity=ident), reads=reads, writes=writes)

    def act(out, in_, func, reads, writes, bias=0.0, scale=1.0, accum=None):
        if accum is None:
            S.op("act", lambda h: h.activation(out=out, in_=in_, func=func, bias=bias, scale=scale), reads=reads, writes=writes)
        else:
            S.op("act", lambda h: h.activation(out=out, in_=in_, func=func, bias=bias, scale=scale, accum_out=accum), reads=reads, writes=writes)

    def tt(eng, out, in0, in1, op, reads, writes):
        S.op(eng, lambda h: h.tensor_tensor(out=out, in0=in0, in1=in1, op=op), reads=reads, writes=writes)

    def ts(eng, out, in0, s1, s2, op0, op1, reads, writes):
        if s2 is None:
            S.op(eng, lambda h: h.tensor_scalar(out=out, in0=in0, scalar1=s1, scalar2=None, op0=op0), reads=reads, writes=writes)
        else:
            S.op(eng, lambda h: h.tensor_scalar(out=out, in0=in0, scalar1=s1, scalar2=s2, op0=op0, op1=op1), reads=reads, writes=writes)

    def stt(out, in0, scalar, in1, op0, op1, reads, writes, accum=None):
        if accum is None:
            S.op("dve", lambda h: h.scalar_tensor_tensor(out=out, in0=in0, scalar=scalar, in1=in1, op0=op0, op1=op1), reads=reads, writes=writes)
        else:
            S.op("dve", lambda h: h.scalar_tensor_tensor(out=out, in0=in0, scalar=scalar, in1=in1, op0=op0, op1=op1, accum_out=accum), reads=reads, writes=writes)

    def cp(eng, out, in_, reads, writes):
        if eng == "act":
            S.op("act", lambda h: h.copy(out=out, in_=in_), reads=reads, writes=writes)
        else:
            S.op(eng, lambda h: h.tensor_copy(out=out, in_=in_), reads=reads, writes=writes)

    def memset(eng, ap, val, writes):
        S.op(eng, lambda h: h.memset(ap, val), writes=writes)

    def dma(eng, out, in_, reads, writes):
        S.op(eng, lambda h: h.dma_start(out=out, in_=in_), reads=reads, writes=writes, dma=True)

    def tap(name, src_ap, shape, reads, dt=F32):
        if name in dbg:
            t = dout("dbg_" + name, shape, dt)
            taps[name] = t
            dma("sp", t, src_ap, reads, [])

    PSF = nc.alloc_psum_tensor("psf", [128, 6, 512], F32)
    psf = [PSF[:, i, :] for i in range(6)]
    psb = [nc.alloc_psum_tensor("psb%d" % i, [128, 1024], BF16) for i in range(2)]
    rr = {"f": 0, "b": 0}

    def pf():
        i = rr["f"]
        rr["f"] = (i + 1) % 6
        return psf[i], "psf%d" % i

    def pf2():
        i = ((rr["f"] + 1) // 2 * 2) % 6
        rr["f"] = (i + 2) % 6
        return PSF[:, i:i + 2, :], ["psf%d" % i, "psf%d" % (i + 1)]

    def pb():
        i = rr["b"]
        rr["b"] = (i + 1) % 2
        return psb[i], "psb%d" % i

    ident = A.alloc("ident", [128, 128], BF16)
    ones_f = A.alloc("ones_f", [128, 128], F32)
    memset("pool", ident[:], 0.0, ["ident"])
    S.op("pool", lambda h: h.affine_select(out=ident[:], in_=ident[:], pattern=[[-1, 128]], compare_op=ALU.not_equal,
                                           fill=1.0, base=0, channel_multiplier=1), reads=["ident"], writes=["ident"])
    memset("pool", ones_f[:], 1.0, ["ones_f"])

    mod_fp = A.alloc("mod_fp", [128, 48], F32)
    g1_fp = A.alloc("g1_fp", [128, 8], F32)
    g2_fp = A.alloc("g2_fp", [128, 8], F32)
    gs1_fp = A.alloc("gs1_fp", [128, 8], F32)
    gs2_fp = A.alloc("gs2_fp", [128, 8], F32)
    m0 = A.mark()
    gate1_b = A.alloc("gate1_b", [128, D], F32)
    gs2_b = A.alloc("gs2_b", [128, D], F32)
    shift2_b = A.alloc("shift2_b", [128, D], F32)
    gate2_b = A.alloc("gate2_b", [128, D], F32)
    c_sb = A.alloc("c_sb", [128, 8], F32)
    sc_sb = A.alloc("sc_sb", [128, 8], F32)
    adab = A.alloc("adab", [1, 6 * D], F32)
    modrow = A.alloc("modrow", [1, 6 * D], F32)
    dma("sp", c_sb[:], c_d, [], ["c_sb"])
    dma("sp", adab[:], adab_d, [], ["adab"])
    dma("sp", g1_fp[:], g1_d, [], ["g1_fp"])
    dma("sp", g2_fp[:], g2_d, [], ["g2_fp"])
    dma("sp", gs2_b[:], g2b_d, [], ["gs2_b"])
    act(sc_sb[:], c_sb[:], AF.Silu, ["c_sb"], ["sc_sb"])
    wblk = [A.alloc("adaw%d" % i, [128, 8, 512], F32) for i in range(2)]
    adaw_v = adaw_d.rearrange("(j p) n -> p j n", p=128)
    for nb in range(12):
        wb = wblk[nb % 2]
        wn = "adaw%d" % (nb % 2)
        dma("sp", wb[:], adaw_v[:, :, nb * 512:(nb + 1) * 512], [], [wn])
        ps, pn = pf()
        for j in range(8):
            mm(ps[0:1, :], sc_sb[:, j:j + 1], wb[:, j, :], j == 0, j == 7, ["sc_sb", wn], [pn])
        tt("dve", modrow[0:1, nb * 512:(nb + 1) * 512], ps[0:1, :], adab[0:1, nb * 512:(nb + 1) * 512], ALU.add,
           [pn, "adab"], [("modrow", nb)])
    ps, pn = pf()
    for j in range(48):
        mm(ps[:, 2 * j:2 * j + 2], modrow[0:1, j * 128:(j + 1) * 128], ones_f[0:1, 0:2], True, True,
           ["modrow", "ones_f"], [pn])
    cp("dve", mod_fp[:], ps[:, 0:96].rearrange("p (j t) -> p j t", t=2)[:, :, 0], [pn], ["mod_fp"])
    stt(gs1_fp[:], mod_fp[:, 8:16], 1.0, g1_fp[:], ALU.add, ALU.mult, ["mod_fp", "g1_fp"], ["gs1_fp"])
    stt(gs2_fp[:], mod_fp[:, 32:40], 1.0, g2_fp[:], ALU.add, ALU.mult, ["mod_fp", "g2_fp"], ["gs2_fp"])
    for seg, dst, dn, kind in ((2, gate1_b, "gate1_b", 0), (4, gs2_b, "gs2_b", 1), (3, shift2_b, "shift2_b", 0),
                               (5, gate2_b, "gate2_b", 0)):
        for hb in range(2):
            ps, pn = pf()
            c0 = seg * D + hb * 512
            mm(ps[:, :], ones_f[0:1, 0:128], modrow[0:1, c0:c0 + 512], True, True, ["ones_f", "modrow"], [pn])
            dsl = dst[:, hb * 512:(hb + 1) * 512]
            if kind == 0:
                cp("act", dsl, ps[:, :], [pn], [(dn, hb)])
            else:
                stt(dsl, ps[:, :], 1.0, dsl, ALU.add, ALU.mult, [pn, (dn, hb)], [(dn, hb)])
    for i_, (bt__, bn__) in enumerate(((gate1_b, "gate1_b"), (gs2_b, "gs2_b"), (shift2_b, "shift2_b"), (gate2_b, "gate2_b"))):
        dma("sp", bc_d[i_], bt__[:], [bn__], ["bc_d"])
    tap("modrow", modrow[:], [1, 6 * D], ["modrow"])
    tap("gs2_b", gs2_b[:], [128, D], ["gs2_b"])
    tap("mod_fp", mod_fp[:], [128, 48], ["mod_fp"])
    S.fence()
    A.release(m0)

    m_rw = A.mark()
    rwT = A.alloc("rwT", [128, 4, SEQ], BF16)
    m_att = A.mark()
    attT = A.alloc("attT", [128, 8, SEQ], BF16)
    m_ht = A.mark()
    hT = A.alloc("hT", [128, 8, SEQ], BF16)
    m1 = A.mark()
    xt = [A.alloc("xt%d" % i, [128, D], F32) for i in range(2)]
    xn = [A.alloc("xn%d" % i, [128, D], BF16) for i in range(2)]
    junk = A.alloc("junk", [128, D], BF16)
    ss = A.alloc("ss", [128, NT], F32)
    rstd = A.alloc("rstd", [128, NT], F32)
    x_v = x_d.rearrange("(t p) d -> t p d", p=128)
    for t in range(NT):
        xb_, xnm = xt[t % 2], "xt%d" % (t % 2)
        nb_, nnm = xn[t % 2], "xn%d" % (t % 2)
        dma("sp", xb_[:], x_v[t], [], [xnm])
        act(junk[:], xb_[:], AF.Square, [xnm], ["junk", ("ss", t)], accum=ss[:, t:t + 1])
        ts("dve", rstd[:, t:t + 1], ss[:, t:t + 1], 1.0 / D, 1e-6, ALU.mult, ALU.add, [("ss", t)], [("rstd", t)])
        act(rstd[:, t:t + 1], rstd[:, t:t + 1], AF.Sqrt, [("rstd", t)], [("rstd", t)])
        S.op("dve", (lambda o: (lambda h: h.reciprocal(out=o, in_=o)))(rstd[:, t:t + 1]), reads=[("rstd", t)], writes=[("rstd", t)])
        act(nb_[:], xb_[:], AF.Copy, [xnm, ("rstd", t)], [nnm], scale=rstd[:, t:t + 1])
        ps, pn = pb()
        for j in range(8):
            tr(ps[:, j * 128:(j + 1) * 128], nb_[:, j * 128:(j + 1) * 128], ident[:], [nnm, "ident"], [pn])
        hsl = hT[:, :, t * 128:(t + 1) * 128]
        psv = ps[:, :].rearrange("p (j t) -> p j t", t=128)
        tt("dve", hsl, psv, gs1_fp[:, 0:8].unsqueeze(2).broadcast_to([128, 8, 128]), ALU.mult, [pn, "gs1_fp"], [("hT", t)])
        tt("pool", hsl, hsl, mod_fp[:, 0:8].unsqueeze(2).broadcast_to([128, 8, 128]), ALU.add, [("hT", t), "mod_fp"], [("hT", t)])
    tap("hT", hT[:], [128, 8, SEQ], ["hT"], BF16)
    S.fence()
    A.release(m1)


    blockones = A.alloc("blockones", [128, 128], BF16)
    memset("pool", blockones[:], 0.0, ["blockones"])
    memset("pool", blockones[0:64, 0:64], 1.0, ["blockones"])
    memset("pool", blockones[64:128, 64:128], 1.0, ["blockones"])
    m3 = A.mark()
    identf = A.alloc("identf", [128, 128], F32)
    memset("pool", identf[:], 0.0, ["identf"])
    S.op("pool", lambda h: h.affine_select(out=identf[:], in_=identf[:], pattern=[[-1, 128]], compare_op=ALU.not_equal,
                                           fill=1.0, base=0, channel_multiplier=1), reads=["identf"], writes=["identf"])
    top3 = A.mark()
    A.off = m_att
    lwin = A.alloc("lwin", [128, SEQ], BF16)
    sg = A.alloc("sg", [128, SEQ], BF16)
    rT = A.alloc("rT", [128, SEQ], F32)
    kTf = A.alloc("kTf", [128, SEQ], F32)
    kkT = A.alloc("kkT", [128, SEQ], F32)
    assert A.off <= m_ht
    A.off = top3
    up_sb = A.alloc("up_sb", [128, 2, 512], BF16)
    gup = A.alloc("gup", [128, 512], BF16)
    lnx = A.alloc("lnx", [128, 2, 512], F32)
    mu = A.alloc("mu", [128, 2, 14], F32)
    c0all = A.alloc("c0all", [128, 14], F32)
    w0a0 = A.alloc("w0a0", [128, 2, 2, 4], F32)
    kkr = A.alloc("kkr", [128, 3, 4], F32)
    cpk = A.alloc("cpk", [128, 1282], BF16)
    zraw = A.alloc("zraw", [128, SEQ + 2], F32)
    wch = [A.alloc("wch%d" % i, [128, 8, 128], BF16) for i in range(2)]
    vbf = A.alloc("vbf", [128, SEQ], BF16)
    asum = A.alloc("asum", [128, SEQ], F32)
    Vtm = A.alloc("Vtm", [128, NT, 128], BF16)
    ysum = A.alloc("ysum", [128, NT, 128], F32)
    rt_ = A.alloc("rt_", [128, SEQ // 2], BF16)
    bt_ = A.alloc("bt_", [128, SEQ // 2], BF16)
    khah = A.alloc("khah", [128, NT // 2, 2, 128], BF16)
    G4 = A.alloc("G4", [128, NT // 2, 2, 4, 128], BF16)
    pcs = A.alloc("pcs", [128, NT], F32)
    Mf = A.alloc("Mf", [128, 64], F32)
    Mbf = A.alloc("Mbf", [128, 64], BF16)
    Xsb = A.alloc("Xsb", [128, 2, 64], BF16)
    Usb = A.alloc("Usb", [128, 2, 64], BF16)
    sig_b = A.alloc("sig_b", [128, 512], F32)
    a_b = A.alloc("a_b", [128, 512], F32)
    cs_b = A.alloc("cs_b", [128, 512], F32)
    e1_b = A.alloc("e1_b", [128, 512], F32)
    tmc_b = A.alloc("tmc_b", [128, 512], F32)
    tme_b = A.alloc("tme_b", [128, 512], F32)
    kd_b = A.alloc("kd_b", [128, 512], F32)
    akk_b = A.alloc("akk_b", [128, 512], F32)
    ex_b = [A.alloc("ex_b%d" % i, [128, 512], F32) for i in range(2)]
    kt_b = A.alloc("kt_b", [128, 512], BF16)
    at_b = A.alloc("at_b", [128, 512], BF16)
    khT_b = A.alloc("khT_b", [128, 512], BF16)
    ahT_b = A.alloc("ahT_b", [128, 512], BF16)
    sq_b = A.alloc("sq_b", [128, 512], BF16)
    rs_b = A.alloc("rs_b", [128, 512], F32)
    Awk = [A.alloc("Awk%d" % i, [128, 8, 128], F32) for i in range(2)]
    Bwk = [A.alloc("Bwk%d" % i, [128, 8, 128], F32) for i in range(2)]
    Sf = A.alloc("Sf", [128, 8, 128], F32)
    gn1 = A.alloc("gn1", [128, 32], F32)
    gn2 = A.alloc("gn2", [128, 32], F32)
    coef = A.alloc("coef", [128, 32], F32)
    rwo = A.alloc("rwo", [128, NT, 128], BF16)

    dma("pool", up_sb[:], up_d, [], ["up_sb"])
    dma("pool", gup[:], gup_d, [], ["gup"])
    dma("sp", lnx[:], lnx_d, [], ["lnx"])
    dma("sp", mu[:], mu_d, [], ["mu"])
    dma("sp", w0a0[:], w0a0_d, [], ["w0a0"])
    dma("sp", kkr[:], kkr_d, [], ["kkr"])
    dma("sp", cpk[:], cpk_d, [], ["cpk"])
    msk4 = lambda dr: cpk[:, dr * 640:dr * 640 + 512]
    mskA = lambda dr: cpk[:, dr * 640 + 512:dr * 640 + 640]
    hsel = cpk[:, 1280:1282]
    tt("dve", c0all[:], mu[:, 0, :], mu[:, 1, :], ALU.add, ["mu"], ["c0all"])
    ts("dve", c0all[:], c0all[:], -1.0, 1.0, ALU.mult, ALU.add, ["c0all"], ["c0all"])
    memset("pool", zraw[:, 0:1], 0.0, ["zraw"])
    memset("pool", zraw[:, SEQ + 1:SEQ + 2], 0.0, ["zraw"])
    win_v3 = win_d.rearrange("(j p) n -> p j n", p=128)
    wcc = [0]

    def zr_chunk(c, dst, dn):
        wi = wcc[0] % 2
        wcc[0] += 1
        col = 1536 + 128 * c
        dma("pool", wch[wi][:], win_v3[:, :, col:col + 128], [], ["wch%d" % wi])
        for blk in range(4):
            ps, pn = pf()
            for j in range(8):
                mm(ps[:, :], wch[wi][:, j, :], hT[:, j, blk * 512:(blk + 1) * 512], j == 0, j == 7, ["wch%d" % wi, "hT"], [pn])
            cp("act", zraw[:, 1 + blk * 512:1 + (blk + 1) * 512], ps[:, :], [pn], [("zraw", blk)])
        act(dst[:], zraw[:, 1:SEQ + 1], AF.Copy, ["zraw", "c0all"], [dn], scale=c0all[:, c:c + 1])
        stt(dst[:], zraw[:, 0:SEQ], mu[:, 0, c:c + 1], dst[:], ALU.mult, ALU.add, ["zraw", "mu", dn], [dn])
        stt(dst[:], zraw[:, 2:SEQ + 2], mu[:, 1, c:c + 1], dst[:], ALU.mult, ALU.add, ["zraw", "mu", dn], [dn])

    zr_chunk(12, rT, "rT")
    act(lwin[0:64, :], rT[0:64, :], AF.Tanh, ["rT"], [("lwin", 0)])
    cp("dve", lwin[64:128, :], rT[64:128, :], ["rT"], [("lwin", 1)])
    zr_chunk(13, kTf, "kTf")
    act(sg[:], kTf[:], AF.Sigmoid, ["kTf"], ["sg"])

    tap("lwin", lwin[:], [128, SEQ], ["lwin"], BF16)
    tap("sg", sg[:], [128, SEQ], ["sg"], BF16)
    for hp in range(4):
        zr_chunk(hp, rT, "rT")
        if hp == 0:
            tap("zr0", rT[:], [128, SEQ], ["rT"])
        zr_chunk(4 + hp, kTf, "kTf")
        zr_chunk(8 + hp, kkT, "kkT")
        cp("act", vbf[:], kkT[:], ["kkT"], ["vbf"])
        for half in range(2):
            ps, pn = pb()
            for ti in range(8):
                t = half * 8 + ti
                tr(ps[:, ti * 128:(ti + 1) * 128], vbf[:, t * 128:(t + 1) * 128], ident[:], ["vbf", "ident"], [pn])
            cp("dve", Vtm[:, half * 8:(half + 1) * 8, :], ps[:, :].rearrange("p (t f) -> p t f", f=128), [pn], ["Vtm"])
        ts("dve", kkT[:], kTf[:], kkr[:, 0, hp:hp + 1], None, ALU.mult, None, ["kTf", "kkr", "vbf"], ["kkT"])
        for blk in range(4):
            tsl = slice(blk * 512, (blk + 1) * 512)
            act(sq_b[:], kkT[:, tsl], AF.Square, ["kkT"], ["sq_b"])
            ps, pn = pf()
            mm(ps[:, :], blockones[:], sq_b[:], True, True, ["blockones", "sq_b"], [pn])
            act(rs_b[:], ps[:, :], AF.Sqrt, [pn], ["rs_b"])
            ts("dve", rs_b[:], rs_b[:], 1e-6, None, ALU.max, None, ["rs_b"], ["rs_b"])
            S.op("dve", (lambda o: (lambda h: h.reciprocal(out=o, in_=o)))(rs_b[:]), reads=["rs_b"], writes=["rs_b"])
            tt("dve", kkT[:, tsl], kkT[:, tsl], rs_b[:], ALU.mult, ["kkT", "rs_b"], ["kkT"])
        if hp == 0:
            tap("kk0", kkT[:], [128, SEQ], ["kkT"])
            tap("vtm0", Vtm[:], [128, NT, 128], ["Vtm"], BF16)
        for dr in range(2):
            cdec = CDEC
            memset("pool", Mf[:], 0.0, ["Mf"])
            memset("pool", Mbf[:], 0.0, ["Mbf"])
            for hf in ((0, 1) if dr == 0 else (1, 0)):
                for blk in (2 * hf, 2 * hf + 1):
                    lb = blk % 2
                    hsl_ = slice(lb * 512, (lb + 1) * 512)
                    tsl = slice(blk * 512, (blk + 1) * 512)
                    ps2, pns = pf2()
                    mm(ps2[:, 0, :], up_sb[0:64, dr, hp * 128:(hp + 1) * 128], lwin[0:64, tsl], True, True, ["up_sb", "lwin"], [pns[0]])
                    mm(ps2[:, 1, :], up_sb[64:128, dr, hp * 128:(hp + 1) * 128], lwin[64:128, tsl], True, True, ["up_sb", "lwin"], [pns[1]])
                    act(sig_b[:], ps2[:, 0, :], AF.Sigmoid, [pns[0], "w0a0"], ["sig_b"], bias=w0a0[:, 0, dr, hp:hp + 1])
                    act(a_b[:], ps2[:, 1, :], AF.Sigmoid, [pns[1], "w0a0"], ["a_b"], bias=w0a0[:, 1, dr, hp:hp + 1])
                    if hp == 0 and blk == 0:
                        tap("sig%d" % dr, sig_b[:], [128, 512], ["sig_b"])
                        tap("a%d" % dr, a_b[:], [128, 512], ["a_b"])
                    if dr == 0:
                        cp("pool", asum[:, tsl], a_b[:], ["a_b"], [("asum", blk)])
                    else:
                        tt("pool", asum[:, tsl], asum[:, tsl], a_b[:], ALU.add, ["a_b", ("asum", blk)], [("asum", blk)])
                    for ti in range(4):
                        S.op("dve", (lambda o, d1: (lambda h: h.tensor_tensor_scan(out=o, data0=ones_f[:, 0:128], data1=d1, initial=0.0,
                                                                                  op0=ALU.mult, op1=ALU.add)))(
                            cs_b[:, ti * 128:(ti + 1) * 128], sig_b[:, ti * 128:(ti + 1) * 128]),
                            reads=["sig_b", "ones_f"], writes=[("cs_b", ti)])
                    tt("dve", e1_b[:], cs_b[:], sig_b[:], ALU.subtract, ["cs_b", "sig_b"], ["e1_b"])
                    csv = cs_b[:].rearrange("p (c t) -> p c t", t=128)
                    totb = csv[:, :, 127:128].broadcast_to([128, 4, 128])
                    tt("dve", tmc_b[:].rearrange("p (c t) -> p c t", t=128), totb, csv, ALU.subtract, ["cs_b"], ["tmc_b"])
                    if dr == 1:
                        tt("dve", tme_b[:].rearrange("p (c t) -> p c t", t=128), totb, e1_b[:].rearrange("p (c t) -> p c t", t=128),
                           ALU.subtract, ["cs_b", "e1_b"], ["tme_b"])
                        pin, pinn, pex, pexn, prem, premn = tme_b, "tme_b", tmc_b, "tmc_b", e1_b, "e1_b"
                    else:
                        pin, pinn, pex, pexn, prem, premn = cs_b, "cs_b", e1_b, "e1_b", tmc_b, "tmc_b"
                    act(pcs[:, blk * 4:(blk + 1) * 4], csv[:, :, 127], AF.Exp, ["cs_b"], [("pcs", blk)], scale=-cdec)
                    ts("dve", kd_b[:], a_b[:], -1.0, kkr[:, 1, hp:hp + 1], ALU.add, ALU.mult, ["a_b", "kkr"], ["kd_b"])
                    stt(kd_b[:], kd_b[:], 1.0, kTf[:, tsl], ALU.add, ALU.mult, ["kd_b", "kTf"], ["kd_b"])
                    tt("pool", akk_b[:], a_b[:], kkT[:, tsl], ALU.mult, ["a_b", "kkT"], ["akk_b"])
                    act(ex_b[0][:], pin[:], AF.Exp, [pinn], ["ex_b0"], scale=-cdec)
                    tt("pool", rt_[:, hsl_], rT[:, tsl], ex_b[0][:], ALU.mult, ["rT", "ex_b0"], [("rt_", lb)])
                    act(ex_b[1][:], pex[:], AF.Exp, [pexn], ["ex_b1"], scale=-cdec)
                    tt("pool", bt_[:, hsl_], kkT[:, tsl], ex_b[1][:], ALU.mult, ["kkT", "ex_b1"], [("bt_", lb)])
                    act(ex_b[0][:], pin[:], AF.Exp, [pinn], ["ex_b0"], scale=cdec)
                    tt("pool", kt_b[:], kd_b[:], ex_b[0][:], ALU.mult, ["kd_b", "ex_b0"], ["kt_b"])
                    stt(at_b[:], akk_b[:], -1.0, ex_b[0][:], ALU.mult, ALU.mult, ["akk_b", "ex_b0"], ["at_b"])
                    act(ex_b[1][:], prem[:], AF.Exp, [premn], ["ex_b1"], scale=-cdec)
                    tt("pool", khT_b[:], kd_b[:], ex_b[1][:], ALU.mult, ["kd_b", "ex_b1"], ["khT_b"])
                    stt(ahT_b[:], akk_b[:], -1.0, ex_b[1][:], ALU.mult, ALU.mult, ["akk_b", "ex_b1"], ["ahT_b"])
                    ps, pn = pb()
                    for ti in range(4):
                        tr(ps[:, (ti * 2) * 128:(ti * 2 + 1) * 128], khT_b[:, ti * 128:(ti + 1) * 128], ident[:], ["khT_b", "ident"], [pn])
                        tr(ps[:, (ti * 2 + 1) * 128:(ti * 2 + 2) * 128], ahT_b[:, ti * 128:(ti + 1) * 128], ident[:], ["ahT_b", "ident"], [pn])
                    cp("act", khah[:, lb * 4:(lb + 1) * 4, :, :], ps[:, :].rearrange("p (t q f) -> p t q f", t=4, q=2), [pn], [("khah", lb)])
                    for ti in range(4):
                        t = lb * 4 + ti
                        fsl = slice(t * 128, (t + 1) * 128)
                        lsl = slice(ti * 128, (ti + 1) * 128)
                        ps2, pns = pf2()
                        for hh in range(2):
                            hs = slice(hh * 64, (hh + 1) * 64)
                            mm(ps2[:, hh, 0:128], at_b[hs, lsl], bt_[hs, fsl], True, True, ["at_b", ("bt_", lb)], [pns[hh]])
                            mm(ps2[:, hh, 128:256], kt_b[hs, lsl], bt_[hs, fsl], True, True, ["kt_b", ("bt_", lb)], [pns[hh]])
                            mm(ps2[:, hh, 256:384], at_b[hs, lsl], rt_[hs, fsl], True, True, ["at_b", ("rt_", lb)], [pns[hh]])
                            mm(ps2[:, hh, 384:512], kt_b[hs, lsl], rt_[hs, fsl], True, True, ["kt_b", ("rt_", lb)], [pns[hh]])
                        tt("dve", G4[:, t, :, :, :].rearrange("p h f q -> p h (f q)"), ps2[:, :, :],
                           msk4(dr).unsqueeze(1).broadcast_to([128, 2, 512]), ALU.mult, pns + ["cpk"], [("G4", t)])
                        ps3, pns3 = pf2()
                        for hh in range(2):
                            hs = slice(hh * 64, (hh + 1) * 64)
                            mm(ps3[:, hh, 0:128], bt_[hs, fsl], at_b[hs, lsl], True, True, ["at_b", ("bt_", lb)], [pns3[hh]])
                        tt("dve", Awk[0][:, ti * 2:ti * 2 + 2, :], ps3[:, :, 0:128], mskA(dr).unsqueeze(1).broadcast_to([128, 2, 128]),
                           ALU.mult, pns3 + ["cpk"], ["Awk0"])
                        tt("dve", Bwk[0][:, ti * 2:ti * 2 + 2, :], ps2[:, :, 0:128], msk4(dr)[:, 0:128].unsqueeze(1).broadcast_to([128, 2, 128]),
                           ALU.mult, pns + ["cpk"], ["Bwk0"])
                    tt("pool", Sf[:], identf[:].unsqueeze(1).broadcast_to([128, 8, 128]), Bwk[0][:], ALU.add, ["identf", "Bwk0"], ["Sf"])
                    cur = 0
                    for lev in range(1, 7):
                        nxt = 1 - cur
                        an_c, bn_c, an_n, bn_n = "Awk%d" % cur, "Bwk%d" % cur, "Awk%d" % nxt, "Bwk%d" % nxt
                        for hb in range(2):
                            psA, pnA = pf()
                            for ii in range(4):
                                i_ = hb * 4 + ii
                                mm(psA[:, ii * 128:(ii + 1) * 128], Bwk[cur][:, i_, :], Awk[cur][:, i_, :], True, True, [an_c, bn_c], [pnA])
                            cp("act", Awk[nxt][:, hb * 4:(hb + 1) * 4, :], psA[:, :].rearrange("p (i f) -> p i f", f=128), [pnA], [(an_n, hb)])
                            if lev < 6:
                                psB, pnB = pf()
                                for ii in range(4):
                                    i_ = hb * 4 + ii
                                    mm(psB[:, ii * 128:(ii + 1) * 128], Awk[cur][:, i_, :], Bwk[cur][:, i_, :], True, True, [an_c, bn_c], [pnB])
                                cp("dve", Bwk[nxt][:, hb * 4:(hb + 1) * 4, :], psB[:, :].rearrange("p (i f) -> p i f", f=128), [pnB], [(bn_n, hb)])
                            psS, pnS = pf()
                            for ii in range(4):
                                i_ = hb * 4 + ii
                                mm(psS[:, ii * 128:(ii + 1) * 128], Awk[nxt][:, i_, :], Sf[:, i_, :], True, True, [(an_n, hb), ("Sf", hb)], [pnS])
                            if lev < 6:
                                tt("dve", Sf[:, hb * 4:(hb + 1) * 4, :], Sf[:, hb * 4:(hb + 1) * 4, :],
                                   psS[:, :].rearrange("p (i f) -> p i f", f=128), ALU.add, [pnS, ("Sf", hb)], [("Sf", hb)])
                            else:
                                t0 = lb * 4 + hb * 2
                                tt("dve", G4[:, t0:t0 + 2, :, 0, :], Sf[:, hb * 4:(hb + 1) * 4, :].rearrange("p (t h) f -> p t h f", h=2),
                                   psS[:, :].rearrange("p (t h f) -> p t h f", t=2, h=2), ALU.add, [pnS, ("Sf", hb)],
                                   [("G4", t0), ("G4", t0 + 1)])
                        cur = nxt
                if hp == 0 and dr == 0 and hf == 0:
                    tap("rt0", rt_[:], [128, SEQ // 2], ["rt_"], BF16)
                    tap("bt0", bt_[:], [128, SEQ // 2], ["bt_"], BF16)
                    tap("G40", G4[:], [128, NT // 2, 2, 4, 128], ["G4"], BF16)
                    tap("khah0", khah[:], [128, NT // 2, 2, 128], ["khah"], BF16)
                    tap("pcs0", pcs[:], [128, NT], ["pcs"])
                order = list(range(8 * hf, 8 * hf + 8)) if dr == 0 else list(range(8 * hf + 7, 8 * hf - 1, -1))
                for t in order:
                    tl = t - 8 * hf
                    fsl = slice(tl * 128, (tl + 1) * 128)
                    blk = t // 4
                    lb = blk % 2
                    psX, pnsX = pf2()
                    for hh in range(2):
                        hs = slice(hh * 64, (hh + 1) * 64)
                        mm(psX[:, hh, 0:64], bt_[hs, fsl], Mbf[hs, :], True, False, [("bt_", lb), "Mbf"], [pnsX[hh]])
                        mm(psX[:, hh, 0:64], G4[:, tl, hh, 1, :], Vtm[:, t, hs], False, True, [("G4", tl), "Vtm"], [pnsX[hh]])
                    cp("act", Xsb[:], psX[:, :, 0:64], pnsX, ["Xsb"])
                    psU, pnU = pf()
                    for hh in range(2):
                        mm(psU[:, hh * 64:(hh + 1) * 64], G4[:, tl, hh, 0, :], Xsb[:, hh, :], True, True, [("G4", tl), "Xsb"], [pnU])
                    cp("dve", Usb[:], psU[:, 0:128].rearrange("p (h v) -> p h v", h=2), [pnU], ["Usb"])
                    psY, pnsY = pf2()
                    for hh in range(2):
                        hs = slice(hh * 64, (hh + 1) * 64)
                        mm(psY[:, hh, 0:64], rt_[hs, fsl], Mbf[hs, :], True, False, [("rt_", lb), "Mbf"], [pnsY[hh]])
                        mm(psY[:, hh, 0:64], G4[:, tl, hh, 2, :], Usb[:, hh, :], False, False, [("G4", tl), "Usb"], [pnsY[hh]])
                        mm(psY[:, hh, 0:64], G4[:, tl, hh, 3, :], Vtm[:, t, hs], False, True, [("G4", tl), "Vtm"], [pnsY[hh]])
                    yv = ysum[:, t, :].rearrange("p (h v) -> p h v", h=2)
                    if dr == 0:
                        cp("act", yv, psY[:, :, 0:64], pnsY, [("ysum", t)])
                    else:
                        tt("dve", yv, yv, psY[:, :, 0:64], ALU.add, pnsY + [("ysum", t)], [("ysum", t)])
                    psM, pnM = pf()
                    for hh in range(2):
                        hs = slice(hh * 64, (hh + 1) * 64)
                        mm(psM[:, hs], khah[:, tl, 1, :], Usb[:, hh, :], True, False, [("khah", lb), "Usb"], [pnM])
                        mm(psM[:, hs], khah[:, tl, 0, :], Vtm[:, t, hs], False, True, [("khah", lb), "Vtm"], [pnM])
                    for hh in range(2):
                        hs = slice(hh * 64, (hh + 1) * 64)
                        stt(Mf[hs, :], Mf[hs, :], pcs[hs, t:t + 1], psM[hs, hs], ALU.mult, ALU.add, [("Mf", hh), ("pcs", blk), pnM], [("Mf", hh)])
                    cp("act", Mbf[:], Mf[:], ["Mf"], ["Mbf"])
            if hp == 0:
                tap("ys%d" % dr, ysum[:], [128, NT, 128], ["ysum"])
        yv3 = ysum[:].rearrange("p t (h v) -> p (t h) v", h=2)
        S.op("dve", lambda h: h.tensor_reduce(out=gn1[:], in_=yv3, axis=AX.X, op=ALU.add), reads=["ysum"], writes=["gn1"])
        ts("dve", gn1[:], gn1[:], 1.0 / 64, None, ALU.mult, None, ["gn1"], ["gn1"])
        tt("dve", yv3, yv3, gn1[:].unsqueeze(2).broadcast_to([128, 32, 64]), ALU.subtract, ["ysum", "gn1"], ["ysum"])
        kd4 = kkT[:].rearrange("p (t v) -> p t v", v=64)
        tt("pool", kd4, yv3, yv3, ALU.mult, ["ysum"], ["kkT"])
        S.op("dve", lambda h: h.tensor_reduce(out=gn2[:], in_=kd4, axis=AX.X, op=ALU.add), reads=["kkT"], writes=["gn2"])
        ts("dve", gn2[:], gn2[:], 1.0 / 64, 64e-5, ALU.mult, ALU.add, ["gn2"], ["gn2"])
        act(gn2[:], gn2[:], AF.Sqrt, ["gn2"], ["gn2"])
        S.op("dve", (lambda o: (lambda h: h.reciprocal(out=o, in_=o)))(gn2[:]), reads=["gn2"], writes=["gn2"])
        tt("dve", yv3, yv3, gn2[:].unsqueeze(2).broadcast_to([128, 32, 64]), ALU.mult, ["ysum", "gn2"], ["ysum"])
        y3 = ysum[:]
        tt("dve", y3, y3, lnx[:, 0, hp * 128:(hp + 1) * 128].unsqueeze(1).broadcast_to([128, NT, 128]), ALU.mult, ["ysum", "lnx"], ["ysum"])
        tt("pool", y3, y3, lnx[:, 1, hp * 128:(hp + 1) * 128].unsqueeze(1).broadcast_to([128, NT, 128]), ALU.add, ["ysum", "lnx"], ["ysum"])
        ts("dve", asum[:], asum[:], 0.5, -1.0, ALU.mult, ALU.add, ["asum"], ["asum"])
        ts("dve", asum[:], asum[:], kkr[:, 1, hp:hp + 1], 1.0, ALU.mult, ALU.add, ["asum", "kkr"], ["asum"])
        tt("pool", asum[:], asum[:], kTf[:], ALU.mult, ["asum", "kTf"], ["asum"])
        tt("pool", asum[:], asum[:], rT[:], ALU.mult, ["asum", "rT"], ["asum"])
        ts("dve", vbf[:], asum[:], kkr[:, 2, hp:hp + 1], None, ALU.mult, None, ["asum", "kkr", "Vtm"], ["vbf"])
        ps, pn = pf()
        for t in range(NT):
            mm(ps[:, t * 2:t * 2 + 2], vbf[:, t * 128:(t + 1) * 128], hsel, True, True, ["vbf", "cpk"], [pn])
        cp("dve", coef[:], ps[:, 0:32], [pn], ["coef"])
        tt("dve", kd4, Vtm[:].rearrange("p t (h v) -> p (t h) v", h=2), coef[:].unsqueeze(2).broadcast_to([128, 32, 64]), ALU.mult,
           ["Vtm", "coef", "kkT"], ["kkT"])
        tt("pool", yv3, yv3, kd4, ALU.add, ["ysum", "kkT"], ["ysum"])
        for g4 in range(4):
            ps, pn = pf()
            for ti in range(4):
                t = g4 * 4 + ti
                mm(ps[:, ti * 128:(ti + 1) * 128], sg[:, t * 128:(t + 1) * 128], gup[:, hp * 128:(hp + 1) * 128], True, True, ["sg", "gup"], [pn])
            tt("dve", rwo[:, g4 * 4:(g4 + 1) * 4, :], ysum[:, g4 * 4:(g4 + 1) * 4, :], ps[:, :].rearrange("p (t f) -> p t f", f=128), ALU.mult,
               ["ysum", pn], [("rwo", g4)])
        for half in range(2):
            ps, pn = pb()
            for ti in range(8):
                t = half * 8 + ti
                tr(ps[:, ti * 128:(ti + 1) * 128], rwo[:, t, :], ident[:], ["rwo", "ident"], [pn])
            cp("act", rwT[:, hp, half * 1024:(half + 1) * 1024], ps[:, :], [pn], [("rwT", hp, half)])
    tap("rwT", rwT[:], [128, 4, SEQ], ["rwT"], BF16)
    S.fence()
    A.release(m3)

    qkg = A.alloc("qkg", [128, 2], F32)
    dma("sp", qkg[:], qkg_d, [], ["qkg"])
    ts("dve", qkg[:, 0:1], qkg[:, 0:1], 0.125, None, ALU.mult, None, ["qkg"], ["qkg"])
    NVT = 17 + 20 + 16
    m2 = A.mark()
    Vaug = A.alloc("Vaug", [128, NVT, 2, 65], BF16)
    memset("pool", Vaug[:, :, :, 64:65], 1.0, ["Vaug"])
    qT = A.alloc("qT", [128, SEQ], BF16)
    kT = A.alloc("kT", [128, SEQ], BF16)
    wq = [A.alloc("wqkv%d" % i, [128, 8, 128], BF16) for i in range(3)]
    acc = A.alloc("acc", [128, 2, SEQ], F32)
    EB = A.alloc("EB", [128, 7, 2, 2, 128], BF16)
    sqb = [A.alloc("sqb%d" % i, [128, 512], BF16) for i in range(2)]
    rsb = [A.alloc("rsb%d" % i, [128, 512], F32) for i in range(2)]
    Eb = [A.alloc("Eb%d" % i, [128, 2, 2, 128], BF16) for i in range(3)]
    PTb = [A.alloc("PTb%d" % i, [128, 2, 2, 128], BF16) for i in range(3)]
    win_v = win_d.rearrange("(j p) n -> p j n", p=128)
    PATS = ((1, 0), (4, 17), (16, 37))
    blkc = [0]
    for hp in range(4):
        dma("sp", EB[:], eb_d[hp].rearrange("p (s h k q) -> p s h k q", s=7, h=2, k=2), [], ["EB"])
        for i, c0 in enumerate((hp * 128, 512 + hp * 128, 1024 + hp * 128)):
            dma("pool", wq[i][:], win_v[:, :, c0:c0 + 128], [], ["wqkv%d" % i])
        for i, (dst, dn) in enumerate(((qT, "qT"), (kT, "kT"))):
            for blk in range(4):
                tsl = slice(blk * 512, (blk + 1) * 512)
                ps, pn = pf()
                for j in range(8):
                    mm(ps[:, :], wq[i][:, j, :], hT[:, j, tsl], j == 0, j == 7, ["wqkv%d" % i, "hT"], [pn])
                bi = blkc[0] % 2
                blkc[0] += 1
                act(sqb[bi][:], ps[:, :], AF.Square, [pn], ["sqb%d" % bi])
                ps2, pn2 = pf()
                mm(ps2[:, :], blockones[:], sqb[bi][:], True, True, ["blockones", "sqb%d" % bi], [pn2])
                act(rsb[bi][:], ps2[:, :], AF.Sqrt, [pn2], ["rsb%d" % bi], bias=1e-6, scale=1.0 / 64)
                S.op("dve", (lambda o: (lambda h: h.reciprocal(out=o, in_=o)))(rsb[bi][:]), reads=["rsb%d" % bi], writes=["rsb%d" % bi])
                stt(dst[:, tsl], ps[:, :], qkg[:, i:i + 1], rsb[bi][:], ALU.mult, ALU.mult, [pn, "qkg", "rsb%d" % bi], [(dn, blk)])
        vtl = []
        for (dil, base) in PATS:
            L = SEQ // dil
            nb = L // 128
            for r in range(dil):
                if nb == 1:
                    vtl.append((base + r, r, dil))
                else:
                    for m in range(nb + 1):
                        l0 = 0 if m == 0 else (L - 128 if m == nb else 128 * m - 64)
                        vtl.append((base + r * (nb + 1) + m, r + dil * l0, dil))
        assert len(vtl) == NVT and [v[0] for v in vtl] == list(range(NVT))
        for g0 in range(0, NVT, 4):
            grp = vtl[g0:g0 + 4]
            ps, pn = pf()
            for gi, (vt, st, dil) in enumerate(grp):
                for j in range(8):
                    mm(ps[:, gi * 128:(gi + 1) * 128], hT[:, j, st:st + 127 * dil + 1:dil], wq[2][:, j, :], j == 0, j == 7,
                       ["hT", "wqkv2"], [pn])
            n = len(grp)
            cp("act", Vaug[:, g0:g0 + n, :, 0:64], ps[:, 0:n * 128].rearrange("p (t h e) -> p t h e", t=n, h=2),
               [pn], [("Vaug", g0 // 4)])
        for pi, (dil, base) in enumerate(PATS):
            L = SEQ // dil
            nb = L // 128
            for r in range(dil):
                for qb in range(nb):
                    qs = r + dil * 128 * qb
                    qsl = slice(qs, qs + 127 * dil + 1, dil)
                    if nb == 1:
                        kts = [(base + r, r)]
                        eset = 6
                    else:
                        mA, mB = qb, qb + 1
                        lA = 0 if qb == 0 else 128 * qb - 64
                        lB = 128 * qb + 64 if qb < nb - 1 else L - 128
                        kts = [(base + r * (nb + 1) + mA, r + dil * lA), (base + r * (nb + 1) + mB, r + dil * lB)]
                        eset = pi * 3 + (0 if qb == 0 else (2 if qb == nb - 1 else 1))
                    nk = len(kts)
                    ps, pns = pf2()
                    for hh in range(2):
                        for kt, (vt, ks) in enumerate(kts):
                            mm(ps[:, hh, kt * 128:(kt + 1) * 128], kT[hh * 64:(hh + 1) * 64, ks:ks + 127 * dil + 1:dil],
                               qT[hh * 64:(hh + 1) * 64, qsl], True, True, ["kT", "qT"], [pns[hh]])
                    bi = blkc[0] % 3
                    blkc[0] += 1
                    psv = ps[:, :, 0:256].rearrange("p h (k q) -> p h k q", k=2)[:, :, 0:nk, :]
                    act(Eb[bi][:, :, 0:nk, :], psv, AF.Exp, pns, ["Eb%d" % bi])
                    tt("pool", PTb[bi][:, :, 0:nk, :], Eb[bi][:, :, 0:nk, :], EB[:, eset, :, 0:nk, :], ALU.mult,
                       ["Eb%d" % bi, "EB"], ["PTb%d" % bi])
                    ps2, pn2 = pf()
                    for hh in range(2):
                        for kt, (vt, ks) in enumerate(kts):
                            mm(ps2[0:65, hh * 128:(hh + 1) * 128], Vaug[:, vt, hh, :], PTb[bi][:, hh, kt, :], kt == 0,
                               kt == nk - 1, [("Vaug", vt // 4), "PTb%d" % bi], [pn2])
                    asl = acc[0:65, :, qsl]
                    p2v = ps2[0:65, 0:256].rearrange("p (h q) -> p h q", h=2)
                    if pi == 0:
                        cp("dve", asl, p2v, [pn2], [("acc", pi, r, qb)])
                    else:
                        tt("dve", asl, p2v, asl, ALU.add, [pn2, "acc"], ["acc"])
        for hh in range(2):
            for blk in range(4):
                tsl = slice(blk * 512, (blk + 1) * 512)
                ps, pn = pf()
                mm(ps[0:64, :], ones_f[64:65, 0:64], acc[64:65, hh, tsl], True, True, ["ones_f", "acc"], [pn])
                bi = blkc[0] % 2
                blkc[0] += 1
                S.op("dve", (lambda o, i_: (lambda h: h.reciprocal(out=o, in_=i_)))(rsb[bi][0:64, :], ps[0:64, :]),
                     reads=[pn], writes=["rsb%d" % bi])
                tt("dve", attT[0:64, 2 * hp + hh, tsl], acc[0:64, hh, tsl], rsb[bi][0:64, :], ALU.mult,
                   ["acc", "rsb%d" % bi], [("attT", hp, hh, blk)])
    tap("attT", attT[0:64], [64, 8, SEQ], ["attT"], BF16)
    S.fence()
    A.release(m2)

    A.release(m_ht)
    x1all = A.alloc("x1all", [128, NT, D], F32)
    bc3 = A.alloc("bc3", [128, 3, D], F32)
    dma("sp", bc3[:, 0, :], bc_d[1], ["bc_d"], [("bc3", 0)])
    dma("sp", bc3[:, 1, :], bc_d[2], ["bc_d"], [("bc3", 1)])
    dma("sp", bc3[:, 2, :], bc_d[3], ["bc_d"], [("bc3", 2)])
    m4 = A.mark()
    woutA = A.alloc("woutA", [64, 8, D], BF16)
    woutR = A.alloc("woutR", [128, 4, D], BF16)
    g1b = A.alloc("g1b", [128, D], F32)
    xts = [A.alloc("xts%d" % i, [128, D], F32) for i in range(2)]
    dma("sp", g1b[:], bc_d[0], ["bc_d"], ["g1b"])
    dma("pool", woutA[:], wout_d[0:512, :].rearrange("(h p) n -> p h n", p=64), [], ["woutA"])
    dma("pool", woutR[:], wout_d[512:1024, :].rearrange("(c p) n -> p c n", p=128), [], ["woutR"])
    tt("pool", woutA[:], woutA[:], g1b[0:64, :].unsqueeze(1).broadcast_to([64, 8, D]), ALU.mult, ["woutA", "g1b"], ["woutA"])
    tt("pool", woutR[:], woutR[:], g1b[:].unsqueeze(1).broadcast_to([128, 4, D]), ALU.mult, ["woutR", "g1b"], ["woutR"])
    for t in range(NT):
        fsl = slice(t * 128, (t + 1) * 128)
        xb_, xnm = xts[t % 2], "xts%d" % (t % 2)
        dma("sp", xb_[:], x_v[t], [], [xnm])
        for nb in range(2):
            ps, pn = pf()
            csl = slice(nb * 512, (nb + 1) * 512)
            for h_ in range(8):
                mm(ps[:, :], attT[0:64, h_, fsl], woutA[0:64, h_, csl], h_ == 0, False, ["attT", "woutA"], [pn])
            for c_ in range(4):
                mm(ps[:, :], rwT[:, c_, fsl], woutR[:, c_, csl], False, c_ == 3, ["rwT", "woutR"], [pn])
            tt("dve", x1all[:, t, csl], ps[:, :], xb_[:, csl], ALU.add, [pn, xnm], [("x1all", t, nb)])
    if "x1" in dbg:
        tap("x1", x1all[:], [128, NT, D], ["x1all"])
    S.fence()
    A.release(m4)

    top4 = A.mark()
    A.off = m_rw
    wq_bf = A.alloc("wq_bf", [128, 8, 2048], BF16)
    skT = A.alloc("skT", [128, 2, 128], BF16)
    iota16 = A.alloc("iota16", [128, 16], F32)
    NGB = 3
    gu = [A.alloc("gu%d" % i, [128, D], BF16) for i in range(NGB)]
    gv = [A.alloc("gv%d" % i, [128, D], BF16) for i in range(NGB)]
    assert A.off <= m_ht
    A.off = top4
    dma("pool", wq_bf[:], wqry_d.rearrange("(j p) n -> p j n", p=128), [], ["wq_bf"])
    dma("pool", skT[:], skT_d, [], ["skT"])
    dma("sp", iota16[:], iota_d, [], ["iota16"])
    h2 = A.alloc("h2", [128, D], F32)
    h2bf = A.alloc("h2bf", [128, D], BF16)
    h2T = A.alloc("h2T", [128, 8, 128], BF16)
    qTs = A.alloc("qTs", [128, 16, 128], BF16)
    bigA = A.alloc("bigA", [128, 2048], F32)
    wk = [A.alloc("wk%d" % i, [128, 128], F32) for i in range(2)]
    wk2 = [A.alloc("wk2%d" % i, [128, 256], F32) for i in range(2)]
    vals = A.alloc("vals", [128, 16, 16], F32)
    idxu = A.alloc("idxu", [128, 16, 16], U32)
    idxf = A.alloc("idxf", [128, 16, 16], F32)
    tops = A.alloc("tops", [128, 8, 16], F32)
    topp = A.alloc("topp", [128, 8, 16], U32)
    hiu = A.alloc("hiu", [128, 8, 16], U32)
    lou = A.alloc("lou", [128, 8, 16], U32)
    hif = A.alloc("hif", [128, 8, 16], F32)
    lof = A.alloc("lof", [128, 8, 16], F32)
    i1s = A.alloc("i1s", [128, 8, 16], F32)
    i2s = A.alloc("i2s", [128, 8, 16], F32)
    eidx = A.alloc("eidx", [128, 128], I32)
    gate = A.alloc("gate", [128, 8, 16], F32)
    gsum = A.alloc("gsum", [128, 8], F32)
    ss2 = A.alloc("ss2", [128, 1], F32)
    pre = A.alloc("pre", [128, 128], F32)
    coefm = A.alloc("coefm", [128, 128], F32)
    junkb = A.alloc("junkb", [128, D], BF16)
    dg = [A.alloc("dg%d" % i, [128, 16, 128], BF16) for i in range(2)]
    otile = A.alloc("otile", [128, D], F32)
    out_v = out_d.rearrange("(t p) d -> t p d", p=128)
    for t in range(NT):
        x1 = x1all[:, t, :]
        x1n = ("x1all", t)
        act(junkb[:], x1, AF.Square, [x1n], ["junkb", "ss2"], accum=ss2[:, 0:1])
        ts("dve", ss2[:], ss2[:], 1.0 / D, 1e-6, ALU.mult, ALU.add, ["ss2"], ["ss2"])
        act(ss2[:], ss2[:], AF.Sqrt, ["ss2"], ["ss2"])
        S.op("dve", (lambda o: (lambda h: h.reciprocal(out=o, in_=o)))(ss2[:]), reads=["ss2"], writes=["ss2"])
        stt(h2[:], x1, ss2[:, 0:1], bc3[:, 0, :], ALU.mult, ALU.mult, [x1n, "ss2", ("bc3", 0)], ["h2"])
        tt("pool", h2[:], h2[:], bc3[:, 1, :], ALU.add, ["h2", ("bc3", 1)], ["h2"])
        cp("act", h2bf[:], h2[:], ["h2"], ["h2bf"])
        ps, pn = pb()
        for j in range(8):
            tr(ps[:, j * 128:(j + 1) * 128], h2bf[:, j * 128:(j + 1) * 128], ident[:], ["h2bf", "ident"], [pn])
        cp("dve", h2T[:], ps[:, :].rearrange("p (j t) -> p j t", t=128), [pn], ["h2T"])
        for g4 in range(4):
            ps, pn = pf()
            for gi in range(4):
                g_ = g4 * 4 + gi
                for kc in range(8):
                    mm(ps[:, gi * 128:(gi + 1) * 128], wq_bf[:, kc, g_ * 128:(g_ + 1) * 128], h2T[:, kc, :], kc == 0, kc == 7,
                       ["wq_bf", "h2T"], [pn])
            cp("act", qTs[:, g4 * 4:(g4 + 1) * 4, :], ps[:, :].rearrange("p (g t) -> p g t", t=128), [pn], [("qTs", g4)])
        for g4 in range(4):
            ps, pn = pf()
            for gi in range(4):
                g_ = g4 * 4 + gi
                mm(ps[:, gi * 128:(gi + 1) * 128], qTs[:, g_, :], skT[:, g_ % 2, :], True, True, [("qTs", g4), "skT"], [pn])
            cp("act", bigA[:, g4 * 512:(g4 + 1) * 512], ps[:, :], [pn], [("bigA", g4)])
        for g_ in range(16):
            scg = bigA[:, g_ * 128:(g_ + 1) * 128]
            w_, wn_ = wk[g_ % 2], "wk%d" % (g_ % 2)
            S.op("dve", (lambda o, i_: (lambda h: h.max(out=o, in_=i_)))(vals[:, g_, 0:8], scg), reads=[("bigA", g_ // 4)], writes=[("vals", g_, 0)])
            S.op("dve", (lambda o, r_, i_: (lambda h: h.match_replace(out=o, in_to_replace=r_, in_values=i_, imm_value=-1e30)))(
                w_[:], vals[:, g_, 0:8], scg), reads=[("bigA", g_ // 4), ("vals", g_, 0)], writes=[wn_])
            S.op("dve", (lambda o, i_: (lambda h: h.max(out=o, in_=i_)))(vals[:, g_, 8:16], w_[:]), reads=[wn_], writes=[("vals", g_, 1)])
            for hf_ in range(2):
                S.op("dve", (lambda o, m_, i_: (lambda h: h.max_index(out=o, in_max=m_, in_values=i_)))(
                    idxu[:, g_, hf_ * 8:(hf_ + 1) * 8], vals[:, g_, hf_ * 8:(hf_ + 1) * 8], scg),
                    reads=[("bigA", g_ // 4), ("vals", g_, hf_)], writes=[("idxu", g_, hf_)])
        cp("dve", idxf[:], idxu[:], ["idxu"], ["idxf"])
        vv = vals[:].rearrange("p (h s) k -> p h s k", s=2)
        cand = bigA[:].rearrange("p (h i j) -> p h i j", h=8, i=16)
        tt("dve", cand, vv[:, :, 0, :].unsqueeze(3).broadcast_to([128, 8, 16, 16]), vv[:, :, 1, :].unsqueeze(2).broadcast_to([128, 8, 16, 16]),
           ALU.add, ["vals"], ["bigA"])
        for h_ in range(8):
            cg = bigA[:, h_ * 256:(h_ + 1) * 256]
            w_, wn_ = wk2[h_ % 2], "wk2%d" % (h_ % 2)
            S.op("dve", (lambda o, i_: (lambda h: h.max(out=o, in_=i_)))(tops[:, h_, 0:8], cg), reads=["bigA"], writes=[("tops", h_, 0)])
            S.op("dve", (lambda o, r_, i_: (lambda h: h.match_replace(out=o, in_to_replace=r_, in_values=i_, imm_value=-1e30)))(
                w_[:], tops[:, h_, 0:8], cg), reads=["bigA", ("tops", h_, 0)], writes=[wn_])
            S.op("dve", (lambda o, i_: (lambda h: h.max(out=o, in_=i_)))(tops[:, h_, 8:16], w_[:]), reads=[wn_], writes=[("tops", h_, 1)])
            for hf_ in range(2):
                S.op("dve", (lambda o, m_, i_: (lambda h: h.max_index(out=o, in_max=m_, in_values=i_)))(
                    topp[:, h_, hf_ * 8:(hf_ + 1) * 8], tops[:, h_, hf_ * 8:(hf_ + 1) * 8], cg),
                    reads=["bigA", ("tops", h_, hf_)], writes=[("topp", h_, hf_)])
        S.op("dve", lambda h: h.tensor_single_scalar(out=hiu[:], in_=topp[:], scalar=4, op=ALU.logical_shift_right), reads=["topp"], writes=["hiu"])
        S.op("dve", lambda h: h.tensor_single_scalar(out=lou[:], in_=topp[:], scalar=15, op=ALU.bitwise_and), reads=["topp"], writes=["lou"])
        cp("dve", hif[:], hiu[:], ["hiu"], ["hif"])
        cp("dve", lof[:], lou[:], ["lou"], ["lof"])
        idv = idxf[:].rearrange("p (h s) k -> p h s k", s=2)
        eq = bigA[:].rearrange("p (h k i) -> p h k i", h=8, k=16)
        io_b = iota16[:].unsqueeze(1).unsqueeze(1).broadcast_to([128, 8, 16, 16])
        for (sel, seln, src_i, dst, dstn) in ((hif, "hif", 0, i1s, "i1s"), (lof, "lof", 1, i2s, "i2s")):
            tt("dve", eq, sel[:].unsqueeze(3).broadcast_to([128, 8, 16, 16]), io_b, ALU.is_equal, [seln, "iota16", "tops", "topp"], ["bigA"])
            tt("dve", eq, eq, idv[:, :, src_i, :].unsqueeze(2).broadcast_to([128, 8, 16, 16]), ALU.mult, ["bigA", "idxf"], ["bigA"])
            S.op("dve", (lambda o: (lambda h: h.tensor_reduce(out=o, in_=eq, axis=AX.X, op=ALU.add)))(dst[:]), reads=["bigA"], writes=[dstn])
        stt(i1s[:], i1s[:], 128.0, i2s[:], ALU.mult, ALU.add, ["i1s", "i2s"], ["i1s"])
        cp("dve", eidx[:], i1s[:].rearrange("p h k -> p (h k)"), ["i1s"], ["eidx"])
        tt("dve", gate[:], tops[:], tops[:, :, 0:1].broadcast_to([128, 8, 16]), ALU.subtract, ["tops"], ["gate"])
        act(gate[:], gate[:], AF.Exp, ["gate"], ["gate"])
        S.op("dve", lambda h: h.tensor_reduce(out=gsum[:], in_=gate[:], axis=AX.X, op=ALU.add), reads=["gate"], writes=["gsum"])
        S.op("dve", lambda h: h.reciprocal(out=gsum[:], in_=gsum[:]), reads=["gsum"], writes=["gsum"])
        tt("dve", gate[:], gate[:], gsum[:].unsqueeze(2).broadcast_to([128, 8, 16]), ALU.mult, ["gate", "gsum"], ["gate"])
        for j in range(128):
            b_ = j % NGB
            S.op("pool", (lambda o, ix: (lambda h: h.indirect_dma_start(out=o, out_offset=None, in_=peeru_d[:, :],
                                                                         in_offset=bass.IndirectOffsetOnAxis(ap=ix, axis=0))))(
                gu[b_][:], eidx[:, j:j + 1]), reads=["eidx"], writes=["gu%d" % b_], dma=True)
            stt(junkb[:], gu[b_][:], 1.0, h2bf[:], ALU.mult, ALU.mult, ["gu%d" % b_, "h2bf"], ["junkb", ("pre", j)], accum=pre[:, j:j + 1])
        act(coefm[:], pre[:], AF.Gelu, ["pre"], ["coefm"])
        tt("dve", coefm[:], coefm[:], gate[:].rearrange("p h k -> p (h k)"), ALU.mult, ["coefm", "gate"], ["coefm"])
        psO = [pf(), pf()]
        for j in range(128):
            b_ = j % NGB
            if j % 16 == 0:
                dgi = (j // 16) % 2
                tt("dve", dg[dgi][:], ident[:].unsqueeze(1).broadcast_to([128, 16, 128]),
                   coefm[:, j:j + 16].unsqueeze(2).broadcast_to([128, 16, 128]), ALU.mult, ["ident", "coefm"], ["dg%d" % dgi])
            S.op("pool", (lambda o, ix: (lambda h: h.indirect_dma_start(out=o, out_offset=None, in_=peerv_d[:, :],
                                                                         in_offset=bass.IndirectOffsetOnAxis(ap=ix, axis=0))))(
                gv[b_][:], eidx[:, j:j + 1]), reads=["eidx"], writes=["gv%d" % b_], dma=True)
            for nb in range(2):
                mm(psO[nb][0][:, :], dg[dgi][:, j % 16, :], gv[b_][:, nb * 512:(nb + 1) * 512], j == 0, j == 127,
                   ["dg%d" % dgi, "gv%d" % b_], [psO[nb][1]])
        for nb in range(2):
            csl = slice(nb * 512, (nb + 1) * 512)
            tt("dve", otile[:, csl], psO[nb][0][:, :], bc3[:, 2, csl], ALU.mult, [psO[nb][1], ("bc3", 2)], [("otile", nb)])
            tt("pool", otile[:, csl], otile[:, csl], x1all[:, t, csl], ALU.add, [("otile", nb), x1n], [("otile", nb)])
        dma("sp", out_v[t], otile[:], ["otile"], ["out_d"])
        if t == 0:
            tap("h2", h2[:], [128, D], ["h2"])
            tap("eidx", eidx[:], [128, 128], ["eidx"], I32)
            tap("gate", gate[:], [128, 8, 16], ["gate"])
            tap("pre", pre[:], [128, 128], ["pre"])
    if dbg:
        print("ops", len(S.ops), {e: S.count[e] for e in ENGS})
    S.emit()
    return taps


def host_inputs(inputs, b):
    f = lambda k: np.ascontiguousarray(inputs[k][0], dtype=np.float32)
    fp = lambda v, n: np.ascontiguousarray(v.reshape(n, 128).T)
    m = {}
    m["x"] = np.ascontiguousarray(inputs["x"][b], dtype=np.float32)
    m["c_fp"] = fp(np.asarray(inputs["c"][b], dtype=np.float32), 8)
    m["ada_w"] = f("ada_w")
    m["ada_b"] = f("ada_b").reshape(1, -1)
    m["g1_fp"] = fp(f("norm1_g"), 8)
    m["g2_fp"] = fp(f("norm2_g"), 8)
    m["g2_b"] = np.ascontiguousarray(np.broadcast_to(f("norm2_g")[None, :], (128, D)))
    m["w_in"] = f("w_in")
    m["qkg"] = np.ascontiguousarray(np.stack([np.tile(f("q_norm_g"), 2), np.tile(f("k_norm_g"), 2)], axis=1))
    m["eb"] = EB_CONST
    m["up"] = np.ascontiguousarray(np.concatenate([f("w_decay_up"), f("a_gate_up")], axis=1).transpose(1, 0, 2))
    m["g_up"] = f("g_up")
    m["lnx"] = np.ascontiguousarray(np.broadcast_to(np.stack([f("lnx_g"), f("lnx_b")])[None], (128, 2, 512)))
    m["mu"] = np.ascontiguousarray(np.stack([fp(f("mu_prev"), 14), fp(f("mu_next"), 14)], axis=1))
    m["w0a0"] = np.ascontiguousarray(np.stack([f("w_decay0").reshape(2, 4, 128), f("a_gate0").reshape(2, 4, 128)]).transpose(3, 0, 1, 2))
    m["kkr"] = np.ascontiguousarray(np.stack([fp(f("k_k"), 4), fp(f("k_a"), 4), fp(f("r_k").reshape(-1), 4)], axis=1))
    m["cpk"] = CPK_CONST
    m["w_out"] = f("w_out")
    m["w_query"] = f("peer_w_query")
    m["skT"] = np.ascontiguousarray(np.stack([f("peer_sub_keys1").T, f("peer_sub_keys2").T], axis=1))
    m["iota16"] = np.ascontiguousarray(np.broadcast_to(np.arange(16, dtype=np.float32)[None, :], (128, 16)))
    m["peer_u"] = f("peer_u")
    m["peer_v"] = f("peer_v")
    return m


def _make_cpk():
    r = np.arange(128)[:, None]
    c = np.arange(128)[None, :]
    SL, SU, IL, IU = (r > c), (r < c), (r >= c), (r <= c)
    out = np.zeros((128, 1282), np.float32)
    out[:, 0:640] = np.concatenate([SU, SU, IU, IU, SL], axis=1)
    out[:, 640:1280] = np.concatenate([SL, SL, IL, IL, SU], axis=1)
    out[0:64, 1280] = 1.0
    out[64:128, 1281] = 1.0
    return np.ascontiguousarray(out.astype(ml_dtypes.bfloat16))


CPK_CONST = _make_cpk()


def _make_eb():
    eb = np.zeros((4, 128, 7, 2, 2, 128), np.float64)
    k = np.arange(128)[:, None]
    q = np.arange(128)[None, :]
    for hp in range(4):
        for hh in range(2):
            slope = 2.0 ** (-(2 * hp + hh + 1))
            for pi, dil in enumerate((1, 4)):
                for ty in range(3):
                    if ty == 0:
                        off = q - k
                        own = k < 64
                    else:
                        off = q - (k - 64)
                        own = k >= 0
                    eb[hp, :, pi * 3 + ty, hh, 0, :] = np.where((np.abs(off) <= 64) & own, np.exp(-slope * dil * np.abs(off)), 0.0)
                    if ty == 2:
                        off = q - k
                        own = k >= 64
                    else:
                        off = q - (k + 64)
                        own = k >= 0
                    eb[hp, :, pi * 3 + ty, hh, 1, :] = np.where((np.abs(off) <= 64) & own, np.exp(-slope * dil * np.abs(off)), 0.0)
            off = q - k
            eb[hp, :, 6, hh, 0, :] = np.where(np.abs(off) <= 64, np.exp(-slope * 16 * np.abs(off)), 0.0)
    return np.ascontiguousarray(eb.reshape(4, 128, -1).astype(ml_dtypes.bfloat16))


EB_CONST = _make_eb()


def kernel(**inputs):
    nc = bass.Bass("TRN2", target_bir_lowering=False)
    build(nc)
    in_maps = [host_inputs(inputs, b) for b in range(8)]
    res = run_bass_kernel_spmd(nc, in_maps, core_ids=list(range(8)))
    return np.stack([np.asarray(r["out"], dtype=np.float32) for r in res.results], axis=0)
```

```python
import contextlib
import numpy as np
import ml_dtypes
import concourse.bass as bass
import concourse.mybir as mybir
from concourse.bass_utils import run_bass_kernel_spmd

F32 = mybir.dt.float32
BF16 = mybir.dt.bfloat16
I32 = mybir.dt.int32
U32 = mybir.dt.uint32
ALU = mybir.AluOpType
AF = mybir.ActivationFunctionType
AX = mybir.AxisListType

N_DMA_SEMS = 24
ENGS = ("pe", "dve", "act", "pool", "sp")

D = 1024
SEQ = 2048
NT = SEQ // 128
INW = 3328
CDEC = float(np.exp(-0.5))
import os
MAXOPS = int(os.environ.get("MK_MAXOPS", "100000000"))
SW_CLEAR = False


class Op:
    __slots__ = ("eng", "fn", "deps", "done", "dma", "clear", "nofence")

    def __init__(self, eng, fn, dma):
        self.eng = eng
        self.fn = fn
        self.dma = dma
        self.deps = []
        self.done = None
        self.clear = None
        self.nofence = False


class Sched:
    def __init__(self, nc):
        self.nc = nc
        self.ops = []
        self.state = {}
        self.count = {e: 0 for e in ENGS}
        self.dma_uses = [0] * N_DMA_SEMS
        self.dma_last = [None] * N_DMA_SEMS
        self.dma_rr = 0
        self.fence_ops = []
        self.resetting = set()
        self.sw_count = 0
        self.sw_gen = {}
        self.sems = {}

    def _keys(self, buf, sub):
        d = self.state.setdefault(buf, {})
        if sub is None:
            keys = list(d.keys())
            if None not in d:
                keys.append(None)
        else:
            keys = [sub, None]
        return d, keys

    def fence(self):
        last = {}
        for o in self.ops:
            if o.nofence:
                continue
            s, v = o.done
            if v > last.get(s, (0, None))[0]:
                last[s] = (v, o)
        self.fence_ops = [o for (_, o) in last.values()]
        self.state = {b: st for b, st in self.state.items() if b in ("tab",)}

    def op(self, eng, fn, reads=(), writes=(), dma=False, swsem=None):
        if len(self.ops) >= MAXOPS:
            return None
        clr = None
        if dma and swsem is not None and SW_CLEAR:
            clr = self.op("pool", (lambda key: (lambda h: h.sem_clear(self.sems[key])))(swsem), reads=reads, writes=writes)
        o = Op(eng, fn, dma)
        o.clear = clr
        deps = list(self.fence_ops)
        if clr is not None:
            deps.append(clr)
        for r in reads:
            buf, sub = (r[0], tuple(r[1:])) if isinstance(r, tuple) else (r, None)
            d, keys = self._keys(buf, sub)
            for k in keys:
                st = d.get(k)
                if st and st[0] is not None:
                    deps.append(st[0])
        for w in writes:
            buf, sub = (w[0], tuple(w[1:])) if isinstance(w, tuple) else (w, None)
            d, keys = self._keys(buf, sub)
            for k in keys:
                st = d.get(k)
                if st:
                    if st[0] is not None:
                        deps.append(st[0])
                    deps.extend(st[1])
        if dma and swsem is not None:
            if SW_CLEAR:
                self.resetting.add(swsem)
            self.sw_gen[swsem] = self.sw_gen.get(swsem, 0) + 1
            o.done = (swsem, 16 * self.sw_gen[swsem])
        elif dma and eng == "pool":
            self.sw_count += 1
            o.done = ("sw%d" % self.sw_count, 16)
        elif dma:
            i = self.dma_rr
            self.dma_rr = (i + 1) % N_DMA_SEMS
            if self.dma_last[i] is not None:
                deps.append(self.dma_last[i])
            self.dma_uses[i] += 1
            o.done = ("dma%d" % i, 16 * self.dma_uses[i])
            self.dma_last[i] = o
        else:
            self.count[eng] += 1
            o.done = (eng, self.count[eng])
        for r in reads:
            buf, sub = (r[0], tuple(r[1:])) if isinstance(r, tuple) else (r, None)
            st = self.state[buf].setdefault(sub, [None, []])
            st[1].append(o)
        for w in writes:
            buf, sub = (w[0], tuple(w[1:])) if isinstance(w, tuple) else (w, None)
            d = self.state[buf]
            if sub is None:
                for k in list(d.keys()):
                    d[k] = [o, []]
                d[None] = [o, []]
            else:
                d[sub] = [o, []]
        seen = {}
        deps = deps + [p.clear for p in deps if getattr(p, "clear", None) is not None]
        for p in deps:
            s, v = p.done
            if eng == "pe" and (not dma) and (not p.dma) and p.eng == "pe":
                continue
            if v > seen.get(s, 0):
                seen[s] = v
        o.deps = list(seen.items())
        self.ops.append(o)
        return o

    def emit(self):
        nc = self.nc
        with contextlib.ExitStack() as es:
            sems = self.sems
            names = list(ENGS) + ["dma%d" % i for i in range(N_DMA_SEMS)]
            for o in self.ops:
                if o.done[0] not in names:
                    names.append(o.done[0])
            for nm in names:
                sems[nm] = es.enter_context(nc.semaphore("s_" + nm))
            block = es.enter_context(nc.Block())
            per = {e: [o for o in self.ops if o.eng == e] for e in ENGS}
            final = {}
            for o in self.ops:
                s, v = o.done
                final[s] = max(final.get(s, 0), v)

            def run(e, h):
                known = {}
                for o in per[e]:
                    for s, v in o.deps:
                        if v > known.get(s, 0):
                            h.wait_ge(sems[s], 16 if s in self.resetting else v)
                            known[s] = v
                    inst = o.fn(h)
                    s, v = o.done
                    inst.then_inc(sems[s], 16 if o.dma else 1)
                if e == "sp":
                    for s, v in final.items():
                        if s in self.resetting:
                            h.wait_ge(sems[s], 16)
                            continue
                        if v > known.get(s, 0):
                            h.wait_ge(sems[s], v)

            @block.tensor
            def _(h):
                run("pe", h)

            @block.vector
            def _(h):
                run("dve", h)

            @block.scalar
            def _(h):
                run("act", h)

            @block.gpsimd
            def _(h):
                run("pool", h)

            @block.sync
            def _(h):
                run("sp", h)


SB_BASE = 16512
SB_END = 229376


class Arena:
    def __init__(self, nc):
        self.nc = nc
        self.off = SB_BASE
        self.n = 0

    def alloc(self, name, shape, dt):
        esz = 2 if dt == BF16 else 4
        sz = int(np.prod(shape[1:])) * esz
        sz = (sz + 63) // 64 * 64
        assert self.off + sz <= SB_END, ("SBUF overflow", name, self.off, sz)
        self.n += 1
        t = self.nc.alloc_sbuf_tensor_at("%s_%d" % (name, self.n), list(shape), dt, offset=self.off)
        self.off += sz
        return t

    def mark(self):
        return self.off

    def release(self, m):
        self.peak = max(getattr(self, "peak", 0), self.off)
        if os.environ.get("MK_VERBOSE"):
            print("arena peak", self.peak - SB_BASE, "of", SB_END - SB_BASE)
        self.off = m


def build(nc, dbg=()):
    S = Sched(nc)
    A = Arena(nc)
    dbg = set(dbg)
    taps = {}

    def din(name, shape, dt=F32):
        return nc.dram_tensor(name, list(shape), dt, kind="ExternalInput").ap()

    def dout(name, shape, dt=F32):
        return nc.dram_tensor(name, list(shape), dt, kind="ExternalOutput").ap()

    x_d = din("x", [SEQ, D])
    c_d = din("c_fp", [128, 8])
    adaw_d = din("ada_w", [D, 6 * D])
    adab_d = din("ada_b", [1, 6 * D])
    g1_d = din("g1_fp", [128, 8])
    g2_d = din("g2_fp", [128, 8])
    g2b_d = din("g2_b", [128, D])
    win_d = din("w_in", [D, INW])
    out_d = dout("out", [SEQ, D])
    qkg_d = din("qkg", [128, 2])
    eb_d = din("eb", [4, 128, 7 * 2 * 2 * 128], BF16)
    up_d = din("up", [128, 2, 512])
    gup_d = din("g_up", [128, 512])
    lnx_d = din("lnx", [128, 2, 512])
    mu_d = din("mu", [128, 2, 14])
    w0a0_d = din("w0a0", [128, 2, 2, 4])
    kkr_d = din("kkr", [128, 3, 4])
    cpk_d = din("cpk", [128, 1282], BF16)
    wout_d = din("w_out", [D, D])
    wqry_d = din("w_query", [D, 2048])
    skT_d = din("skT", [128, 2, 128])
    iota_d = din("iota16", [128, 16])
    peeru_d = din("peer_u", [16384, D])
    peerv_d = din("peer_v", [16384, D])
    tab_d = nc.dram_tensor("uv_tab", [16384, 2 * D], BF16, kind="Internal").ap()
    bc_d = nc.dram_tensor("bc_scratch", [4, 128, D], F32, kind="Internal").ap()

    def mm(out, lhsT, rhs, start, stop, reads, writes):
        S.op("pe", lambda h: h.matmul(out, lhsT=lhsT, rhs=rhs, start=start, stop=stop), reads=reads, writes=writes)

    def tr(out, in_, ident, reads, writes):
        S.op("pe", lambda h: h.transpose(out=out, in_=in_, identity=ident), reads=reads, writes=writes)

    def act(out, in_, func, reads, writes, bias=0.0, scale=1.0, accum=None):
        if accum is None:
            S.op("act", lambda h: h.activation(out=out, in_=in_, func=func, bias=bias, scale=scale), reads=reads, writes=writes)
        else:
            S.op("act", lambda h: h.activation(out=out, in_=in_, func=func, bias=bias, scale=scale, accum_out=accum), reads=reads, writes=writes)

    def tt(eng, out, in0, in1, op, reads, writes):
        S.op(eng, lambda h: h.tensor_tensor(out=out, in0=in0, in1=in1, op=op), reads=reads, writes=writes)

    def ts(eng, out, in0, s1, s2, op0, op1, reads, writes):
        if s2 is None:
            S.op(eng, lambda h: h.tensor_scalar(out=out, in0=in0, scalar1=s1, scalar2=None, op0=op0), reads=reads, writes=writes)
        else:
            S.op(eng, lambda h: h.tensor_scalar(out=out, in0=in0, scalar1=s1, scalar2=s2, op0=op0, op1=op1), reads=reads, writes=writes)

    def stt(out, in0, scalar, in1, op0, op1, reads, writes, accum=None):
        if accum is None:
            S.op("dve", lambda h: h.scalar_tensor_tensor(out=out, in0=in0, scalar=scalar, in1=in1, op0=op0, op1=op1), reads=reads, writes=writes)
        else:
            S.op("dve", lambda h: h.scalar_tensor_tensor(out=out, in0=in0, scalar=scalar, in1=in1, op0=op0, op1=op1, accum_out=accum), reads=reads, writes=writes)

    def cp(eng, out, in_, reads, writes):
        if eng == "act":
            S.op("act", lambda h: h.copy(out=out, in_=in_), reads=reads, writes=writes)
        else:
            S.op(eng, lambda h: h.tensor_copy(out=out, in_=in_), reads=reads, writes=writes)

    def memset(eng, ap, val, writes):
        S.op(eng, lambda h: h.memset(ap, val), writes=writes)

    def dma(eng, out, in_, reads, writes):
        S.op(eng, lambda h: h.dma_start(out=out, in_=in_), reads=reads, writes=writes, dma=True)

    def tap(name, src_ap, shape, reads, dt=F32):
        if name in dbg:
            t = dout("dbg_" + name, shape, dt)
            taps[name] = t
            dma("sp", t, src_ap, reads, [])

    PSF = nc.alloc_psum_tensor("psf", [128, 6, 512], F32)
    psf = [PSF[:, i, :] for i in range(6)]
    psb = [nc.alloc_psum_tensor("psb%d" % i, [128, 1024], BF16) for i in range(2)]
    rr = {"f": 0, "b": 0}

    def pf():
        i = rr["f"]
        rr["f"] = (i + 1) % 6
        return psf[i], "psf%d" % i

    def pf2():
        i = ((rr["f"] + 1) // 2 * 2) % 6
        rr["f"] = (i + 2) % 6
        return PSF[:, i:i + 2, :], ["psf%d" % i, "psf%d" % (i + 1)]

    def pb():
        i = rr["b"]
        rr["b"] = (i + 1) % 2
        return psb[i], "psb%d" % i

    ident = A.alloc("ident", [128, 128], BF16)
    ones_f = A.alloc("ones_f", [128, 128], F32)
    memset("pool", ident[:], 0.0, ["ident"])
    S.op("pool", lambda h: h.affine_select(out=ident[:], in_=ident[:], pattern=[[-1, 128]], compare_op=ALU.not_equal,
                                           fill=1.0, base=0, channel_multiplier=1), reads=["ident"], writes=["ident"])
    memset("pool", ones_f[:], 1.0, ["ones_f"])

    for r0 in range(0, 16384, 2048):
        for hv, src in ((0, peeru_d), (1, peerv_d)):
            o_ = S.op("pool", (lambda o, i: (lambda h: h.dma_start(out=o, in_=i)))(tab_d[r0:r0 + 2048, hv * D:(hv + 1) * D], src[r0:r0 + 2048, :]),
                      reads=[], writes=[("tab", r0, hv)], dma=True)
            if o_ is not None:
                o_.nofence = True

    mod_fp = A.alloc("mod_fp", [128, 48], F32)
    g1_fp = A.alloc("g1_fp", [128, 8], F32)
    g2_fp = A.alloc("g2_fp", [128, 8], F32)
    gs1_fp = A.alloc("gs1_fp", [128, 8], F32)
    gs2_fp = A.alloc("gs2_fp", [128, 8], F32)
    m0 = A.mark()
    gate1_b = A.alloc("gate1_b", [128, D], F32)
    gs2_b = A.alloc("gs2_b", [128, D], F32)
    shift2_b = A.alloc("shift2_b", [128, D], F32)
    gate2_b = A.alloc("gate2_b", [128, D], F32)
    c_sb = A.alloc("c_sb", [128, 8], F32)
    sc_sb = A.alloc("sc_sb", [128, 8], F32)
    adab = A.alloc("adab", [1, 6 * D], F32)
    modrow = A.alloc("modrow", [1, 6 * D], F32)
    dma("sp", c_sb[:], c_d, [], ["c_sb"])
    dma("sp", adab[:], adab_d, [], ["adab"])
    dma("sp", g1_fp[:], g1_d, [], ["g1_fp"])
    dma("sp", g2_fp[:], g2_d, [], ["g2_fp"])
    dma("sp", gs2_b[:], g2b_d, [], ["gs2_b"])
    act(sc_sb[:], c_sb[:], AF.Silu, ["c_sb"], ["sc_sb"])
    wblk = [A.alloc("adaw%d" % i, [128, 8, 512], F32) for i in range(2)]
    adaw_v = adaw_d.rearrange("(j p) n -> p j n", p=128)
    for nb in range(12):
        wb = wblk[nb % 2]
        wn = "adaw%d" % (nb % 2)
        dma("sp", wb[:], adaw_v[:, :, nb * 512:(nb + 1) * 512], [], [wn])
        ps, pn = pf()
        for j in range(8):
            mm(ps[0:1, :], sc_sb[:, j:j + 1], wb[:, j, :], j == 0, j == 7, ["sc_sb", wn], [pn])
        tt("dve", modrow[0:1, nb * 512:(nb + 1) * 512], ps[0:1, :], adab[0:1, nb * 512:(nb + 1) * 512], ALU.add,
           [pn, "adab"], [("modrow", nb)])
    ps, pn = pf()
    for j in range(48):
        mm(ps[:, 2 * j:2 * j + 2], modrow[0:1, j * 128:(j + 1) * 128], ones_f[0:1, 0:2], True, True,
           ["modrow", "ones_f"], [pn])
    cp("dve", mod_fp[:], ps[:, 0:96].rearrange("p (j t) -> p j t", t=2)[:, :, 0], [pn], ["mod_fp"])
    stt(gs1_fp[:], mod_fp[:, 8:16], 1.0, g1_fp[:], ALU.add, ALU.mult, ["mod_fp", "g1_fp"], ["gs1_fp"])
    stt(gs2_fp[:], mod_fp[:, 32:40], 1.0, g2_fp[:], ALU.add, ALU.mult, ["mod_fp", "g2_fp"], ["gs2_fp"])
    for seg, dst, dn, kind in ((2, gate1_b, "gate1_b", 0), (4, gs2_b, "gs2_b", 1), (3, shift2_b, "shift2_b", 0),
                               (5, gate2_b, "gate2_b", 0)):
        for hb in range(2):
            ps, pn = pf()
            c0 = seg * D + hb * 512
            mm(ps[:, :], ones_f[0:1, 0:128], modrow[0:1, c0:c0 + 512], True, True, ["ones_f", "modrow"], [pn])
            dsl = dst[:, hb * 512:(hb + 1) * 512]
            if kind == 0:
                cp("act", dsl, ps[:, :], [pn], [(dn, hb)])
            else:
                stt(dsl, ps[:, :], 1.0, dsl, ALU.add, ALU.mult, [pn, (dn, hb)], [(dn, hb)])
    for i_, (bt__, bn__) in enumerate(((gate1_b, "gate1_b"), (gs2_b, "gs2_b"), (shift2_b, "shift2_b"), (gate2_b, "gate2_b"))):
        dma("sp", bc_d[i_], bt__[:], [bn__], ["bc_d"])
    tap("modrow", modrow[:], [1, 6 * D], ["modrow"])
    tap("gs2_b", gs2_b[:], [128, D], ["gs2_b"])
    tap("mod_fp", mod_fp[:], [128, 48], ["mod_fp"])
    S.fence()
    A.release(m0)

    m_rw = A.mark()
    rwT = A.alloc("rwT", [128, 4, SEQ], BF16)
    m_att = A.mark()
    attT = A.alloc("attT", [128, 8, SEQ], BF16)
    m_ht = A.mark()
    hT = A.alloc("hT", [128, 8, SEQ], BF16)
    m1 = A.mark()
    xt = [A.alloc("xt%d" % i, [128, D], F32) for i in range(2)]
    xn = [A.alloc("xn%d" % i, [128, D], BF16) for i in range(2)]
    junk = A.alloc("junk", [128, D], BF16)
    ss = A.alloc("ss", [128, NT], F32)
    rstd = A.alloc("rstd", [128, NT], F32)
    x_v = x_d.rearrange("(t p) d -> t p d", p=128)
    for t in range(NT):
        xb_, xnm = xt[t % 2], "xt%d" % (t % 2)
        nb_, nnm = xn[t % 2], "xn%d" % (t % 2)
        dma("sp", xb_[:], x_v[t], [], [xnm])
        act(junk[:], xb_[:], AF.Square, [xnm], ["junk", ("ss", t)], accum=ss[:, t:t + 1])
        ts("dve", rstd[:, t:t + 1], ss[:, t:t + 1], 1.0 / D, 1e-6, ALU.mult, ALU.add, [("ss", t)], [("rstd", t)])
        act(rstd[:, t:t + 1], rstd[:, t:t + 1], AF.Sqrt, [("rstd", t)], [("rstd", t)])
        S.op("dve", (lambda o: (lambda h: h.reciprocal(out=o, in_=o)))(rstd[:, t:t + 1]), reads=[("rstd", t)], writes=[("rstd", t)])
        act(nb_[:], xb_[:], AF.Copy, [xnm, ("rstd", t)], [nnm], scale=rstd[:, t:t + 1])
        ps, pn = pb()
        for j in range(8):
            tr(ps[:, j * 128:(j + 1) * 128], nb_[:, j * 128:(j + 1) * 128], ident[:], [nnm, "ident"], [pn])
        hsl = hT[:, :, t * 128:(t + 1) * 128]
        psv = ps[:, :].rearrange("p (j t) -> p j t", t=128)
        tt("dve", hsl, psv, gs1_fp[:, 0:8].unsqueeze(2).broadcast_to([128, 8, 128]), ALU.mult, [pn, "gs1_fp"], [("hT", t)])
        tt("pool", hsl, hsl, mod_fp[:, 0:8].unsqueeze(2).broadcast_to([128, 8, 128]), ALU.add, [("hT", t), "mod_fp"], [("hT", t)])
    tap("hT", hT[:], [128, 8, SEQ], ["hT"], BF16)
    S.fence()
    A.release(m1)


    blockones = A.alloc("blockones", [128, 128], BF16)
    memset("pool", blockones[:], 0.0, ["blockones"])
    memset("pool", blockones[0:64, 0:64], 1.0, ["blockones"])
    memset("pool", blockones[64:128, 64:128], 1.0, ["blockones"])
    m3 = A.mark()
    identf = A.alloc("identf", [128, 128], F32)
    memset("pool", identf[:], 0.0, ["identf"])
    S.op("pool", lambda h: h.affine_select(out=identf[:], in_=identf[:], pattern=[[-1, 128]], compare_op=ALU.not_equal,
                                           fill=1.0, base=0, channel_multiplier=1), reads=["identf"], writes=["identf"])
    top3 = A.mark()
    A.off = m_att
    lwin = A.alloc("lwin", [128, SEQ], BF16)
    sg = A.alloc("sg", [128, SEQ], BF16)
    rT = A.alloc("rT", [128, SEQ], F32)
    kTf = A.alloc("kTf", [128, SEQ], F32)
    kkT = A.alloc("kkT", [128, SEQ], F32)
    assert A.off <= m_ht
    A.off = top3
    up_sb = A.alloc("up_sb", [128, 2, 512], BF16)
    gup = A.alloc("gup", [128, 512], BF16)
    lnx = A.alloc("lnx", [128, 2, 512], F32)
    mu = A.alloc("mu", [128, 2, 14], F32)
    c0all = A.alloc("c0all", [128, 14], F32)
    w0a0 = A.alloc("w0a0", [128, 2, 2, 4], F32)
    kkr = A.alloc("kkr", [128, 3, 4], F32)
    cpk = A.alloc("cpk", [128, 1282], BF16)
    zraw = A.alloc("zraw", [128, SEQ + 2], F32)
    wch = [A.alloc("wch%d" % i, [128, 8, 128], BF16) for i in range(2)]
    vbf = A.alloc("vbf", [128, SEQ], BF16)
    asum = A.alloc("asum", [128, SEQ], F32)
    Vtm = A.alloc("Vtm", [128, NT, 128], BF16)
    ysum = A.alloc("ysum", [128, NT, 128], F32)
    rt_ = A.alloc("rt_", [128, SEQ // 2], BF16)
    bt_ = A.alloc("bt_", [128, SEQ // 2], BF16)
    khah = A.alloc("khah", [128, NT // 2, 2, 128], BF16)
    G4 = A.alloc("G4", [128, NT // 2, 2, 4, 128], BF16)
    pcs = A.alloc("pcs", [128, NT], F32)
    Mf = A.alloc("Mf", [128, 64], F32)
    Mbf = A.alloc("Mbf", [128, 64], BF16)
    Xsb = A.alloc("Xsb", [128, 2, 64], BF16)
    Usb = A.alloc("Usb", [128, 2, 64], BF16)
    sig_b = A.alloc("sig_b", [128, 512], F32)
    a_b = A.alloc("a_b", [128, 512], F32)
    cs_b = A.alloc("cs_b", [128, 512], F32)
    e1_b = A.alloc("e1_b", [128, 512], F32)
    tmc_b = A.alloc("tmc_b", [128, 512], F32)
    tme_b = A.alloc("tme_b", [128, 512], F32)
    kd_b = A.alloc("kd_b", [128, 512], F32)
    akk_b = A.alloc("akk_b", [128, 512], F32)
    ex_b = [A.alloc("ex_b%d" % i, [128, 512], F32) for i in range(2)]
    kt_b = A.alloc("kt_b", [128, 512], BF16)
    at_b = A.alloc("at_b", [128, 512], BF16)
    khT_b = A.alloc("khT_b", [128, 512], BF16)
    ahT_b = A.alloc("ahT_b", [128, 512], BF16)
    sq_b = A.alloc("sq_b", [128, 512], BF16)
    rs_b = A.alloc("rs_b", [128, 512], F32)
    Awk = [A.alloc("Awk%d" % i, [128, 8, 128], F32) for i in range(2)]
    Bwk = [A.alloc("Bwk%d" % i, [128, 8, 128], F32) for i in range(2)]
    Sf = A.alloc("Sf", [128, 8, 128], F32)
    gn1 = A.alloc("gn1", [128, 32], F32)
    gn2 = A.alloc("gn2", [128, 32], F32)
    coef = A.alloc("coef", [128, 32], F32)
    rwo = A.alloc("rwo", [128, NT, 128], BF16)

    dma("pool", up_sb[:], up_d, [], ["up_sb"])
    dma("pool", gup[:], gup_d, [], ["gup"])
    dma("sp", lnx[:], lnx_d, [], ["lnx"])
    dma("sp", mu[:], mu_d, [], ["mu"])
    dma("sp", w0a0[:], w0a0_d, [], ["w0a0"])
    dma("sp", kkr[:], kkr_d, [], ["kkr"])
    dma("sp", cpk[:], cpk_d, [], ["cpk"])
    msk4 = lambda dr: cpk[:, dr * 640:dr * 640 + 512]
    mskA = lambda dr: cpk[:, dr * 640 + 512:dr * 640 + 640]
    hsel = cpk[:, 1280:1282]
    tt("dve", c0all[:], mu[:, 0, :], mu[:, 1, :], ALU.add, ["mu"], ["c0all"])
    ts("dve", c0all[:], c0all[:], -1.0, 1.0, ALU.mult, ALU.add, ["c0all"], ["c0all"])
    memset("pool", zraw[:, 0:1], 0.0, ["zraw"])
    memset("pool", zraw[:, SEQ + 1:SEQ + 2], 0.0, ["zraw"])
    win_v3 = win_d.rearrange("(j p) n -> p j n", p=128)
    wcc = [0]

    def zr_chunk(c, dst, dn):
        wi = wcc[0] % 2
        wcc[0] += 1
        col = 1536 + 128 * c
        dma("pool", wch[wi][:], win_v3[:, :, col:col + 128], [], ["wch%d" % wi])
        for blk in range(4):
            ps, pn = pf()
            for j in range(8):
                mm(ps[:, :], wch[wi][:, j, :], hT[:, j, blk * 512:(blk + 1) * 512], j == 0, j == 7, ["wch%d" % wi, "hT"], [pn])
            cp("act", zraw[:, 1 + blk * 512:1 + (blk + 1) * 512], ps[:, :], [pn], [("zraw", blk)])
        act(dst[:], zraw[:, 1:SEQ + 1], AF.Copy, ["zraw", "c0all"], [dn], scale=c0all[:, c:c + 1])
        stt(dst[:], zraw[:, 0:SEQ], mu[:, 0, c:c + 1], dst[:], ALU.mult, ALU.add, ["zraw", "mu", dn], [dn])
        stt(dst[:], zraw[:, 2:SEQ + 2], mu[:, 1, c:c + 1], dst[:], ALU.mult, ALU.add, ["zraw", "mu", dn], [dn])

    zr_chunk(12, rT, "rT")
    act(lwin[0:64, :], rT[0:64, :], AF.Tanh, ["rT"], [("lwin", 0)])
    cp("dve", lwin[64:128, :], rT[64:128, :], ["rT"], [("lwin", 1)])
    zr_chunk(13, kTf, "kTf")
    act(sg[:], kTf[:], AF.Sigmoid, ["kTf"], ["sg"])

    tap("lwin", lwin[:], [128, SEQ], ["lwin"], BF16)
    tap("sg", sg[:], [128, SEQ], ["sg"], BF16)
    for hp in range(4):
        zr_chunk(hp, rT, "rT")
        if hp == 0:
            tap("zr0", rT[:], [128, SEQ], ["rT"])
        zr_chunk(4 + hp, kTf, "kTf")
        zr_chunk(8 + hp, kkT, "kkT")
        cp("act", vbf[:], kkT[:], ["kkT"], ["vbf"])
        for half in range(2):
            ps, pn = pb()
            for ti in range(8):
                t = half * 8 + ti
                tr(ps[:, ti * 128:(ti + 1) * 128], vbf[:, t * 128:(t + 1) * 128], ident[:], ["vbf", "ident"], [pn])
            cp("dve", Vtm[:, half * 8:(half + 1) * 8, :], ps[:, :].rearrange("p (t f) -> p t f", f=128), [pn], ["Vtm"])
        ts("dve", kkT[:], kTf[:], kkr[:, 0, hp:hp + 1], None, ALU.mult, None, ["kTf", "kkr", "vbf"], ["kkT"])
        for blk in range(4):
            tsl = slice(blk * 512, (blk + 1) * 512)
            act(sq_b[:], kkT[:, tsl], AF.Square, ["kkT"], ["sq_b"])
            ps, pn = pf()
            mm(ps[:, :], blockones[:], sq_b[:], True, True, ["blockones", "sq_b"], [pn])
            act(rs_b[:], ps[:, :], AF.Sqrt, [pn], ["rs_b"])
            ts("dve", rs_b[:], rs_b[:], 1e-6, None, ALU.max, None, ["rs_b"], ["rs_b"])
            S.op("dve", (lambda o: (lambda h: h.reciprocal(out=o, in_=o)))(rs_b[:]), reads=["rs_b"], writes=["rs_b"])
            tt("dve", kkT[:, tsl], kkT[:, tsl], rs_b[:], ALU.mult, ["kkT", "rs_b"], ["kkT"])
        if hp == 0:
            tap("kk0", kkT[:], [128, SEQ], ["kkT"])
            tap("vtm0", Vtm[:], [128, NT, 128], ["Vtm"], BF16)
        for dr in range(2):
            cdec = CDEC
            memset("pool", Mf[:], 0.0, ["Mf"])
            memset("pool", Mbf[:], 0.0, ["Mbf"])
            for hf in ((0, 1) if dr == 0 else (1, 0)):
                for blk in (2 * hf, 2 * hf + 1):
                    lb = blk % 2
                    hsl_ = slice(lb * 512, (lb + 1) * 512)
                    tsl = slice(blk * 512, (blk + 1) * 512)
                    ps2, pns = pf2()
                    mm(ps2[:, 0, :], up_sb[0:64, dr, hp * 128:(hp + 1) * 128], lwin[0:64, tsl], True, True, ["up_sb", "lwin"], [pns[0]])
                    mm(ps2[:, 1, :], up_sb[64:128, dr, hp * 128:(hp + 1) * 128], lwin[64:128, tsl], True, True, ["up_sb", "lwin"], [pns[1]])
                    act(sig_b[:], ps2[:, 0, :], AF.Sigmoid, [pns[0], "w0a0"], ["sig_b"], bias=w0a0[:, 0, dr, hp:hp + 1])
                    act(a_b[:], ps2[:, 1, :], AF.Sigmoid, [pns[1], "w0a0"], ["a_b"], bias=w0a0[:, 1, dr, hp:hp + 1])
                    if hp == 0 and blk == 0:
                        tap("sig%d" % dr, sig_b[:], [128, 512], ["sig_b"])
                        tap("a%d" % dr, a_b[:], [128, 512], ["a_b"])
                    if dr == 0:
                        cp("pool", asum[:, tsl], a_b[:], ["a_b"], [("asum", blk)])
                    else:
                        tt("pool", asum[:, tsl], asum[:, tsl], a_b[:], ALU.add, ["a_b", ("asum", blk)], [("asum", blk)])
                    for ti in range(4):
                        S.op("dve", (lambda o, d1: (lambda h: h.tensor_tensor_scan(out=o, data0=ones_f[:, 0:128], data1=d1, initial=0.0,
                                                                                  op0=ALU.mult, op1=ALU.add)))(
                            cs_b[:, ti * 128:(ti + 1) * 128], sig_b[:, ti * 128:(ti + 1) * 128]),
                            reads=["sig_b", "ones_f"], writes=[("cs_b", ti)])
                    tt("dve", e1_b[:], cs_b[:], sig_b[:], ALU.subtract, ["cs_b", "sig_b"], ["e1_b"])
                    csv = cs_b[:].rearrange("p (c t) -> p c t", t=128)
                    totb = csv[:, :, 127:128].broadcast_to([128, 4, 128])
                    tt("dve", tmc_b[:].rearrange("p (c t) -> p c t", t=128), totb, csv, ALU.subtract, ["cs_b"], ["tmc_b"])
                    if dr == 1:
                        tt("dve", tme_b[:].rearrange("p (c t) -> p c t", t=128), totb, e1_b[:].rearrange("p (c t) -> p c t", t=128),
                           ALU.subtract, ["cs_b", "e1_b"], ["tme_b"])
                        pin, pinn, pex, pexn, prem, premn = tme_b, "tme_b", tmc_b, "tmc_b", e1_b, "e1_b"
                    else:
                        pin, pinn, pex, pexn, prem, premn = cs_b, "cs_b", e1_b, "e1_b", tmc_b, "tmc_b"
                    act(pcs[:, blk * 4:(blk + 1) * 4], csv[:, :, 127], AF.Exp, ["cs_b"], [("pcs", blk)], scale=-cdec)
                    ts("dve", kd_b[:], a_b[:], -1.0, kkr[:, 1, hp:hp + 1], ALU.add, ALU.mult, ["a_b", "kkr"], ["kd_b"])
                    stt(kd_b[:], kd_b[:], 1.0, kTf[:, tsl], ALU.add, ALU.mult, ["kd_b", "kTf"], ["kd_b"])
                    tt("pool", akk_b[:], a_b[:], kkT[:, tsl], ALU.mult, ["a_b", "kkT"], ["akk_b"])
                    act(ex_b[0][:], pin[:], AF.Exp, [pinn], ["ex_b0"], scale=-cdec)
                    tt("pool", rt_[:, hsl_], rT[:, tsl], ex_b[0][:], ALU.mult, ["rT", "ex_b0"], [("rt_", lb)])
                    act(ex_b[1][:], pex[:], AF.Exp, [pexn], ["ex_b1"], scale=-cdec)
                    tt("pool", bt_[:, hsl_], kkT[:, tsl], ex_b[1][:], ALU.mult, ["kkT", "ex_b1"], [("bt_", lb)])
                    act(ex_b[0][:], pin[:], AF.Exp, [pinn], ["ex_b0"], scale=cdec)
                    tt("pool", kt_b[:], kd_b[:], ex_b[0][:], ALU.mult, ["kd_b", "ex_b0"], ["kt_b"])
                    stt(at_b[:], akk_b[:], -1.0, ex_b[0][:], ALU.mult, ALU.mult, ["akk_b", "ex_b0"], ["at_b"])
                    act(ex_b[1][:], prem[:], AF.Exp, [premn], ["ex_b1"], scale=-cdec)
                    tt("pool", khT_b[:], kd_b[:], ex_b[1][:], ALU.mult, ["kd_b", "ex_b1"], ["khT_b"])
                    stt(ahT_b[:], akk_b[:], -1.0, ex_b[1][:], ALU.mult, ALU.mult, ["akk_b", "ex_b1"], ["ahT_b"])
                    ps, pn = pb()
                    for ti in range(4):
                        tr(ps[:, (ti * 2) * 128:(ti * 2 + 1) * 128], khT_b[:, ti * 128:(ti + 1) * 128], ident[:], ["khT_b", "ident"], [pn])
                        tr(ps[:, (ti * 2 + 1) * 128:(ti * 2 + 2) * 128], ahT_b[:, ti * 128:(ti + 1) * 128], ident[:], ["ahT_b", "ident"], [pn])
                    cp("act", khah[:, lb * 4:(lb + 1) * 4, :, :], ps[:, :].rearrange("p (t q f) -> p t q f", t=4, q=2), [pn], [("khah", lb)])
                    for ti in range(4):
                        t = lb * 4 + ti
                        fsl = slice(t * 128, (t + 1) * 128)
                        lsl = slice(ti * 128, (ti + 1) * 128)
                        ps2, pns = pf2()
                        for hh in range(2):
                            hs = slice(hh * 64, (hh + 1) * 64)
                            mm(ps2[:, hh, 0:128], at_b[hs, lsl], bt_[hs, fsl], True, True, ["at_b", ("bt_", lb)], [pns[hh]])
                            mm(ps2[:, hh, 128:256], kt_b[hs, lsl], bt_[hs, fsl], True, True, ["kt_b", ("bt_", lb)], [pns[hh]])
                            mm(ps2[:, hh, 256:384], at_b[hs, lsl], rt_[hs, fsl], True, True, ["at_b", ("rt_", lb)], [pns[hh]])
                            mm(ps2[:, hh, 384:512], kt_b[hs, lsl], rt_[hs, fsl], True, True, ["kt_b", ("rt_", lb)], [pns[hh]])
                        tt("dve", G4[:, t, :, :, :].rearrange("p h f q -> p h (f q)"), ps2[:, :, :],
                           msk4(dr).unsqueeze(1).broadcast_to([128, 2, 512]), ALU.mult, pns + ["cpk"], [("G4", t)])
                        ps3, pns3 = pf2()
                        for hh in range(2):
                            hs = slice(hh * 64, (hh + 1) * 64)
                            mm(ps3[:, hh, 0:128], bt_[hs, fsl], at_b[hs, lsl], True, True, ["at_b", ("bt_", lb)], [pns3[hh]])
                        tt("dve", Awk[0][:, ti * 2:ti * 2 + 2, :], ps3[:, :, 0:128], mskA(dr).unsqueeze(1).broadcast_to([128, 2, 128]),
                           ALU.mult, pns3 + ["cpk"], ["Awk0"])
                        tt("dve", Bwk[0][:, ti * 2:ti * 2 + 2, :], ps2[:, :, 0:128], msk4(dr)[:, 0:128].unsqueeze(1).broadcast_to([128, 2, 128]),
                           ALU.mult, pns + ["cpk"], ["Bwk0"])
                    tt("pool", Sf[:], identf[:].unsqueeze(1).broadcast_to([128, 8, 128]), Bwk[0][:], ALU.add, ["identf", "Bwk0"], ["Sf"])
                    cur = 0
                    for lev in range(1, 7):
                        nxt = 1 - cur
                        an_c, bn_c, an_n, bn_n = "Awk%d" % cur, "Bwk%d" % cur, "Awk%d" % nxt, "Bwk%d" % nxt
                        for hb in range(2):
                            psA, pnA = pf()
                            for ii in range(4):
                                i_ = hb * 4 + ii
                                mm(psA[:, ii * 128:(ii + 1) * 128], Bwk[cur][:, i_, :], Awk[cur][:, i_, :], True, True, [an_c, bn_c], [pnA])
                            cp("act", Awk[nxt][:, hb * 4:(hb + 1) * 4, :], psA[:, :].rearrange("p (i f) -> p i f", f=128), [pnA], [(an_n, hb)])
                            if lev < 6:
                                psB, pnB = pf()
                                for ii in range(4):
                                    i_ = hb * 4 + ii
                                    mm(psB[:, ii * 128:(ii + 1) * 128], Awk[cur][:, i_, :], Bwk[cur][:, i_, :], True, True, [an_c, bn_c], [pnB])
                                cp("dve", Bwk[nxt][:, hb * 4:(hb + 1) * 4, :], psB[:, :].rearrange("p (i f) -> p i f", f=128), [pnB], [(bn_n, hb)])
                            psS, pnS = pf()
                            for ii in range(4):
                                i_ = hb * 4 + ii
                                mm(psS[:, ii * 128:(ii + 1) * 128], Awk[nxt][:, i_, :], Sf[:, i_, :], True, True, [(an_n, hb), ("Sf", hb)], [pnS])
                            if lev < 6:
                                tt("dve", Sf[:, hb * 4:(hb + 1) * 4, :], Sf[:, hb * 4:(hb + 1) * 4, :],
                                   psS[:, :].rearrange("p (i f) -> p i f", f=128), ALU.add, [pnS, ("Sf", hb)], [("Sf", hb)])
                            else:
                                t0 = lb * 4 + hb * 2
                                tt("dve", G4[:, t0:t0 + 2, :, 0, :], Sf[:, hb * 4:(hb + 1) * 4, :].rearrange("p (t h) f -> p t h f", h=2),
                                   psS[:, :].rearrange("p (t h f) -> p t h f", t=2, h=2), ALU.add, [pnS, ("Sf", hb)],
                                   [("G4", t0), ("G4", t0 + 1)])
                        cur = nxt
                if hp == 0 and dr == 0 and hf == 0:
                    tap("rt0", rt_[:], [128, SEQ // 2], ["rt_"], BF16)
                    tap("bt0", bt_[:], [128, SEQ // 2], ["bt_"], BF16)
                    tap("G40", G4[:], [128, NT // 2, 2, 4, 128], ["G4"], BF16)
                    tap("khah0", khah[:], [128, NT // 2, 2, 128], ["khah"], BF16)
                    tap("pcs0", pcs[:], [128, NT], ["pcs"])
                order = list(range(8 * hf, 8 * hf + 8)) if dr == 0 else list(range(8 * hf + 7, 8 * hf - 1, -1))
                for t in order:
                    tl = t - 8 * hf
                    fsl = slice(tl * 128, (tl + 1) * 128)
                    blk = t // 4
                    lb = blk % 2
                    psX, pnsX = pf2()
                    for hh in range(2):
                        hs = slice(hh * 64, (hh + 1) * 64)
                        mm(psX[:, hh, 0:64], bt_[hs, fsl], Mbf[hs, :], True, False, [("bt_", lb), "Mbf"], [pnsX[hh]])
                        mm(psX[:, hh, 0:64], G4[:, tl, hh, 1, :], Vtm[:, t, hs], False, True, [("G4", tl), "Vtm"], [pnsX[hh]])
                    cp("act", Xsb[:], psX[:, :, 0:64], pnsX, ["Xsb"])
                    psU, pnU = pf()
                    for hh in range(2):
                        mm(psU[:, hh * 64:(hh + 1) * 64], G4[:, tl, hh, 0, :], Xsb[:, hh, :], True, True, [("G4", tl), "Xsb"], [pnU])
                    cp("dve", Usb[:], psU[:, 0:128].rearrange("p (h v) -> p h v", h=2), [pnU], ["Usb"])
                    psY, pnsY = pf2()
                    for hh in range(2):
                        hs = slice(hh * 64, (hh + 1) * 64)
                        mm(psY[:, hh, 0:64], rt_[hs, fsl], Mbf[hs, :], True, False, [("rt_", lb), "Mbf"], [pnsY[hh]])
                        mm(psY[:, hh, 0:64], G4[:, tl, hh, 2, :], Usb[:, hh, :], False, False, [("G4", tl), "Usb"], [pnsY[hh]])
                        mm(psY[:, hh, 0:64], G4[:, tl, hh, 3, :], Vtm[:, t, hs], False, True, [("G4", tl), "Vtm"], [pnsY[hh]])
                    yv = ysum[:, t, :].rearrange("p (h v) -> p h v", h=2)
                    if dr == 0:
                        cp("act", yv, psY[:, :, 0:64], pnsY, [("ysum", t)])
                    else:
                        tt("dve", yv, yv, psY[:, :, 0:64], ALU.add, pnsY + [("ysum", t)], [("ysum", t)])
                    psM, pnM = pf()
                    for hh in range(2):
                        hs = slice(hh * 64, (hh + 1) * 64)
                        mm(psM[:, hs], khah[:, tl, 1, :], Usb[:, hh, :], True, False, [("khah", lb), "Usb"], [pnM])
                        mm(psM[:, hs], khah[:, tl, 0, :], Vtm[:, t, hs], False, True, [("khah", lb), "Vtm"], [pnM])
                    for hh in range(2):
                        hs = slice(hh * 64, (hh + 1) * 64)
                        stt(Mf[hs, :], Mf[hs, :], pcs[hs, t:t + 1], psM[hs, hs], ALU.mult, ALU.add, [("Mf", hh), ("pcs", blk), pnM], [("Mf", hh)])
                    cp("act", Mbf[:], Mf[:], ["Mf"], ["Mbf"])
            if hp == 0:
                tap("ys%d" % dr, ysum[:], [128, NT, 128], ["ysum"])
        yv3 = ysum[:].rearrange("p t (h v) -> p (t h) v", h=2)
        S.op("dve", lambda h: h.tensor_reduce(out=gn1[:], in_=yv3, axis=AX.X, op=ALU.add), reads=["ysum"], writes=["gn1"])
        ts("dve", gn1[:], gn1[:], 1.0 / 64, None, ALU.mult, None, ["gn1"], ["gn1"])
        tt("dve", yv3, yv3, gn1[:].unsqueeze(2).broadcast_to([128, 32, 64]), ALU.subtract, ["ysum", "gn1"], ["ysum"])
        kd4 = kkT[:].rearrange("p (t v) -> p t v", v=64)
        tt("pool", kd4, yv3, yv3, ALU.mult, ["ysum"], ["kkT"])
        S.op("dve", lambda h: h.tensor_reduce(out=gn2[:], in_=kd4, axis=AX.X, op=ALU.add), reads=["kkT"], writes=["gn2"])
        ts("dve", gn2[:], gn2[:], 1.0 / 64, 64e-5, ALU.mult, ALU.add, ["gn2"], ["gn2"])
        act(gn2[:], gn2[:], AF.Sqrt, ["gn2"], ["gn2"])
        S.op("dve", (lambda o: (lambda h: h.reciprocal(out=o, in_=o)))(gn2[:]), reads=["gn2"], writes=["gn2"])
        tt("dve", yv3, yv3, gn2[:].unsqueeze(2).broadcast_to([128, 32, 64]), ALU.mult, ["ysum", "gn2"], ["ysum"])
        y3 = ysum[:]
        tt("dve", y3, y3, lnx[:, 0, hp * 128:(hp + 1) * 128].unsqueeze(1).broadcast_to([128, NT, 128]), ALU.mult, ["ysum", "lnx"], ["ysum"])
        tt("pool", y3, y3, lnx[:, 1, hp * 128:(hp + 1) * 128].unsqueeze(1).broadcast_to([128, NT, 128]), ALU.add, ["ysum", "lnx"], ["ysum"])
        ts("dve", asum[:], asum[:], 0.5, -1.0, ALU.mult, ALU.add, ["asum"], ["asum"])
        ts("dve", asum[:], asum[:], kkr[:, 1, hp:hp + 1], 1.0, ALU.mult, ALU.add, ["asum", "kkr"], ["asum"])
        tt("pool", asum[:], asum[:], kTf[:], ALU.mult, ["asum", "kTf"], ["asum"])
        tt("pool", asum[:], asum[:], rT[:], ALU.mult, ["asum", "rT"], ["asum"])
        ts("dve", vbf[:], asum[:], kkr[:, 2, hp:hp + 1], None, ALU.mult, None, ["asum", "kkr", "Vtm"], ["vbf"])
        ps, pn = pf()
        for t in range(NT):
            mm(ps[:, t * 2:t * 2 + 2], vbf[:, t * 128:(t + 1) * 128], hsel, True, True, ["vbf", "cpk"], [pn])
        cp("dve", coef[:], ps[:, 0:32], [pn], ["coef"])
        tt("dve", kd4, Vtm[:].rearrange("p t (h v) -> p (t h) v", h=2), coef[:].unsqueeze(2).broadcast_to([128, 32, 64]), ALU.mult,
           ["Vtm", "coef", "kkT"], ["kkT"])
        tt("pool", yv3, yv3, kd4, ALU.add, ["ysum", "kkT"], ["ysum"])
        for g4 in range(4):
            ps, pn = pf()
            for ti in range(4):
                t = g4 * 4 + ti
                mm(ps[:, ti * 128:(ti + 1) * 128], sg[:, t * 128:(t + 1) * 128], gup[:, hp * 128:(hp + 1) * 128], True, True, ["sg", "gup"], [pn])
            tt("dve", rwo[:, g4 * 4:(g4 + 1) * 4, :], ysum[:, g4 * 4:(g4 + 1) * 4, :], ps[:, :].rearrange("p (t f) -> p t f", f=128), ALU.mult,
               ["ysum", pn], [("rwo", g4)])
        for half in range(2):
            ps, pn = pb()
            for ti in range(8):
                t = half * 8 + ti
                tr(ps[:, ti * 128:(ti + 1) * 128], rwo[:, t, :], ident[:], ["rwo", "ident"], [pn])
            cp("act", rwT[:, hp, half * 1024:(half + 1) * 1024], ps[:, :], [pn], [("rwT", hp, half)])
    tap("rwT", rwT[:], [128, 4, SEQ], ["rwT"], BF16)
    S.fence()
    A.release(m3)

    qkg = A.alloc("qkg", [128, 2], F32)
    dma("sp", qkg[:], qkg_d, [], ["qkg"])
    ts("dve", qkg[:, 0:1], qkg[:, 0:1], 0.125, None, ALU.mult, None, ["qkg"], ["qkg"])
    NVT = 17 + 20 + 16
    m2 = A.mark()
    Vaug = A.alloc("Vaug", [128, NVT, 2, 65], BF16)
    memset("pool", Vaug[:, :, :, 64:65], 1.0, ["Vaug"])
    qT = A.alloc("qT", [128, SEQ], BF16)
    kT = A.alloc("kT", [128, SEQ], BF16)
    wq = [A.alloc("wqkv%d" % i, [128, 8, 128], BF16) for i in range(3)]
    acc = A.alloc("acc", [128, 2, SEQ], F32)
    EB = A.alloc("EB", [128, 7, 2, 2, 128], BF16)
    sqb = [A.alloc("sqb%d" % i, [128, 512], BF16) for i in range(2)]
    rsb = [A.alloc("rsb%d" % i, [128, 512], F32) for i in range(2)]
    Eb = [A.alloc("Eb%d" % i, [128, 2, 2, 128], BF16) for i in range(3)]
    PTb = [A.alloc("PTb%d" % i, [128, 2, 2, 128], BF16) for i in range(3)]
    win_v = win_d.rearrange("(j p) n -> p j n", p=128)
    PATS = ((1, 0), (4, 17), (16, 37))
    blkc = [0]
    for hp in range(4):
        dma("sp", EB[:], eb_d[hp].rearrange("p (s h k q) -> p s h k q", s=7, h=2, k=2), [], ["EB"])
        for i, c0 in enumerate((hp * 128, 512 + hp * 128, 1024 + hp * 128)):
            dma("pool", wq[i][:], win_v[:, :, c0:c0 + 128], [], ["wqkv%d" % i])
        for i, (dst, dn) in enumerate(((qT, "qT"), (kT, "kT"))):
            for blk in range(4):
                tsl = slice(blk * 512, (blk + 1) * 512)
                ps, pn = pf()
                for j in range(8):
                    mm(ps[:, :], wq[i][:, j, :], hT[:, j, tsl], j == 0, j == 7, ["wqkv%d" % i, "hT"], [pn])
                bi = blkc[0] % 2
                blkc[0] += 1
                act(sqb[bi][:], ps[:, :], AF.Square, [pn], ["sqb%d" % bi])
                ps2, pn2 = pf()
                mm(ps2[:, :], blockones[:], sqb[bi][:], True, True, ["blockones", "sqb%d" % bi], [pn2])
                act(rsb[bi][:], ps2[:, :], AF.Sqrt, [pn2], ["rsb%d" % bi], bias=1e-6, scale=1.0 / 64)
                S.op("dve", (lambda o: (lambda h: h.reciprocal(out=o, in_=o)))(rsb[bi][:]), reads=["rsb%d" % bi], writes=["rsb%d" % bi])
                stt(dst[:, tsl], ps[:, :], qkg[:, i:i + 1], rsb[bi][:], ALU.mult, ALU.mult, [pn, "qkg", "rsb%d" % bi], [(dn, blk)])
        vtl = []
        for (dil, base) in PATS:
            L = SEQ // dil
            nb = L // 128
            for r in range(dil):
                if nb == 1:
                    vtl.append((base + r, r, dil))
                else:
                    for m in range(nb + 1):
                        l0 = 0 if m == 0 else (L - 128 if m == nb else 128 * m - 64)
                        vtl.append((base + r * (nb + 1) + m, r + dil * l0, dil))
        assert len(vtl) == NVT and [v[0] for v in vtl] == list(range(NVT))
        for g0 in range(0, NVT, 4):
            grp = vtl[g0:g0 + 4]
            ps, pn = pf()
            for gi, (vt, st, dil) in enumerate(grp):
                for j in range(8):
                    mm(ps[:, gi * 128:(gi + 1) * 128], hT[:, j, st:st + 127 * dil + 1:dil], wq[2][:, j, :], j == 0, j == 7,
                       ["hT", "wqkv2"], [pn])
            n = len(grp)
            cp("act", Vaug[:, g0:g0 + n, :, 0:64], ps[:, 0:n * 128].rearrange("p (t h e) -> p t h e", t=n, h=2),
               [pn], [("Vaug", g0 // 4)])
        for pi, (dil, base) in enumerate(PATS):
            L = SEQ // dil
            nb = L // 128
            for r in range(dil):
                for qb in range(nb):
                    qs = r + dil * 128 * qb
                    qsl = slice(qs, qs + 127 * dil + 1, dil)
                    if nb == 1:
                        kts = [(base + r, r)]
                        eset = 6
                    else:
                        mA, mB = qb, qb + 1
                        lA = 0 if qb == 0 else 128 * qb - 64
                        lB = 128 * qb + 64 if qb < nb - 1 else L - 128
                        kts = [(base + r * (nb + 1) + mA, r + dil * lA), (base + r * (nb + 1) + mB, r + dil * lB)]
                        eset = pi * 3 + (0 if qb == 0 else (2 if qb == nb - 1 else 1))
                    nk = len(kts)
                    ps, pns = pf2()
                    for hh in range(2):
                        for kt, (vt, ks) in enumerate(kts):
                            mm(ps[:, hh, kt * 128:(kt + 1) * 128], kT[hh * 64:(hh + 1) * 64, ks:ks + 127 * dil + 1:dil],
                               qT[hh * 64:(hh + 1) * 64, qsl], True, True, ["kT", "qT"], [pns[hh]])
                    bi = blkc[0] % 3
                    blkc[0] += 1
                    psv = ps[:, :, 0:256].rearrange("p h (k q) -> p h k q", k=2)[:, :, 0:nk, :]
                    act(Eb[bi][:, :, 0:nk, :], psv, AF.Exp, pns, ["Eb%d" % bi])
                    tt("pool", PTb[bi][:, :, 0:nk, :], Eb[bi][:, :, 0:nk, :], EB[:, eset, :, 0:nk, :], ALU.mult,
                       ["Eb%d" % bi, "EB"], ["PTb%d" % bi])
                    ps2, pn2 = pf()
                    for hh in range(2):
                        for kt, (vt, ks) in enumerate(kts):
                            mm(ps2[0:65, hh * 128:(hh + 1) * 128], Vaug[:, vt, hh, :], PTb[bi][:, hh, kt, :], kt == 0,
                               kt == nk - 1, [("Vaug", vt // 4), "PTb%d" % bi], [pn2])
                    asl = acc[0:65, :, qsl]
                    p2v = ps2[0:65, 0:256].rearrange("p (h q) -> p h q", h=2)
                    if pi == 0:
                        cp("dve", asl, p2v, [pn2], [("acc", pi, r, qb)])
                    else:
                        tt("dve", asl, p2v, asl, ALU.add, [pn2, "acc"], ["acc"])
        for hh in range(2):
            for blk in range(4):
                tsl = slice(blk * 512, (blk + 1) * 512)
                ps, pn = pf()
                mm(ps[0:64, :], ones_f[64:65, 0:64], acc[64:65, hh, tsl], True, True, ["ones_f", "acc"], [pn])
                bi = blkc[0] % 2
                blkc[0] += 1
                S.op("dve", (lambda o, i_: (lambda h: h.reciprocal(out=o, in_=i_)))(rsb[bi][0:64, :], ps[0:64, :]),
                     reads=[pn], writes=["rsb%d" % bi])
                tt("dve", attT[0:64, 2 * hp + hh, tsl], acc[0:64, hh, tsl], rsb[bi][0:64, :], ALU.mult,
                   ["acc", "rsb%d" % bi], [("attT", hp, hh, blk)])
    tap("attT", attT[0:64], [64, 8, SEQ], ["attT"], BF16)
    S.fence()
    A.release(m2)

    A.release(m_ht)
    x1all = A.alloc("x1all", [128, NT, D], F32)
    bc3 = A.alloc("bc3", [128, 3, D], F32)
    dma("sp", bc3[:, 0, :], bc_d[1], ["bc_d"], [("bc3", 0)])
    dma("sp", bc3[:, 1, :], bc_d[2], ["bc_d"], [("bc3", 1)])
    dma("sp", bc3[:, 2, :], bc_d[3], ["bc_d"], [("bc3", 2)])
    m4 = A.mark()
    woutA = A.alloc("woutA", [64, 8, D], BF16)
    woutR = A.alloc("woutR", [128, 4, D], BF16)
    g1b = A.alloc("g1b", [128, D], F32)
    xts = [A.alloc("xts%d" % i, [128, D], F32) for i in range(2)]
    dma("sp", g1b[:], bc_d[0], ["bc_d"], ["g1b"])
    dma("pool", woutA[:], wout_d[0:512, :].rearrange("(h p) n -> p h n", p=64), [], ["woutA"])
    dma("pool", woutR[:], wout_d[512:1024, :].rearrange("(c p) n -> p c n", p=128), [], ["woutR"])
    tt("pool", woutA[:], woutA[:], g1b[0:64, :].unsqueeze(1).broadcast_to([64, 8, D]), ALU.mult, ["woutA", "g1b"], ["woutA"])
    tt("pool", woutR[:], woutR[:], g1b[:].unsqueeze(1).broadcast_to([128, 4, D]), ALU.mult, ["woutR", "g1b"], ["woutR"])
    for t in range(NT):
        fsl = slice(t * 128, (t + 1) * 128)
        xb_, xnm = xts[t % 2], "xts%d" % (t % 2)
        dma("sp", xb_[:], x_v[t], [], [xnm])
        for nb in range(2):
            ps, pn = pf()
            csl = slice(nb * 512, (nb + 1) * 512)
            for h_ in range(8):
                mm(ps[:, :], attT[0:64, h_, fsl], woutA[0:64, h_, csl], h_ == 0, False, ["attT", "woutA"], [pn])
            for c_ in range(4):
                mm(ps[:, :], rwT[:, c_, fsl], woutR[:, c_, csl], False, c_ == 3, ["rwT", "woutR"], [pn])
            tt("dve", x1all[:, t, csl], ps[:, :], xb_[:, csl], ALU.add, [pn, xnm], [("x1all", t, nb)])
    if "x1" in dbg:
        tap("x1", x1all[:], [128, NT, D], ["x1all"])
    S.fence()
    A.release(m4)

    top4 = A.mark()
    A.off = m_rw
    wq_bf = A.alloc("wq_bf", [128, 8, 2048], BF16)
    skT = A.alloc("skT", [128, 2, 128], BF16)
    iota16 = A.alloc("iota16", [128, 16], F32)
    h2T = A.alloc("h2T", [128, 8, 128], BF16)
    qTs = A.alloc("qTs", [128, 16, 128], BF16)
    junkb = A.alloc("junkb", [128, D], BF16)
    h2bf = [A.alloc("h2bf%d" % i, [128, D], BF16) for i in range(2)]
    assert A.off <= m_ht
    A.off = top4
    dma("pool", wq_bf[:], wqry_d.rearrange("(j p) n -> p j n", p=128), [], ["wq_bf"])
    dma("pool", skT[:], skT_d, [], ["skT"])
    dma("sp", iota16[:], iota_d, [], ["iota16"])
    h2 = A.alloc("h2", [128, D], F32)
    bigA = A.alloc("bigA", [128, 2048], F32)
    wk = [A.alloc("wk%d" % i, [128, 128], F32) for i in range(2)]
    wk2 = [A.alloc("wk2%d" % i, [128, 256], F32) for i in range(2)]
    vals = A.alloc("vals", [128, 16, 16], F32)
    idxu = A.alloc("idxu", [128, 16, 16], U32)
    idxf = A.alloc("idxf", [128, 16, 16], F32)
    tops = A.alloc("tops", [128, 8, 16], F32)
    topp = A.alloc("topp", [128, 8, 16], U32)
    hiu = A.alloc("hiu", [128, 8, 16], U32)
    lou = A.alloc("lou", [128, 8, 16], U32)
    hif = A.alloc("hif", [128, 8, 16], F32)
    lof = A.alloc("lof", [128, 8, 16], F32)
    i1s = A.alloc("i1s", [128, 8, 16], F32)
    i2s = A.alloc("i2s", [128, 8, 16], F32)
    eidx = [A.alloc("eidx%d" % i, [128, 128], I32) for i in range(2)]
    gate = [A.alloc("gate%d" % i, [128, 8, 16], F32) for i in range(2)]
    gsum = A.alloc("gsum", [128, 8], F32)
    ss2 = A.alloc("ss2", [128, 1], F32)
    GJ = 4
    pre4 = [A.alloc("pre4_%d" % i, [128, GJ], F32) for i in range(2)]
    coef4 = [A.alloc("coef4_%d" % i, [128, GJ], F32) for i in range(2)]
    dg4 = [A.alloc("dg4_%d" % i, [128, GJ, 128], BF16) for i in range(2)]
    guv = [[A.alloc("guv%d_%d" % (i, jj), [128, 2 * D], BF16) for jj in range(GJ)] for i in range(2)]
    otile = A.alloc("otile", [128, D], F32)
    out_v = out_d.rearrange("(t p) d -> t p d", p=128)
    NPF = [4]

    def pf4():
        i = rr["f"] % 4
        rr["f"] = (i + 1) % 4
        return psf[i], "psf%d" % i

    def retr(t, sl):
        x1 = x1all[:, t, :]
        x1n = ("x1all", t)
        hb, hbn = h2bf[sl], "h2bf%d" % sl
        ei, ein = eidx[sl], "eidx%d" % sl
        ga, gan = gate[sl], "gate%d" % sl
        act(junkb[:], x1, AF.Square, [x1n], ["junkb", "ss2"], accum=ss2[:, 0:1])
        ts("dve", ss2[:], ss2[:], 1.0 / D, 1e-6, ALU.mult, ALU.add, ["ss2"], ["ss2"])
        act(ss2[:], ss2[:], AF.Sqrt, ["ss2"], ["ss2"])
        S.op("dve", (lambda o: (lambda h: h.reciprocal(out=o, in_=o)))(ss2[:]), reads=["ss2"], writes=["ss2"])
        yield
        stt(h2[:], x1, ss2[:, 0:1], bc3[:, 0, :], ALU.mult, ALU.mult, [x1n, "ss2", ("bc3", 0)], ["h2"])
        tt("pool", h2[:], h2[:], bc3[:, 1, :], ALU.add, ["h2", ("bc3", 1)], ["h2"])
        cp("act", hb[:], h2[:], ["h2"], [hbn])
        yield
        ps, pn = pb()
        for j in range(8):
            tr(ps[:, j * 128:(j + 1) * 128], hb[:, j * 128:(j + 1) * 128], ident[:], [hbn, "ident"], [pn])
        cp("act", h2T[:], ps[:, :].rearrange("p (j t) -> p j t", t=128), [pn], ["h2T"])
        yield
        for g4 in range(4):
            ps, pn = pf4()
            for gi in range(4):
                g_ = g4 * 4 + gi
                for kc in range(8):
                    mm(ps[:, gi * 128:(gi + 1) * 128], wq_bf[:, kc, g_ * 128:(g_ + 1) * 128], h2T[:, kc, :], kc == 0, kc == 7,
                       ["wq_bf", "h2T"], [pn])
            cp("act", qTs[:, g4 * 4:(g4 + 1) * 4, :], ps[:, :].rearrange("p (g t) -> p g t", t=128), [pn], [("qTs", g4)])
            yield
        for g4 in range(4):
            ps, pn = pf4()
            for gi in range(4):
                g_ = g4 * 4 + gi
                mm(ps[:, gi * 128:(gi + 1) * 128], qTs[:, g_, :], skT[:, g_ % 2, :], True, True, [("qTs", g4), "skT"], [pn])
            cp("act", bigA[:, g4 * 512:(g4 + 1) * 512], ps[:, :], [pn], [("bigA", g4)])
            yield
        for g_ in range(16):
            scg = bigA[:, g_ * 128:(g_ + 1) * 128]
            w_, wn_ = wk[g_ % 2], "wk%d" % (g_ % 2)
            S.op("dve", (lambda o, i_: (lambda h: h.max(out=o, in_=i_)))(vals[:, g_, 0:8], scg), reads=[("bigA", g_ // 4)], writes=[("vals", g_, 0)])
            S.op("dve", (lambda o, r_, i_: (lambda h: h.match_replace(out=o, in_to_replace=r_, in_values=i_, imm_value=-1e30)))(
                w_[:], vals[:, g_, 0:8], scg), reads=[("bigA", g_ // 4), ("vals", g_, 0)], writes=[wn_])
            S.op("dve", (lambda o, i_: (lambda h: h.max(out=o, in_=i_)))(vals[:, g_, 8:16], w_[:]), reads=[wn_], writes=[("vals", g_, 1)])
            for hf_ in range(2):
                S.op("dve", (lambda o, m_, i_: (lambda h: h.max_index(out=o, in_max=m_, in_values=i_)))(
                    idxu[:, g_, hf_ * 8:(hf_ + 1) * 8], vals[:, g_, hf_ * 8:(hf_ + 1) * 8], scg),
                    reads=[("bigA", g_ // 4), ("vals", g_, hf_)], writes=[("idxu", g_, hf_)])
            yield
        cp("pool", idxf[:], idxu[:], ["idxu"], ["idxf"])
        vv = vals[:].rearrange("p (h s) k -> p h s k", s=2)
        cand = bigA[:].rearrange("p (h i j) -> p h i j", h=8, i=16)
        tt("dve", cand, vv[:, :, 0, :].unsqueeze(3).broadcast_to([128, 8, 16, 16]), vv[:, :, 1, :].unsqueeze(2).broadcast_to([128, 8, 16, 16]),
           ALU.add, ["vals"], ["bigA"])
        yield
        for h_ in range(8):
            cg = bigA[:, h_ * 256:(h_ + 1) * 256]
            w_, wn_ = wk2[h_ % 2], "wk2%d" % (h_ % 2)
            S.op("dve", (lambda o, i_: (lambda h: h.max(out=o, in_=i_)))(tops[:, h_, 0:8], cg), reads=["bigA"], writes=[("tops", h_, 0)])
            S.op("dve", (lambda o, r_, i_: (lambda h: h.match_replace(out=o, in_to_replace=r_, in_values=i_, imm_value=-1e30)))(
                w_[:], tops[:, h_, 0:8], cg), reads=["bigA", ("tops", h_, 0)], writes=[wn_])
            S.op("dve", (lambda o, i_: (lambda h: h.max(out=o, in_=i_)))(tops[:, h_, 8:16], w_[:]), reads=[wn_], writes=[("tops", h_, 1)])
            for hf_ in range(2):
                S.op("dve", (lambda o, m_, i_: (lambda h: h.max_index(out=o, in_max=m_, in_values=i_)))(
                    topp[:, h_, hf_ * 8:(hf_ + 1) * 8], tops[:, h_, hf_ * 8:(hf_ + 1) * 8], cg),
                    reads=["bigA", ("tops", h_, hf_)], writes=[("topp", h_, hf_)])
            yield
        S.op("dve", lambda h: h.tensor_single_scalar(out=hiu[:], in_=topp[:], scalar=4, op=ALU.logical_shift_right), reads=["topp"], writes=["hiu"])
        S.op("dve", lambda h: h.tensor_single_scalar(out=lou[:], in_=topp[:], scalar=15, op=ALU.bitwise_and), reads=["topp"], writes=["lou"])
        cp("pool", hif[:], hiu[:], ["hiu"], ["hif"])
        cp("pool", lof[:], lou[:], ["lou"], ["lof"])
        yield
        idv = idxf[:].rearrange("p (h s) k -> p h s k", s=2)
        eq = bigA[:].rearrange("p (h k i) -> p h k i", h=8, k=16)
        io_b = iota16[:].unsqueeze(1).unsqueeze(1).broadcast_to([128, 8, 16, 16])
        for (sel, seln, src_i, dst, dstn) in ((hif, "hif", 0, i1s, "i1s"), (lof, "lof", 1, i2s, "i2s")):
            tt("dve", eq, sel[:].unsqueeze(3).broadcast_to([128, 8, 16, 16]), io_b, ALU.is_equal, [seln, "iota16", "tops", "topp"], ["bigA"])
            yield
            tt("dve", eq, eq, idv[:, :, src_i, :].unsqueeze(2).broadcast_to([128, 8, 16, 16]), ALU.mult, ["bigA", "idxf"], ["bigA"])
            yield
            S.op("dve", (lambda o: (lambda h: h.tensor_reduce(out=o, in_=eq, axis=AX.X, op=ALU.add)))(dst[:]), reads=["bigA"], writes=[dstn])
            yield
        stt(i1s[:], i1s[:], 128.0, i2s[:], ALU.mult, ALU.add, ["i1s", "i2s"], ["i1s"])
        cp("dve", ei[:], i1s[:].rearrange("p h k -> p (h k)"), ["i1s"], [ein])
        tt("dve", ga[:], tops[:], tops[:, :, 0:1].broadcast_to([128, 8, 16]), ALU.subtract, ["tops"], [gan])
        act(ga[:], ga[:], AF.Exp, [gan], [gan])
        S.op("dve", (lambda g_: (lambda h: h.tensor_reduce(out=gsum[:], in_=g_, axis=AX.X, op=ALU.add)))(ga[:]), reads=[gan], writes=["gsum"])
        S.op("dve", lambda h: h.reciprocal(out=gsum[:], in_=gsum[:]), reads=["gsum"], writes=["gsum"])
        tt("dve", ga[:], ga[:], gsum[:].unsqueeze(2).broadcast_to([128, 8, 16]), ALU.mult, [gan, "gsum"], [gan])
        if t == 0:
            tap("eidx", ei[:], [128, 128], [ein], I32)
            tap("gate", ga[:], [128, 8, 16], [gan])
        yield

    NGRP = 128 // GJ

    def issue_gathers(t, sl, g):
        bs = (t * NGRP + g) % 2
        for jj in range(GJ):
            j = g * GJ + jj
            S.op("pool", (lambda o, ix: (lambda h: h.indirect_dma_start(out=o, out_offset=None, in_=tab_d[:, :],
                                                                         in_offset=bass.IndirectOffsetOnAxis(ap=ix, axis=0))))(
                guv[bs][jj][:], eidx[sl][:, j:j + 1]), reads=["eidx%d" % sl, "tab"], writes=["guv%d_%d" % (bs, jj)], dma=True,
                swsem="gsem%d_%d" % (bs, jj))

    def consume(t, sl, g, psO):
        bs = (t * NGRP + g) % 2
        j0 = g * GJ
        for jj in range(GJ):
            stt(junkb[:], guv[bs][jj][:, 0:D], 1.0, h2bf[sl][:], ALU.mult, ALU.mult, ["guv%d_%d" % (bs, jj), "h2bf%d" % sl],
                ["junkb", ("pre4_%d" % bs, jj)], accum=pre4[bs][:, jj:jj + 1])
        act(coef4[bs][:], pre4[bs][:], AF.Gelu, ["pre4_%d" % bs], ["coef4_%d" % bs])
        tt("dve", coef4[bs][:], coef4[bs][:], gate[sl][:].rearrange("p h k -> p (h k)")[:, j0:j0 + GJ], ALU.mult,
           ["coef4_%d" % bs, "gate%d" % sl], ["coef4_%d" % bs])
        tt("dve", dg4[bs][:], ident[:].unsqueeze(1).broadcast_to([128, GJ, 128]), coef4[bs][:].unsqueeze(2).broadcast_to([128, GJ, 128]),
           ALU.mult, ["ident", "coef4_%d" % bs], ["dg4_%d" % bs])
        for jj in range(GJ):
            j = j0 + jj
            for nb in range(2):
                mm(psf[4 + nb][:, :], dg4[bs][:, jj, :], guv[bs][jj][:, D + nb * 512:D + (nb + 1) * 512], j == 0, j == 127,
                   ["dg4_%d" % bs, "guv%d_%d" % (bs, jj)], ["psf%d" % (4 + nb)])

    for _ in retr(0, 0):
        pass
    for t in range(NT):
        sl = t % 2
        nxt = retr(t + 1, (t + 1) % 2) if t + 1 < NT else None
        issue_gathers(t, sl, 0)
        for g in range(NGRP):
            if g + 1 < NGRP:
                issue_gathers(t, sl, g + 1)
            consume(t, sl, g, None)
            if nxt is not None:
                for _ in range(2):
                    try:
                        next(nxt)
                    except StopIteration:
                        nxt = None
                        break
        if nxt is not None:
            for _ in nxt:
                pass
        for nb in range(2):
            csl = slice(nb * 512, (nb + 1) * 512)
            tt("dve", otile[:, csl], psf[4 + nb][:, :], bc3[:, 2, csl], ALU.mult, ["psf%d" % (4 + nb), ("bc3", 2)], [("otile", nb)])
            tt("pool", otile[:, csl], otile[:, csl], x1all[:, t, csl], ALU.add, [("otile", nb), ("x1all", t)], [("otile", nb)])
        dma("sp", out_v[t], otile[:], ["otile"], ["out_d"])
    if dbg:
        print("ops", len(S.ops), {e: S.count[e] for e in ENGS})
    S.emit()
    return taps


def host_inputs(inputs, b):
    f = lambda k: np.ascontiguousarray(inputs[k][0], dtype=np.float32)
    fp = lambda v, n: np.ascontiguousarray(v.reshape(n, 128).T)
    m = {}
    m["x"] = np.ascontiguousarray(inputs["x"][b], dtype=np.float32)
    m["c_fp"] = fp(np.asarray(inputs["c"][b], dtype=np.float32), 8)
    m["ada_w"] = f("ada_w")
    m["ada_b"] = f("ada_b").reshape(1, -1)
    m["g1_fp"] = fp(f("norm1_g"), 8)
    m["g2_fp"] = fp(f("norm2_g"), 8)
    m["g2_b"] = np.ascontiguousarray(np.broadcast_to(f("norm2_g")[None, :], (128, D)))
    m["w_in"] = f("w_in")
    m["qkg"] = np.ascontiguousarray(np.stack([np.tile(f("q_norm_g"), 2), np.tile(f("k_norm_g"), 2)], axis=1))
    m["eb"] = EB_CONST
    m["up"] = np.ascontiguousarray(np.concatenate([f("w_decay_up"), f("a_gate_up")], axis=1).transpose(1, 0, 2))
    m["g_up"] = f("g_up")
    m["lnx"] = np.ascontiguousarray(np.broadcast_to(np.stack([f("lnx_g"), f("lnx_b")])[None], (128, 2, 512)))
    m["mu"] = np.ascontiguousarray(np.stack([fp(f("mu_prev"), 14), fp(f("mu_next"), 14)], axis=1))
    m["w0a0"] = np.ascontiguousarray(np.stack([f("w_decay0").reshape(2, 4, 128), f("a_gate0").reshape(2, 4, 128)]).transpose(3, 0, 1, 2))
    m["kkr"] = np.ascontiguousarray(np.stack([fp(f("k_k"), 4), fp(f("k_a"), 4), fp(f("r_k").reshape(-1), 4)], axis=1))
    m["cpk"] = CPK_CONST
    m["w_out"] = f("w_out")
    m["w_query"] = f("peer_w_query")
    m["skT"] = np.ascontiguousarray(np.stack([f("peer_sub_keys1").T, f("peer_sub_keys2").T], axis=1))
    m["iota16"] = np.ascontiguousarray(np.broadcast_to(np.arange(16, dtype=np.float32)[None, :], (128, 16)))
    m["peer_u"] = f("peer_u")
    m["peer_v"] = f("peer_v")
    return m


def _make_cpk():
    r = np.arange(128)[:, None]
    c = np.arange(128)[None, :]
    SL, SU, IL, IU = (r > c), (r < c), (r >= c), (r <= c)
    out = np.zeros((128, 1282), np.float32)
    out[:, 0:640] = np.concatenate([SU, SU, IU, IU, SL], axis=1)
    out[:, 640:1280] = np.concatenate([SL, SL, IL, IL, SU], axis=1)
    out[0:64, 1280] = 1.0
    out[64:128, 1281] = 1.0
    return np.ascontiguousarray(out.astype(ml_dtypes.bfloat16))


CPK_CONST = _make_cpk()


def _make_eb():
    eb = np.zeros((4, 128, 7, 2, 2, 128), np.float64)
    k = np.arange(128)[:, None]
    q = np.arange(128)[None, :]
    for hp in range(4):
        for hh in range(2):
            slope = 2.0 ** (-(2 * hp + hh + 1))
            for pi, dil in enumerate((1, 4)):
                for ty in range(3):
                    if ty == 0:
                        off = q - k
                        own = k < 64
                    else:
                        off = q - (k - 64)
                        own = k >= 0
                    eb[hp, :, pi * 3 + ty, hh, 0, :] = np.where((np.abs(off) <= 64) & own, np.exp(-slope * dil * np.abs(off)), 0.0)
                    if ty == 2:
                        off = q - k
                        own = k >= 64
                    else:
                        off = q - (k + 64)
                        own = k >= 0
                    eb[hp, :, pi * 3 + ty, hh, 1, :] = np.where((np.abs(off) <= 64) & own, np.exp(-slope * dil * np.abs(off)), 0.0)
            off = q - k
            eb[hp, :, 6, hh, 0, :] = np.where(np.abs(off) <= 64, np.exp(-slope * 16 * np.abs(off)), 0.0)
    return np.ascontiguousarray(eb.reshape(4, 128, -1).astype(ml_dtypes.bfloat16))


EB_CONST = _make_eb()


def kernel(**inputs):
    nc = bass.Bass("TRN2", target_bir_lowering=False)
    build(nc)
    in_maps = [host_inputs(inputs, b) for b in range(8)]
    res = run_bass_kernel_spmd(nc, in_maps, core_ids=list(range(8)))
    return np.stack([np.asarray(r["out"], dtype=np.float32) for r in res.results], axis=0)
```

```python
import contextlib
import numpy as np
import ml_dtypes
import concourse.bass as bass
import concourse.mybir as mybir
from concourse.bass_utils import run_bass_kernel_spmd

F32 = mybir.dt.float32
BF16 = mybir.dt.bfloat16
I32 = mybir.dt.int32
U32 = mybir.dt.uint32
ALU = mybir.AluOpType
AF = mybir.ActivationFunctionType
AX = mybir.AxisListType

N_DMA_SEMS = 24
ENGS = ("pe", "dve", "act", "pool", "sp")

D = 1024
SEQ = 2048
NT = SEQ // 128
INW = 3328
CDEC = float(np.exp(-0.5))
import os
MAXOPS = int(os.environ.get("MK_MAXOPS", "100000000"))
SW_CLEAR = False
NLEV = int(os.environ.get('MK_NLEV', '5'))


class Op:
    __slots__ = ("eng", "fn", "deps", "done", "dma", "clear", "nofence")

    def __init__(self, eng, fn, dma):
        self.eng = eng
        self.fn = fn
        self.dma = dma
        self.deps = []
        self.done = None
        self.clear = None
        self.nofence = False


class Sched:
    def __init__(self, nc):
        self.nc = nc
        self.ops = []
        self.state = {}
        self.count = {e: 0 for e in ENGS}
        self.dma_uses = [0] * N_DMA_SEMS
        self.dma_last = [None] * N_DMA_SEMS
        self.dma_rr = 0
        self.fence_ops = []
        self.resetting = set()
        self.sw_count = 0
        self.sw_gen = {}
        self.sems = {}

    def _keys(self, buf, sub):
        d = self.state.setdefault(buf, {})
        if sub is None:
            keys = list(d.keys())
            if None not in d:
                keys.append(None)
        else:
            keys = [sub, None]
        return d, keys

    def fence(self):
        last = {}
        for o in self.ops:
            if o.nofence:
                continue
            s, v = o.done
            if v > last.get(s, (0, None))[0]:
                last[s] = (v, o)
        self.fence_ops = [o for (_, o) in last.values()]
        self.state = {b: st for b, st in self.state.items() if b in ("tab", "x1_d")}

    def op(self, eng, fn, reads=(), writes=(), dma=False, swsem=None):
        if len(self.ops) >= MAXOPS:
            return None
        clr = None
        if dma and swsem is not None and SW_CLEAR:
            clr = self.op("pool", (lambda key: (lambda h: h.sem_clear(self.sems[key])))(swsem), reads=reads, writes=writes)
        o = Op(eng, fn, dma)
        o.clear = clr
        deps = list(self.fence_ops)
        if clr is not None:
            deps.append(clr)
        for r in reads:
            buf, sub = (r[0], tuple(r[1:])) if isinstance(r, tuple) else (r, None)
            d, keys = self._keys(buf, sub)
            for k in keys:
                st = d.get(k)
                if st and st[0] is not None:
                    deps.append(st[0])
        for w in writes:
            buf, sub = (w[0], tuple(w[1:])) if isinstance(w, tuple) else (w, None)
            d, keys = self._keys(buf, sub)
            for k in keys:
                st = d.get(k)
                if st:
                    if st[0] is not None:
                        deps.append(st[0])
                    deps.extend(st[1])
        if dma and swsem is not None:
            if SW_CLEAR:
                self.resetting.add(swsem)
            self.sw_gen[swsem] = self.sw_gen.get(swsem, 0) + 1
            o.done = (swsem, 16 * self.sw_gen[swsem])
        elif dma and eng == "pool":
            self.sw_count += 1
            o.done = ("sw%d" % self.sw_count, 16)
        elif dma:
            i = self.dma_rr
            self.dma_rr = (i + 1) % N_DMA_SEMS
            if self.dma_last[i] is not None:
                deps.append(self.dma_last[i])
            self.dma_uses[i] += 1
            o.done = ("dma%d" % i, 16 * self.dma_uses[i])
            self.dma_last[i] = o
        else:
            self.count[eng] += 1
            o.done = (eng, self.count[eng])
        for r in reads:
            buf, sub = (r[0], tuple(r[1:])) if isinstance(r, tuple) else (r, None)
            st = self.state[buf].setdefault(sub, [None, []])
            st[1].append(o)
        for w in writes:
            buf, sub = (w[0], tuple(w[1:])) if isinstance(w, tuple) else (w, None)
            d = self.state[buf]
            if sub is None:
                for k in list(d.keys()):
                    d[k] = [o, []]
                d[None] = [o, []]
            else:
                d[sub] = [o, []]
        seen = {}
        deps = deps + [p.clear for p in deps if getattr(p, "clear", None) is not None]
        for p in deps:
            s, v = p.done
            if eng == "pe" and (not dma) and (not p.dma) and p.eng == "pe":
                continue
            if v > seen.get(s, 0):
                seen[s] = v
        o.deps = list(seen.items())
        self.ops.append(o)
        return o

    def emit(self):
        nc = self.nc
        with contextlib.ExitStack() as es:
            sems = self.sems
            names = list(ENGS) + ["dma%d" % i for i in range(N_DMA_SEMS)]
            for o in self.ops:
                if o.done[0] not in names:
                    names.append(o.done[0])
            for nm in names:
                sems[nm] = es.enter_context(nc.semaphore("s_" + nm))
            block = es.enter_context(nc.Block())
            per = {e: [o for o in self.ops if o.eng == e] for e in ENGS}
            final = {}
            for o in self.ops:
                s, v = o.done
                final[s] = max(final.get(s, 0), v)

            def run(e, h):
                known = {}
                for o in per[e]:
                    for s, v in o.deps:
                        if v > known.get(s, 0):
                            h.wait_ge(sems[s], 16 if s in self.resetting else v)
                            known[s] = v
                    inst = o.fn(h)
                    s, v = o.done
                    inst.then_inc(sems[s], 16 if o.dma else 1)
                if e == "sp":
                    for s, v in final.items():
                        if s in self.resetting:
                            h.wait_ge(sems[s], 16)
                            continue
                        if v > known.get(s, 0):
                            h.wait_ge(sems[s], v)

            @block.tensor
            def _(h):
                run("pe", h)

            @block.vector
            def _(h):
                run("dve", h)

            @block.scalar
            def _(h):
                run("act", h)

            @block.gpsimd
            def _(h):
                run("pool", h)

            @block.sync
            def _(h):
                run("sp", h)


SB_BASE = 16512
SB_END = 229376


class Arena:
    def __init__(self, nc):
        self.nc = nc
        self.off = SB_BASE
        self.n = 0

    def alloc(self, name, shape, dt):
        esz = 2 if dt == BF16 else 4
        sz = int(np.prod(shape[1:])) * esz
        sz = (sz + 63) // 64 * 64
        assert self.off + sz <= SB_END, ("SBUF overflow", name, self.off, sz)
        self.n += 1
        t = self.nc.alloc_sbuf_tensor_at("%s_%d" % (name, self.n), list(shape), dt, offset=self.off)
        self.off += sz
        return t

    def mark(self):
        return self.off

    def release(self, m):
        self.peak = max(getattr(self, "peak", 0), self.off)
        if os.environ.get("MK_VERBOSE"):
            print("arena peak", self.peak - SB_BASE, "of", SB_END - SB_BASE)
        self.off = m


def build(nc, dbg=()):
    S = Sched(nc)
    A = Arena(nc)
    dbg = set(dbg)
    taps = {}

    def din(name, shape, dt=F32):
        return nc.dram_tensor(name, list(shape), dt, kind="ExternalInput").ap()

    def dout(name, shape, dt=F32):
        return nc.dram_tensor(name, list(shape), dt, kind="ExternalOutput").ap()

    x_d = din("x", [SEQ, D])
    c_d = din("c_fp", [128, 8])
    adaw_d = din("ada_w", [D, 6 * D])
    adab_d = din("ada_b", [1, 6 * D])
    g1_d = din("g1_fp", [128, 8])
    g2_d = din("g2_fp", [128, 8])
    g2b_d = din("g2_b", [128, D])
    win_d = din("w_in", [D, INW])
    out_d = dout("out", [SEQ, D])
    qkg_d = din("qkg", [128, 2])
    eb_d = din("eb", [4, 128, 7 * 2 * 2 * 128], BF16)
    up_d = din("up", [128, 2, 512])
    gup_d = din("g_up", [128, 512])
    lnx_d = din("lnx", [128, 2, 512])
    mu_d = din("mu", [128, 2, 14])
    w0a0_d = din("w0a0", [128, 2, 2, 4])
    kkr_d = din("kkr", [128, 3, 4])
    cpk_d = din("cpk", [128, 1282], BF16)
    wout_d = din("w_out", [D, D])
    wqry_d = din("w_query", [D, 2048])
    skT_d = din("skT", [128, 2, 128])
    iota_d = din("iota16", [128, 16])
    peeru_d = din("peer_u", [16384, D])
    peerv_d = din("peer_v", [16384, D])
    tab_d = nc.dram_tensor("uv_tab", [16384, 2 * D], BF16, kind="Internal").ap()
    x1_d = nc.dram_tensor("x1_scratch", [NT, 128, D], F32, kind="Internal").ap()
    bc_d = nc.dram_tensor("bc_scratch", [4, 128, D], F32, kind="Internal").ap()

    def mm(out, lhsT, rhs, start, stop, reads, writes):
        S.op("pe", lambda h: h.matmul(out, lhsT=lhsT, rhs=rhs, start=start, stop=stop), reads=reads, writes=writes)

    def tr(out, in_, ident, reads, writes):
        S.op("pe", lambda h: h.transpose(out=out, in_=in_, identity=ident), reads=reads, writes=writes)

    def act(out, in_, func, reads, writes, bias=0.0, scale=1.0, accum=None):
        if accum is None:
            S.op("act", lambda h: h.activation(out=out, in_=in_, func=func, bias=bias, scale=scale), reads=reads, writes=writes)
        else:
            S.op("act", lambda h: h.activation(out=out, in_=in_, func=func, bias=bias, scale=scale, accum_out=accum), reads=reads, writes=writes)

    def tt(eng, out, in0, in1, op, reads, writes):
        S.op(eng, lambda h: h.tensor_tensor(out=out, in0=in0, in1=in1, op=op), reads=reads, writes=writes)

    def ts(eng, out, in0, s1, s2, op0, op1, reads, writes):
        if s2 is None:
            S.op(eng, lambda h: h.tensor_scalar(out=out, in0=in0, scalar1=s1, scalar2=None, op0=op0), reads=reads, writes=writes)
        else:
            S.op(eng, lambda h: h.tensor_scalar(out=out, in0=in0, scalar1=s1, scalar2=s2, op0=op0, op1=op1), reads=reads, writes=writes)

    def stt(out, in0, scalar, in1, op0, op1, reads, writes, accum=None):
        if accum is None:
            S.op("dve", lambda h: h.scalar_tensor_tensor(out=out, in0=in0, scalar=scalar, in1=in1, op0=op0, op1=op1), reads=reads, writes=writes)
        else:
            S.op("dve", lambda h: h.scalar_tensor_tensor(out=out, in0=in0, scalar=scalar, in1=in1, op0=op0, op1=op1, accum_out=accum), reads=reads, writes=writes)

    def cp(eng, out, in_, reads, writes):
        if eng == "act":
            S.op("act", lambda h: h.copy(out=out, in_=in_), reads=reads, writes=writes)
        else:
            S.op(eng, lambda h: h.tensor_copy(out=out, in_=in_), reads=reads, writes=writes)

    def memset(eng, ap, val, writes):
        S.op(eng, lambda h: h.memset(ap, val), writes=writes)

    def dma(eng, out, in_, reads, writes):
        S.op(eng, lambda h: h.dma_start(out=out, in_=in_), reads=reads, writes=writes, dma=True)

    def tap(name, src_ap, shape, reads, dt=F32):
        if name in dbg:
            t = dout("dbg_" + name, shape, dt)
            taps[name] = t
            dma("sp", t, src_ap, reads, [])

    PSF = nc.alloc_psum_tensor("psf", [128, 6, 512], F32)
    psf = [PSF[:, i, :] for i in range(6)]
    psb = [nc.alloc_psum_tensor("psb%d" % i, [128, 1024], BF16) for i in range(2)]
    rr = {"f": 0, "b": 0}

    def pf():
        i = rr["f"]
        rr["f"] = (i + 1) % 6
        return psf[i], "psf%d" % i

    def pf2():
        i = ((rr["f"] + 1) // 2 * 2) % 6
        rr["f"] = (i + 2) % 6
        return PSF[:, i:i + 2, :], ["psf%d" % i, "psf%d" % (i + 1)]

    def pb():
        i = rr["b"]
        rr["b"] = (i + 1) % 2
        return psb[i], "psb%d" % i

    ident = A.alloc("ident", [128, 128], BF16)
    ones_f = A.alloc("ones_f", [128, 128], F32)
    memset("pool", ident[:], 0.0, ["ident"])
    S.op("pool", lambda h: h.affine_select(out=ident[:], in_=ident[:], pattern=[[-1, 128]], compare_op=ALU.not_equal,
                                           fill=1.0, base=0, channel_multiplier=1), reads=["ident"], writes=["ident"])
    memset("pool", ones_f[:], 1.0, ["ones_f"])

    for r0 in range(0, 16384, 2048):
        for hv, src in ((0, peeru_d), (1, peerv_d)):
            o_ = S.op("pool", (lambda o, i: (lambda h: h.dma_start(out=o, in_=i)))(tab_d[r0:r0 + 2048, hv * D:(hv + 1) * D], src[r0:r0 + 2048, :]),
                      reads=[], writes=[("tab", r0, hv)], dma=True)
            if o_ is not None:
                o_.nofence = True

    mod_fp = A.alloc("mod_fp", [128, 48], F32)
    g1_fp = A.alloc("g1_fp", [128, 8], F32)
    g2_fp = A.alloc("g2_fp", [128, 8], F32)
    gs1_fp = A.alloc("gs1_fp", [128, 8], F32)
    gs2_fp = A.alloc("gs2_fp", [128, 8], F32)
    m0 = A.mark()
    gate1_b = A.alloc("gate1_b", [128, D], F32)
    gs2_b = A.alloc("gs2_b", [128, D], F32)
    shift2_b = A.alloc("shift2_b", [128, D], F32)
    gate2_b = A.alloc("gate2_b", [128, D], F32)
    c_sb = A.alloc("c_sb", [128, 8], F32)
    sc_sb = A.alloc("sc_sb", [128, 8], F32)
    adab = A.alloc("adab", [1, 6 * D], F32)
    modrow = A.alloc("modrow", [1, 6 * D], F32)
    dma("sp", c_sb[:], c_d, [], ["c_sb"])
    dma("sp", adab[:], adab_d, [], ["adab"])
    dma("sp", g1_fp[:], g1_d, [], ["g1_fp"])
    dma("sp", g2_fp[:], g2_d, [], ["g2_fp"])
    dma("sp", gs2_b[:], g2b_d, [], ["gs2_b"])
    act(sc_sb[:], c_sb[:], AF.Silu, ["c_sb"], ["sc_sb"])
    wblk = [A.alloc("adaw%d" % i, [128, 8, 512], F32) for i in range(2)]
    adaw_v = adaw_d.rearrange("(j p) n -> p j n", p=128)
    for nb in range(12):
        wb = wblk[nb % 2]
        wn = "adaw%d" % (nb % 2)
        dma("sp", wb[:], adaw_v[:, :, nb * 512:(nb + 1) * 512], [], [wn])
        ps, pn = pf()
        for j in range(8):
            mm(ps[0:1, :], sc_sb[:, j:j + 1], wb[:, j, :], j == 0, j == 7, ["sc_sb", wn], [pn])
        tt("dve", modrow[0:1, nb * 512:(nb + 1) * 512], ps[0:1, :], adab[0:1, nb * 512:(nb + 1) * 512], ALU.add,
           [pn, "adab"], [("modrow", nb)])
    ps, pn = pf()
    for j in range(48):
        mm(ps[:, 2 * j:2 * j + 2], modrow[0:1, j * 128:(j + 1) * 128], ones_f[0:1, 0:2], True, True,
           ["modrow", "ones_f"], [pn])
    cp("dve", mod_fp[:], ps[:, 0:96].rearrange("p (j t) -> p j t", t=2)[:, :, 0], [pn], ["mod_fp"])
    stt(gs1_fp[:], mod_fp[:, 8:16], 1.0, g1_fp[:], ALU.add, ALU.mult, ["mod_fp", "g1_fp"], ["gs1_fp"])
    stt(gs2_fp[:], mod_fp[:, 32:40], 1.0, g2_fp[:], ALU.add, ALU.mult, ["mod_fp", "g2_fp"], ["gs2_fp"])
    for seg, dst, dn, kind in ((2, gate1_b, "gate1_b", 0), (4, gs2_b, "gs2_b", 1), (3, shift2_b, "shift2_b", 0),
                               (5, gate2_b, "gate2_b", 0)):
        for hb in range(2):
            ps, pn = pf()
            c0 = seg * D + hb * 512
            mm(ps[:, :], ones_f[0:1, 0:128], modrow[0:1, c0:c0 + 512], True, True, ["ones_f", "modrow"], [pn])
            dsl = dst[:, hb * 512:(hb + 1) * 512]
            if kind == 0:
                cp("act", dsl, ps[:, :], [pn], [(dn, hb)])
            else:
                stt(dsl, ps[:, :], 1.0, dsl, ALU.add, ALU.mult, [pn, (dn, hb)], [(dn, hb)])
    for i_, (bt__, bn__) in enumerate(((gate1_b, "gate1_b"), (gs2_b, "gs2_b"), (shift2_b, "shift2_b"), (gate2_b, "gate2_b"))):
        dma("sp", bc_d[i_], bt__[:], [bn__], ["bc_d"])
    tap("modrow", modrow[:], [1, 6 * D], ["modrow"])
    tap("gs2_b", gs2_b[:], [128, D], ["gs2_b"])
    tap("mod_fp", mod_fp[:], [128, 48], ["mod_fp"])
    S.fence()
    A.release(m0)

    m_rw = A.mark()
    rwT = A.alloc("rwT", [128, 4, SEQ], BF16)
    m_att = A.mark()
    attT = A.alloc("attT", [128, 8, SEQ], BF16)
    m_ht = A.mark()
    hT = A.alloc("hT", [128, 8, SEQ], BF16)
    m1 = A.mark()
    xt = [A.alloc("xt%d" % i, [128, D], F32) for i in range(2)]
    xn = [A.alloc("xn%d" % i, [128, D], BF16) for i in range(2)]
    junk = A.alloc("junk", [128, D], BF16)
    ss = A.alloc("ss", [128, NT], F32)
    rstd = A.alloc("rstd", [128, NT], F32)
    x_v = x_d.rearrange("(t p) d -> t p d", p=128)
    for t in range(NT):
        xb_, xnm = xt[t % 2], "xt%d" % (t % 2)
        nb_, nnm = xn[t % 2], "xn%d" % (t % 2)
        dma("sp", xb_[:], x_v[t], [], [xnm])
        act(junk[:], xb_[:], AF.Square, [xnm], ["junk", ("ss", t)], accum=ss[:, t:t + 1])
        ts("dve", rstd[:, t:t + 1], ss[:, t:t + 1], 1.0 / D, 1e-6, ALU.mult, ALU.add, [("ss", t)], [("rstd", t)])
        act(rstd[:, t:t + 1], rstd[:, t:t + 1], AF.Sqrt, [("rstd", t)], [("rstd", t)])
        S.op("dve", (lambda o: (lambda h: h.reciprocal(out=o, in_=o)))(rstd[:, t:t + 1]), reads=[("rstd", t)], writes=[("rstd", t)])
        act(nb_[:], xb_[:], AF.Copy, [xnm, ("rstd", t)], [nnm], scale=rstd[:, t:t + 1])
        ps, pn = pb()
        for j in range(8):
            tr(ps[:, j * 128:(j + 1) * 128], nb_[:, j * 128:(j + 1) * 128], ident[:], [nnm, "ident"], [pn])
        hsl = hT[:, :, t * 128:(t + 1) * 128]
        psv = ps[:, :].rearrange("p (j t) -> p j t", t=128)
        tt("dve", hsl, psv, gs1_fp[:, 0:8].unsqueeze(2).broadcast_to([128, 8, 128]), ALU.mult, [pn, "gs1_fp"], [("hT", t)])
        tt("pool", hsl, hsl, mod_fp[:, 0:8].unsqueeze(2).broadcast_to([128, 8, 128]), ALU.add, [("hT", t), "mod_fp"], [("hT", t)])
    tap("hT", hT[:], [128, 8, SEQ], ["hT"], BF16)
    S.fence()
    A.release(m1)


    blockones = A.alloc("blockones", [128, 128], BF16)
    memset("pool", blockones[:], 0.0, ["blockones"])
    memset("pool", blockones[0:64, 0:64], 1.0, ["blockones"])
    memset("pool", blockones[64:128, 64:128], 1.0, ["blockones"])
    m3 = A.mark()
    identf = A.alloc("identf", [128, 128], F32)
    memset("pool", identf[:], 0.0, ["identf"])
    S.op("pool", lambda h: h.affine_select(out=identf[:], in_=identf[:], pattern=[[-1, 128]], compare_op=ALU.not_equal,
                                           fill=1.0, base=0, channel_multiplier=1), reads=["identf"], writes=["identf"])
    top3 = A.mark()
    A.off = m_att
    lwin = A.alloc("lwin", [128, SEQ], BF16)
    sg = A.alloc("sg", [128, SEQ], BF16)
    rT = A.alloc("rT", [128, SEQ], F32)
    kTf = A.alloc("kTf", [128, SEQ], F32)
    kkT = A.alloc("kkT", [128, SEQ], F32)
    assert A.off <= m_ht
    A.off = top3
    up_sb = A.alloc("up_sb", [128, 2, 512], BF16)
    gup = A.alloc("gup", [128, 512], BF16)
    lnx = A.alloc("lnx", [128, 2, 512], F32)
    mu = A.alloc("mu", [128, 2, 14], F32)
    c0all = A.alloc("c0all", [128, 14], F32)
    w0a0 = A.alloc("w0a0", [128, 2, 2, 4], F32)
    kkr = A.alloc("kkr", [128, 3, 4], F32)
    cpk = A.alloc("cpk", [128, 1282], BF16)
    zraw = A.alloc("zraw", [128, SEQ + 2], F32)
    wch = [A.alloc("wch%d" % i, [128, 8, 128], BF16) for i in range(2)]
    vbf = A.alloc("vbf", [128, SEQ], BF16)
    asum = A.alloc("asum", [128, SEQ], F32)
    Vtm = A.alloc("Vtm", [128, NT, 128], BF16)
    ysum = A.alloc("ysum", [128, NT, 128], F32)
    rt_ = A.alloc("rt_", [128, SEQ // 2], BF16)
    bt_ = A.alloc("bt_", [128, SEQ // 2], BF16)
    khah = A.alloc("khah", [128, NT // 2, 2, 128], BF16)
    G4 = A.alloc("G4", [128, NT // 2, 2, 4, 128], BF16)
    pcs = A.alloc("pcs", [128, NT], F32)
    Mf = A.alloc("Mf", [128, 64], F32)
    Mbf = A.alloc("Mbf", [128, 64], BF16)
    Xsb = A.alloc("Xsb", [128, 2, 64], BF16)
    Usb = A.alloc("Usb", [128, 2, 64], BF16)
    sig_b = A.alloc("sig_b", [128, 512], F32)
    a_b = A.alloc("a_b", [128, 512], F32)
    cs_b = A.alloc("cs_b", [128, 512], F32)
    e1_b = A.alloc("e1_b", [128, 512], F32)
    tmc_b = A.alloc("tmc_b", [128, 512], F32)
    tme_b = A.alloc("tme_b", [128, 512], F32)
    kd_b = A.alloc("kd_b", [128, 512], F32)
    akk_b = A.alloc("akk_b", [128, 512], F32)
    ex_b = [A.alloc("ex_b%d" % i, [128, 512], F32) for i in range(2)]
    kt_b = A.alloc("kt_b", [128, 512], BF16)
    at_b = A.alloc("at_b", [128, 512], BF16)
    khT_b = A.alloc("khT_b", [128, 512], BF16)
    ahT_b = A.alloc("ahT_b", [128, 512], BF16)
    sq_b = A.alloc("sq_b", [128, 512], BF16)
    rs_b = A.alloc("rs_b", [128, 512], F32)
    Awk = [A.alloc("Awk%d" % i, [128, 8, 128], F32) for i in range(2)]
    Bwk = [A.alloc("Bwk%d" % i, [128, 8, 128], F32) for i in range(2)]
    Sf = A.alloc("Sf", [128, 8, 128], F32)
    gn1 = A.alloc("gn1", [128, 32], F32)
    gn2 = A.alloc("gn2", [128, 32], F32)
    coef = A.alloc("coef", [128, 32], F32)
    rwo = A.alloc("rwo", [128, NT, 128], BF16)

    dma("pool", up_sb[:], up_d, [], ["up_sb"])
    dma("pool", gup[:], gup_d, [], ["gup"])
    dma("sp", lnx[:], lnx_d, [], ["lnx"])
    dma("sp", mu[:], mu_d, [], ["mu"])
    dma("sp", w0a0[:], w0a0_d, [], ["w0a0"])
    dma("sp", kkr[:], kkr_d, [], ["kkr"])
    dma("sp", cpk[:], cpk_d, [], ["cpk"])
    msk4 = lambda dr: cpk[:, dr * 640:dr * 640 + 512]
    mskA = lambda dr: cpk[:, dr * 640 + 512:dr * 640 + 640]
    hsel = cpk[:, 1280:1282]
    tt("dve", c0all[:], mu[:, 0, :], mu[:, 1, :], ALU.add, ["mu"], ["c0all"])
    ts("dve", c0all[:], c0all[:], -1.0, 1.0, ALU.mult, ALU.add, ["c0all"], ["c0all"])
    memset("pool", zraw[:, 0:1], 0.0, ["zraw"])
    memset("pool", zraw[:, SEQ + 1:SEQ + 2], 0.0, ["zraw"])
    win_v3 = win_d.rearrange("(j p) n -> p j n", p=128)
    wcc = [0]

    def zr_chunk(c, dst, dn):
        wi = wcc[0] % 2
        wcc[0] += 1
        col = 1536 + 128 * c
        dma("pool", wch[wi][:], win_v3[:, :, col:col + 128], [], ["wch%d" % wi])
        for blk in range(4):
            ps, pn = pf()
            for j in range(8):
                mm(ps[:, :], wch[wi][:, j, :], hT[:, j, blk * 512:(blk + 1) * 512], j == 0, j == 7, ["wch%d" % wi, "hT"], [pn])
            cp("act", zraw[:, 1 + blk * 512:1 + (blk + 1) * 512], ps[:, :], [pn], [("zraw", blk)])
        act(dst[:], zraw[:, 1:SEQ + 1], AF.Copy, ["zraw", "c0all"], [dn], scale=c0all[:, c:c + 1])
        stt(dst[:], zraw[:, 0:SEQ], mu[:, 0, c:c + 1], dst[:], ALU.mult, ALU.add, ["zraw", "mu", dn], [dn])
        stt(dst[:], zraw[:, 2:SEQ + 2], mu[:, 1, c:c + 1], dst[:], ALU.mult, ALU.add, ["zraw", "mu", dn], [dn])

    zr_chunk(12, rT, "rT")
    act(lwin[0:64, :], rT[0:64, :], AF.Tanh, ["rT"], [("lwin", 0)])
    cp("dve", lwin[64:128, :], rT[64:128, :], ["rT"], [("lwin", 1)])
    zr_chunk(13, kTf, "kTf")
    act(sg[:], kTf[:], AF.Sigmoid, ["kTf"], ["sg"])

    tap("lwin", lwin[:], [128, SEQ], ["lwin"], BF16)
    tap("sg", sg[:], [128, SEQ], ["sg"], BF16)
    for hp in range(4):
        zr_chunk(hp, rT, "rT")
        if hp == 0:
            tap("zr0", rT[:], [128, SEQ], ["rT"])
        zr_chunk(4 + hp, kTf, "kTf")
        zr_chunk(8 + hp, kkT, "kkT")
        cp("act", vbf[:], kkT[:], ["kkT"], ["vbf"])
        for half in range(2):
            ps, pn = pb()
            for ti in range(8):
                t = half * 8 + ti
                tr(ps[:, ti * 128:(ti + 1) * 128], vbf[:, t * 128:(t + 1) * 128], ident[:], ["vbf", "ident"], [pn])
            cp("dve", Vtm[:, half * 8:(half + 1) * 8, :], ps[:, :].rearrange("p (t f) -> p t f", f=128), [pn], ["Vtm"])
        ts("dve", kkT[:], kTf[:], kkr[:, 0, hp:hp + 1], None, ALU.mult, None, ["kTf", "kkr", "vbf"], ["kkT"])
        for blk in range(4):
            tsl = slice(blk * 512, (blk + 1) * 512)
            act(sq_b[:], kkT[:, tsl], AF.Square, ["kkT"], ["sq_b"])
            ps, pn = pf()
            mm(ps[:, :], blockones[:], sq_b[:], True, True, ["blockones", "sq_b"], [pn])
            act(rs_b[:], ps[:, :], AF.Sqrt, [pn], ["rs_b"])
            ts("dve", rs_b[:], rs_b[:], 1e-6, None, ALU.max, None, ["rs_b"], ["rs_b"])
            S.op("dve", (lambda o: (lambda h: h.reciprocal(out=o, in_=o)))(rs_b[:]), reads=["rs_b"], writes=["rs_b"])
            tt("dve", kkT[:, tsl], kkT[:, tsl], rs_b[:], ALU.mult, ["kkT", "rs_b"], ["kkT"])
        if hp == 0:
            tap("kk0", kkT[:], [128, SEQ], ["kkT"])
            tap("vtm0", Vtm[:], [128, NT, 128], ["Vtm"], BF16)
        for dr in range(2):
            cdec = CDEC
            memset("pool", Mf[:], 0.0, ["Mf"])
            memset("pool", Mbf[:], 0.0, ["Mbf"])
            for hf in ((0, 1) if dr == 0 else (1, 0)):
                for blk in (2 * hf, 2 * hf + 1):
                    lb = blk % 2
                    hsl_ = slice(lb * 512, (lb + 1) * 512)
                    tsl = slice(blk * 512, (blk + 1) * 512)
                    ps2, pns = pf2()
                    mm(ps2[:, 0, :], up_sb[0:64, dr, hp * 128:(hp + 1) * 128], lwin[0:64, tsl], True, True, ["up_sb", "lwin"], [pns[0]])
                    mm(ps2[:, 1, :], up_sb[64:128, dr, hp * 128:(hp + 1) * 128], lwin[64:128, tsl], True, True, ["up_sb", "lwin"], [pns[1]])
                    act(sig_b[:], ps2[:, 0, :], AF.Sigmoid, [pns[0], "w0a0"], ["sig_b"], bias=w0a0[:, 0, dr, hp:hp + 1])
                    act(a_b[:], ps2[:, 1, :], AF.Sigmoid, [pns[1], "w0a0"], ["a_b"], bias=w0a0[:, 1, dr, hp:hp + 1])
                    if hp == 0 and blk == 0:
                        tap("sig%d" % dr, sig_b[:], [128, 512], ["sig_b"])
                        tap("a%d" % dr, a_b[:], [128, 512], ["a_b"])
                    if dr == 0:
                        cp("pool", asum[:, tsl], a_b[:], ["a_b"], [("asum", blk)])
                    else:
                        tt("pool", asum[:, tsl], asum[:, tsl], a_b[:], ALU.add, ["a_b", ("asum", blk)], [("asum", blk)])
                    for ti in range(4):
                        S.op("dve", (lambda o, d1: (lambda h: h.tensor_tensor_scan(out=o, data0=ones_f[:, 0:128], data1=d1, initial=0.0,
                                                                                  op0=ALU.mult, op1=ALU.add)))(
                            cs_b[:, ti * 128:(ti + 1) * 128], sig_b[:, ti * 128:(ti + 1) * 128]),
                            reads=["sig_b", "ones_f"], writes=[("cs_b", ti)])
                    tt("dve", e1_b[:], cs_b[:], sig_b[:], ALU.subtract, ["cs_b", "sig_b"], ["e1_b"])
                    csv = cs_b[:].rearrange("p (c t) -> p c t", t=128)
                    totb = csv[:, :, 127:128].broadcast_to([128, 4, 128])
                    tt("dve", tmc_b[:].rearrange("p (c t) -> p c t", t=128), totb, csv, ALU.subtract, ["cs_b"], ["tmc_b"])
                    if dr == 1:
                        tt("dve", tme_b[:].rearrange("p (c t) -> p c t", t=128), totb, e1_b[:].rearrange("p (c t) -> p c t", t=128),
                           ALU.subtract, ["cs_b", "e1_b"], ["tme_b"])
                        pin, pinn, pex, pexn, prem, premn = tme_b, "tme_b", tmc_b, "tmc_b", e1_b, "e1_b"
                    else:
                        pin, pinn, pex, pexn, prem, premn = cs_b, "cs_b", e1_b, "e1_b", tmc_b, "tmc_b"
                    act(pcs[:, blk * 4:(blk + 1) * 4], csv[:, :, 127], AF.Exp, ["cs_b"], [("pcs", blk)], scale=-cdec)
                    ts("dve", kd_b[:], a_b[:], -1.0, kkr[:, 1, hp:hp + 1], ALU.add, ALU.mult, ["a_b", "kkr"], ["kd_b"])
                    stt(kd_b[:], kd_b[:], 1.0, kTf[:, tsl], ALU.add, ALU.mult, ["kd_b", "kTf"], ["kd_b"])
                    tt("pool", akk_b[:], a_b[:], kkT[:, tsl], ALU.mult, ["a_b", "kkT"], ["akk_b"])
                    act(ex_b[0][:], pin[:], AF.Exp, [pinn], ["ex_b0"], scale=-cdec)
                    tt("pool", rt_[:, hsl_], rT[:, tsl], ex_b[0][:], ALU.mult, ["rT", "ex_b0"], [("rt_", lb)])
                    act(ex_b[1][:], pex[:], AF.Exp, [pexn], ["ex_b1"], scale=-cdec)
                    tt("pool", bt_[:, hsl_], kkT[:, tsl], ex_b[1][:], ALU.mult, ["kkT", "ex_b1"], [("bt_", lb)])
                    act(ex_b[0][:], pin[:], AF.Exp, [pinn], ["ex_b0"], scale=cdec)
                    tt("pool", kt_b[:], kd_b[:], ex_b[0][:], ALU.mult, ["kd_b", "ex_b0"], ["kt_b"])
                    stt(at_b[:], akk_b[:], -1.0, ex_b[0][:], ALU.mult, ALU.mult, ["akk_b", "ex_b0"], ["at_b"])
                    act(ex_b[1][:], prem[:], AF.Exp, [premn], ["ex_b1"], scale=-cdec)
                    tt("pool", khT_b[:], kd_b[:], ex_b[1][:], ALU.mult, ["kd_b", "ex_b1"], ["khT_b"])
                    stt(ahT_b[:], akk_b[:], -1.0, ex_b[1][:], ALU.mult, ALU.mult, ["akk_b", "ex_b1"], ["ahT_b"])
                    ps, pn = pb()
                    for ti in range(4):
                        tr(ps[:, (ti * 2) * 128:(ti * 2 + 1) * 128], khT_b[:, ti * 128:(ti + 1) * 128], ident[:], ["khT_b", "ident"], [pn])
                        tr(ps[:, (ti * 2 + 1) * 128:(ti * 2 + 2) * 128], ahT_b[:, ti * 128:(ti + 1) * 128], ident[:], ["ahT_b", "ident"], [pn])
                    cp("act", khah[:, lb * 4:(lb + 1) * 4, :, :], ps[:, :].rearrange("p (t q f) -> p t q f", t=4, q=2), [pn], [("khah", lb)])
                    for ti in range(4):
                        t = lb * 4 + ti
                        fsl = slice(t * 128, (t + 1) * 128)
                        lsl = slice(ti * 128, (ti + 1) * 128)
                        ps2, pns = pf2()
                        for hh in range(2):
                            hs = slice(hh * 64, (hh + 1) * 64)
                            mm(ps2[:, hh, 0:128], at_b[hs, lsl], bt_[hs, fsl], True, True, ["at_b", ("bt_", lb)], [pns[hh]])
                            mm(ps2[:, hh, 128:256], kt_b[hs, lsl], bt_[hs, fsl], True, True, ["kt_b", ("bt_", lb)], [pns[hh]])
                            mm(ps2[:, hh, 256:384], at_b[hs, lsl], rt_[hs, fsl], True, True, ["at_b", ("rt_", lb)], [pns[hh]])
                            mm(ps2[:, hh, 384:512], kt_b[hs, lsl], rt_[hs, fsl], True, True, ["kt_b", ("rt_", lb)], [pns[hh]])
                        tt("dve", G4[:, t, :, :, :].rearrange("p h f q -> p h (f q)"), ps2[:, :, :],
                           msk4(dr).unsqueeze(1).broadcast_to([128, 2, 512]), ALU.mult, pns + ["cpk"], [("G4", t)])
                        ps3, pns3 = pf2()
                        for hh in range(2):
                            hs = slice(hh * 64, (hh + 1) * 64)
                            mm(ps3[:, hh, 0:128], bt_[hs, fsl], at_b[hs, lsl], True, True, ["at_b", ("bt_", lb)], [pns3[hh]])
                        tt("dve", Awk[0][:, ti * 2:ti * 2 + 2, :], ps3[:, :, 0:128], mskA(dr).unsqueeze(1).broadcast_to([128, 2, 128]),
                           ALU.mult, pns3 + ["cpk"], ["Awk0"])
                        tt("dve", Bwk[0][:, ti * 2:ti * 2 + 2, :], ps2[:, :, 0:128], msk4(dr)[:, 0:128].unsqueeze(1).broadcast_to([128, 2, 128]),
                           ALU.mult, pns + ["cpk"], ["Bwk0"])
                    tt("pool", Sf[:], identf[:].unsqueeze(1).broadcast_to([128, 8, 128]), Bwk[0][:], ALU.add, ["identf", "Bwk0"], ["Sf"])
                    cur = 0
                    for lev in range(1, NLEV + 1):
                        nxt = 1 - cur
                        an_c, bn_c, an_n, bn_n = "Awk%d" % cur, "Bwk%d" % cur, "Awk%d" % nxt, "Bwk%d" % nxt
                        for hb in range(2):
                            psA, pnA = pf()
                            for ii in range(4):
                                i_ = hb * 4 + ii
                                mm(psA[:, ii * 128:(ii + 1) * 128], Bwk[cur][:, i_, :], Awk[cur][:, i_, :], True, True, [an_c, bn_c], [pnA])
                            cp("act", Awk[nxt][:, hb * 4:(hb + 1) * 4, :], psA[:, :].rearrange("p (i f) -> p i f", f=128), [pnA], [(an_n, hb)])
                            if lev < NLEV:
                                psB, pnB = pf()
                                for ii in range(4):
                                    i_ = hb * 4 + ii
                                    mm(psB[:, ii * 128:(ii + 1) * 128], Awk[cur][:, i_, :], Bwk[cur][:, i_, :], True, True, [an_c, bn_c], [pnB])
                                cp("dve", Bwk[nxt][:, hb * 4:(hb + 1) * 4, :], psB[:, :].rearrange("p (i f) -> p i f", f=128), [pnB], [(bn_n, hb)])
                            psS, pnS = pf()
                            for ii in range(4):
                                i_ = hb * 4 + ii
                                mm(psS[:, ii * 128:(ii + 1) * 128], Awk[nxt][:, i_, :], Sf[:, i_, :], True, True, [(an_n, hb), ("Sf", hb)], [pnS])
                            if lev < NLEV:
                                tt("dve", Sf[:, hb * 4:(hb + 1) * 4, :], Sf[:, hb * 4:(hb + 1) * 4, :],
                                   psS[:, :].rearrange("p (i f) -> p i f", f=128), ALU.add, [pnS, ("Sf", hb)], [("Sf", hb)])
                            else:
                                t0 = lb * 4 + hb * 2
                                tt("dve", G4[:, t0:t0 + 2, :, 0, :], Sf[:, hb * 4:(hb + 1) * 4, :].rearrange("p (t h) f -> p t h f", h=2),
                                   psS[:, :].rearrange("p (t h f) -> p t h f", t=2, h=2), ALU.add, [pnS, ("Sf", hb)],
                                   [("G4", t0), ("G4", t0 + 1)])
                        cur = nxt
                if hp == 0 and dr == 0 and hf == 0:
                    tap("rt0", rt_[:], [128, SEQ // 2], ["rt_"], BF16)
                    tap("bt0", bt_[:], [128, SEQ // 2], ["bt_"], BF16)
                    tap("G40", G4[:], [128, NT // 2, 2, 4, 128], ["G4"], BF16)
                    tap("khah0", khah[:], [128, NT // 2, 2, 128], ["khah"], BF16)
                    tap("pcs0", pcs[:], [128, NT], ["pcs"])
                order = list(range(8 * hf, 8 * hf + 8)) if dr == 0 else list(range(8 * hf + 7, 8 * hf - 1, -1))
                for t in order:
                    tl = t - 8 * hf
                    fsl = slice(tl * 128, (tl + 1) * 128)
                    blk = t // 4
                    lb = blk % 2
                    psX, pnsX = pf2()
                    for hh in range(2):
                        hs = slice(hh * 64, (hh + 1) * 64)
                        mm(psX[:, hh, 0:64], bt_[hs, fsl], Mbf[hs, :], True, False, [("bt_", lb), "Mbf"], [pnsX[hh]])
                        mm(psX[:, hh, 0:64], G4[:, tl, hh, 1, :], Vtm[:, t, hs], False, True, [("G4", tl), "Vtm"], [pnsX[hh]])
                    cp("act", Xsb[:], psX[:, :, 0:64], pnsX, ["Xsb"])
                    psU, pnU = pf()
                    for hh in range(2):
                        mm(psU[:, hh * 64:(hh + 1) * 64], G4[:, tl, hh, 0, :], Xsb[:, hh, :], True, True, [("G4", tl), "Xsb"], [pnU])
                    cp("dve", Usb[:], psU[:, 0:128].rearrange("p (h v) -> p h v", h=2), [pnU], ["Usb"])
                    psY, pnsY = pf2()
                    for hh in range(2):
                        hs = slice(hh * 64, (hh + 1) * 64)
                        mm(psY[:, hh, 0:64], rt_[hs, fsl], Mbf[hs, :], True, False, [("rt_", lb), "Mbf"], [pnsY[hh]])
                        mm(psY[:, hh, 0:64], G4[:, tl, hh, 2, :], Usb[:, hh, :], False, False, [("G4", tl), "Usb"], [pnsY[hh]])
                        mm(psY[:, hh, 0:64], G4[:, tl, hh, 3, :], Vtm[:, t, hs], False, True, [("G4", tl), "Vtm"], [pnsY[hh]])
                    yv = ysum[:, t, :].rearrange("p (h v) -> p h v", h=2)
                    if dr == 0:
                        cp("act", yv, psY[:, :, 0:64], pnsY, [("ysum", t)])
                    else:
                        tt("dve", yv, yv, psY[:, :, 0:64], ALU.add, pnsY + [("ysum", t)], [("ysum", t)])
                    psM, pnM = pf()
                    for hh in range(2):
                        hs = slice(hh * 64, (hh + 1) * 64)
                        mm(psM[:, hs], khah[:, tl, 1, :], Usb[:, hh, :], True, False, [("khah", lb), "Usb"], [pnM])
                        mm(psM[:, hs], khah[:, tl, 0, :], Vtm[:, t, hs], False, True, [("khah", lb), "Vtm"], [pnM])
                    for hh in range(2):
                        hs = slice(hh * 64, (hh + 1) * 64)
                        stt(Mf[hs, :], Mf[hs, :], pcs[hs, t:t + 1], psM[hs, hs], ALU.mult, ALU.add, [("Mf", hh), ("pcs", blk), pnM], [("Mf", hh)])
                    cp("act", Mbf[:], Mf[:], ["Mf"], ["Mbf"])
            if hp == 0:
                tap("ys%d" % dr, ysum[:], [128, NT, 128], ["ysum"])
        yv3 = ysum[:].rearrange("p t (h v) -> p (t h) v", h=2)
        S.op("dve", lambda h: h.tensor_reduce(out=gn1[:], in_=yv3, axis=AX.X, op=ALU.add), reads=["ysum"], writes=["gn1"])
        ts("dve", gn1[:], gn1[:], 1.0 / 64, None, ALU.mult, None, ["gn1"], ["gn1"])
        tt("dve", yv3, yv3, gn1[:].unsqueeze(2).broadcast_to([128, 32, 64]), ALU.subtract, ["ysum", "gn1"], ["ysum"])
        kd4 = kkT[:].rearrange("p (t v) -> p t v", v=64)
        tt("pool", kd4, yv3, yv3, ALU.mult, ["ysum"], ["kkT"])
        S.op("dve", lambda h: h.tensor_reduce(out=gn2[:], in_=kd4, axis=AX.X, op=ALU.add), reads=["kkT"], writes=["gn2"])
        ts("dve", gn2[:], gn2[:], 1.0 / 64, 64e-5, ALU.mult, ALU.add, ["gn2"], ["gn2"])
        act(gn2[:], gn2[:], AF.Sqrt, ["gn2"], ["gn2"])
        S.op("dve", (lambda o: (lambda h: h.reciprocal(out=o, in_=o)))(gn2[:]), reads=["gn2"], writes=["gn2"])
        tt("dve", yv3, yv3, gn2[:].unsqueeze(2).broadcast_to([128, 32, 64]), ALU.mult, ["ysum", "gn2"], ["ysum"])
        y3 = ysum[:]
        tt("dve", y3, y3, lnx[:, 0, hp * 128:(hp + 1) * 128].unsqueeze(1).broadcast_to([128, NT, 128]), ALU.mult, ["ysum", "lnx"], ["ysum"])
        tt("pool", y3, y3, lnx[:, 1, hp * 128:(hp + 1) * 128].unsqueeze(1).broadcast_to([128, NT, 128]), ALU.add, ["ysum", "lnx"], ["ysum"])
        ts("dve", asum[:], asum[:], 0.5, -1.0, ALU.mult, ALU.add, ["asum"], ["asum"])
        ts("dve", asum[:], asum[:], kkr[:, 1, hp:hp + 1], 1.0, ALU.mult, ALU.add, ["asum", "kkr"], ["asum"])
        tt("pool", asum[:], asum[:], kTf[:], ALU.mult, ["asum", "kTf"], ["asum"])
        tt("pool", asum[:], asum[:], rT[:], ALU.mult, ["asum", "rT"], ["asum"])
        ts("dve", vbf[:], asum[:], kkr[:, 2, hp:hp + 1], None, ALU.mult, None, ["asum", "kkr", "Vtm"], ["vbf"])
        ps, pn = pf()
        for t in range(NT):
            mm(ps[:, t * 2:t * 2 + 2], vbf[:, t * 128:(t + 1) * 128], hsel, True, True, ["vbf", "cpk"], [pn])
        cp("dve", coef[:], ps[:, 0:32], [pn], ["coef"])
        tt("dve", kd4, Vtm[:].rearrange("p t (h v) -> p (t h) v", h=2), coef[:].unsqueeze(2).broadcast_to([128, 32, 64]), ALU.mult,
           ["Vtm", "coef", "kkT"], ["kkT"])
        tt("pool", yv3, yv3, kd4, ALU.add, ["ysum", "kkT"], ["ysum"])
        for g4 in range(4):
            ps, pn = pf()
            for ti in range(4):
                t = g4 * 4 + ti
                mm(ps[:, ti * 128:(ti + 1) * 128], sg[:, t * 128:(t + 1) * 128], gup[:, hp * 128:(hp + 1) * 128], True, True, ["sg", "gup"], [pn])
            tt("dve", rwo[:, g4 * 4:(g4 + 1) * 4, :], ysum[:, g4 * 4:(g4 + 1) * 4, :], ps[:, :].rearrange("p (t f) -> p t f", f=128), ALU.mult,
               ["ysum", pn], [("rwo", g4)])
        for half in range(2):
            ps, pn = pb()
            for ti in range(8):
                t = half * 8 + ti
                tr(ps[:, ti * 128:(ti + 1) * 128], rwo[:, t, :], ident[:], ["rwo", "ident"], [pn])
            cp("act", rwT[:, hp, half * 1024:(half + 1) * 1024], ps[:, :], [pn], [("rwT", hp, half)])
    tap("rwT", rwT[:], [128, 4, SEQ], ["rwT"], BF16)
    S.fence()
    A.release(m3)

    qkg = A.alloc("qkg", [128, 2], F32)
    dma("sp", qkg[:], qkg_d, [], ["qkg"])
    ts("dve", qkg[:, 0:1], qkg[:, 0:1], 0.125, None, ALU.mult, None, ["qkg"], ["qkg"])
    NVT = 17 + 20 + 16
    m2 = A.mark()
    Vaug = A.alloc("Vaug", [128, NVT, 2, 65], BF16)
    memset("pool", Vaug[:, :, :, 64:65], 1.0, ["Vaug"])
    qT = A.alloc("qT", [128, SEQ], BF16)
    kT = A.alloc("kT", [128, SEQ], BF16)
    wq = [A.alloc("wqkv%d" % i, [128, 8, 128], BF16) for i in range(3)]
    acc = A.alloc("acc", [128, 2, SEQ], F32)
    EB = A.alloc("EB", [128, 7, 2, 2, 128], BF16)
    sqb = [A.alloc("sqb%d" % i, [128, 512], BF16) for i in range(2)]
    rsb = [A.alloc("rsb%d" % i, [128, 512], F32) for i in range(2)]
    Eb = [A.alloc("Eb%d" % i, [128, 2, 2, 128], BF16) for i in range(3)]
    PTb = [A.alloc("PTb%d" % i, [128, 2, 2, 128], BF16) for i in range(3)]
    win_v = win_d.rearrange("(j p) n -> p j n", p=128)
    PATS = ((1, 0), (4, 17), (16, 37))
    blkc = [0]
    for hp in range(4):
        dma("sp", EB[:], eb_d[hp].rearrange("p (s h k q) -> p s h k q", s=7, h=2, k=2), [], ["EB"])
        for i, c0 in enumerate((hp * 128, 512 + hp * 128, 1024 + hp * 128)):
            dma("pool", wq[i][:], win_v[:, :, c0:c0 + 128], [], ["wqkv%d" % i])
        for i, (dst, dn) in enumerate(((qT, "qT"), (kT, "kT"))):
            for blk in range(4):
                tsl = slice(blk * 512, (blk + 1) * 512)
                ps, pn = pf()
                for j in range(8):
                    mm(ps[:, :], wq[i][:, j, :], hT[:, j, tsl], j == 0, j == 7, ["wqkv%d" % i, "hT"], [pn])
                bi = blkc[0] % 2
                blkc[0] += 1
                act(sqb[bi][:], ps[:, :], AF.Square, [pn], ["sqb%d" % bi])
                ps2, pn2 = pf()
                mm(ps2[:, :], blockones[:], sqb[bi][:], True, True, ["blockones", "sqb%d" % bi], [pn2])
                act(rsb[bi][:], ps2[:, :], AF.Sqrt, [pn2], ["rsb%d" % bi], bias=1e-6, scale=1.0 / 64)
                S.op("dve", (lambda o: (lambda h: h.reciprocal(out=o, in_=o)))(rsb[bi][:]), reads=["rsb%d" % bi], writes=["rsb%d" % bi])
                stt(dst[:, tsl], ps[:, :], qkg[:, i:i + 1], rsb[bi][:], ALU.mult, ALU.mult, [pn, "qkg", "rsb%d" % bi], [(dn, blk)])
        vtl = []
        for (dil, base) in PATS:
            L = SEQ // dil
            nb = L // 128
            for r in range(dil):
                if nb == 1:
                    vtl.append((base + r, r, dil))
                else:
                    for m in range(nb + 1):
                        l0 = 0 if m == 0 else (L - 128 if m == nb else 128 * m - 64)
                        vtl.append((base + r * (nb + 1) + m, r + dil * l0, dil))
        assert len(vtl) == NVT and [v[0] for v in vtl] == list(range(NVT))
        for g0 in range(0, NVT, 4):
            grp = vtl[g0:g0 + 4]
            ps, pn = pf()
            for gi, (vt, st, dil) in enumerate(grp):
                for j in range(8):
                    mm(ps[:, gi * 128:(gi + 1) * 128], hT[:, j, st:st + 127 * dil + 1:dil], wq[2][:, j, :], j == 0, j == 7,
                       ["hT", "wqkv2"], [pn])
            n = len(grp)
            cp("act", Vaug[:, g0:g0 + n, :, 0:64], ps[:, 0:n * 128].rearrange("p (t h e) -> p t h e", t=n, h=2),
               [pn], [("Vaug", g0 // 4)])
        for pi, (dil, base) in enumerate(PATS):
            L = SEQ // dil
            nb = L // 128
            for r in range(dil):
                for qb in range(nb):
                    qs = r + dil * 128 * qb
                    qsl = slice(qs, qs + 127 * dil + 1, dil)
                    if nb == 1:
                        kts = [(base + r, r)]
                        eset = 6
                    else:
                        mA, mB = qb, qb + 1
                        lA = 0 if qb == 0 else 128 * qb - 64
                        lB = 128 * qb + 64 if qb < nb - 1 else L - 128
                        kts = [(base + r * (nb + 1) + mA, r + dil * lA), (base + r * (nb + 1) + mB, r + dil * lB)]
                        eset = pi * 3 + (0 if qb == 0 else (2 if qb == nb - 1 else 1))
                    nk = len(kts)
                    ps, pns = pf2()
                    for hh in range(2):
                        for kt, (vt, ks) in enumerate(kts):
                            mm(ps[:, hh, kt * 128:(kt + 1) * 128], kT[hh * 64:(hh + 1) * 64, ks:ks + 127 * dil + 1:dil],
                               qT[hh * 64:(hh + 1) * 64, qsl], True, True, ["kT", "qT"], [pns[hh]])
                    bi = blkc[0] % 3
                    blkc[0] += 1
                    psv = ps[:, :, 0:256].rearrange("p h (k q) -> p h k q", k=2)[:, :, 0:nk, :]
                    act(Eb[bi][:, :, 0:nk, :], psv, AF.Exp, pns, ["Eb%d" % bi])
                    tt("pool", PTb[bi][:, :, 0:nk, :], Eb[bi][:, :, 0:nk, :], EB[:, eset, :, 0:nk, :], ALU.mult,
                       ["Eb%d" % bi, "EB"], ["PTb%d" % bi])
                    ps2, pn2 = pf()
                    for hh in range(2):
                        for kt, (vt, ks) in enumerate(kts):
                            mm(ps2[0:65, hh * 128:(hh + 1) * 128], Vaug[:, vt, hh, :], PTb[bi][:, hh, kt, :], kt == 0,
                               kt == nk - 1, [("Vaug", vt // 4), "PTb%d" % bi], [pn2])
                    asl = acc[0:65, :, qsl]
                    p2v = ps2[0:65, 0:256].rearrange("p (h q) -> p h q", h=2)
                    if pi == 0:
                        cp("dve", asl, p2v, [pn2], [("acc", pi, r, qb)])
                    else:
                        tt("dve", asl, p2v, asl, ALU.add, [pn2, "acc"], ["acc"])
        for hh in range(2):
            for blk in range(4):
                tsl = slice(blk * 512, (blk + 1) * 512)
                ps, pn = pf()
                mm(ps[0:64, :], ones_f[64:65, 0:64], acc[64:65, hh, tsl], True, True, ["ones_f", "acc"], [pn])
                bi = blkc[0] % 2
                blkc[0] += 1
                S.op("dve", (lambda o, i_: (lambda h: h.reciprocal(out=o, in_=i_)))(rsb[bi][0:64, :], ps[0:64, :]),
                     reads=[pn], writes=["rsb%d" % bi])
                tt("dve", attT[0:64, 2 * hp + hh, tsl], acc[0:64, hh, tsl], rsb[bi][0:64, :], ALU.mult,
                   ["acc", "rsb%d" % bi], [("attT", hp, hh, blk)])
    tap("attT", attT[0:64], [64, 8, SEQ], ["attT"], BF16)
    S.fence()
    A.release(m2)

    A.release(m_ht)
    bc3 = A.alloc("bc3", [128, 3, D], F32)
    dma("sp", bc3[:, 0, :], bc_d[1], ["bc_d"], [("bc3", 0)])
    dma("sp", bc3[:, 1, :], bc_d[2], ["bc_d"], [("bc3", 1)])
    dma("sp", bc3[:, 2, :], bc_d[3], ["bc_d"], [("bc3", 2)])
    m4 = A.mark()
    woutA = A.alloc("woutA", [64, 8, D], BF16)
    woutR = A.alloc("woutR", [128, 4, D], BF16)
    g1b = A.alloc("g1b", [128, D], F32)
    xts = [A.alloc("xts%d" % i, [128, D], F32) for i in range(2)]
    dma("sp", g1b[:], bc_d[0], ["bc_d"], ["g1b"])
    dma("pool", woutA[:], wout_d[0:512, :].rearrange("(h p) n -> p h n", p=64), [], ["woutA"])
    dma("pool", woutR[:], wout_d[512:1024, :].rearrange("(c p) n -> p c n", p=128), [], ["woutR"])
    tt("pool", woutA[:], woutA[:], g1b[0:64, :].unsqueeze(1).broadcast_to([64, 8, D]), ALU.mult, ["woutA", "g1b"], ["woutA"])
    tt("pool", woutR[:], woutR[:], g1b[:].unsqueeze(1).broadcast_to([128, 4, D]), ALU.mult, ["woutR", "g1b"], ["woutR"])
    for t in range(NT):
        fsl = slice(t * 128, (t + 1) * 128)
        xb_, xnm = xts[t % 2], "xts%d" % (t % 2)
        dma("sp", xb_[:], x_v[t], [], [xnm])
        for nb in range(2):
            ps, pn = pf()
            csl = slice(nb * 512, (nb + 1) * 512)
            for h_ in range(8):
                mm(ps[:, :], attT[0:64, h_, fsl], woutA[0:64, h_, csl], h_ == 0, False, ["attT", "woutA"], [pn])
            for c_ in range(4):
                mm(ps[:, :], rwT[:, c_, fsl], woutR[:, c_, csl], False, c_ == 3, ["rwT", "woutR"], [pn])
            tt("dve", xb_[:, csl], ps[:, :], xb_[:, csl], ALU.add, [pn, xnm], [xnm])
        dma("sp", x1_d[t], xb_[:], [xnm], [("x1_d", t)])
    S.fence()
    A.release(m4)

    top4 = A.mark()
    A.off = m_rw
    wq_bf = A.alloc("wq_bf", [128, 8, 2048], BF16)
    skT = A.alloc("skT", [128, 2, 128], BF16)
    iota16 = A.alloc("iota16", [128, 16], F32)
    h2T = A.alloc("h2T", [128, 8, 128], BF16)
    qTs = A.alloc("qTs", [128, 16, 128], BF16)
    junkb = A.alloc("junkb", [128, D], BF16)
    junka = A.alloc("junka", [128, D], BF16)
    h2bf = [A.alloc("h2bf%d" % i, [128, D], BF16) for i in range(2)]
    assert A.off <= m_ht
    A.off = top4
    dma("pool", wq_bf[:], wqry_d.rearrange("(j p) n -> p j n", p=128), [], ["wq_bf"])
    dma("pool", skT[:], skT_d, [], ["skT"])
    dma("sp", iota16[:], iota_d, [], ["iota16"])
    h2 = A.alloc("h2", [128, D], F32)
    x1b = [A.alloc("x1b%d" % i, [128, D], F32) for i in range(2)]
    bigA = A.alloc("bigA", [128, 2048], F32)
    wk = [A.alloc("wk%d" % i, [128, 128], F32) for i in range(4)]
    wk2 = [A.alloc("wk2%d" % i, [128, 256], F32) for i in range(4)]
    vals = A.alloc("vals", [128, 16, 16], F32)
    idxu = A.alloc("idxu", [128, 16, 16], U32)
    idxf = A.alloc("idxf", [128, 16, 16], F32)
    tops = A.alloc("tops", [128, 8, 16], F32)
    topp = A.alloc("topp", [128, 8, 16], U32)
    hiu = A.alloc("hiu", [128, 8, 16], U32)
    lou = A.alloc("lou", [128, 8, 16], U32)
    hif = A.alloc("hif", [128, 8, 16], F32)
    lof = A.alloc("lof", [128, 8, 16], F32)
    i1s = A.alloc("i1s", [128, 8, 16], F32)
    i2s = A.alloc("i2s", [128, 8, 16], F32)
    eidx = [A.alloc("eidx%d" % i, [128, 128], I32) for i in range(2)]
    gate = [A.alloc("gate%d" % i, [128, 8, 16], F32) for i in range(2)]
    gsum = A.alloc("gsum", [128, 8], F32)
    ss2 = A.alloc("ss2", [128, 1], F32)
    GJ = 4
    NSET = int(os.environ.get('MK_NSET', '6'))
    pre4 = [A.alloc("pre4_%d" % i, [128, GJ], F32) for i in range(NSET)]
    coef4 = [A.alloc("coef4_%d" % i, [128, GJ], F32) for i in range(NSET)]
    dg4 = [A.alloc("dg4_%d" % i, [128, GJ, 128], BF16) for i in range(NSET)]
    guv = [[A.alloc("guv%d_%d" % (i, jj), [128, 2 * D], BF16) for jj in range(GJ)] for i in range(NSET)]
    otile = A.alloc("otile", [128, D], F32)
    if os.environ.get("MK_VERBOSE"):
        print("P4b arena top", A.off - SB_BASE)
    out_v = out_d.rearrange("(t p) d -> t p d", p=128)
    NPF = [4]

    def pf4():
        i = rr["f"] % 4
        rr["f"] = (i + 1) % 4
        return psf[i], "psf%d" % i

    def retr(t, sl):
        x1 = x1b[sl][:]
        x1n = "x1b%d" % sl
        dma("sp", x1b[sl][:], x1_d[t], [("x1_d", t)], [x1n])
        hb, hbn = h2bf[sl], "h2bf%d" % sl
        ei, ein = eidx[sl], "eidx%d" % sl
        ga, gan = gate[sl], "gate%d" % sl
        act(junka[:], x1, AF.Square, [x1n], ["ss2"], accum=ss2[:, 0:1])
        ts("dve", ss2[:], ss2[:], 1.0 / D, 1e-6, ALU.mult, ALU.add, ["ss2"], ["ss2"])
        act(ss2[:], ss2[:], AF.Sqrt, ["ss2"], ["ss2"])
        S.op("dve", (lambda o: (lambda h: h.reciprocal(out=o, in_=o)))(ss2[:]), reads=["ss2"], writes=["ss2"])
        yield
        stt(h2[:], x1, ss2[:, 0:1], bc3[:, 0, :], ALU.mult, ALU.mult, [x1n, "ss2", ("bc3", 0)], ["h2"])
        tt("dve", h2[:], h2[:], bc3[:, 1, :], ALU.add, ["h2", ("bc3", 1)], ["h2"])
        cp("act", hb[:], h2[:], ["h2"], [hbn])
        yield
        ps, pn = pb()
        for j in range(8):
            tr(ps[:, j * 128:(j + 1) * 128], hb[:, j * 128:(j + 1) * 128], ident[:], [hbn, "ident"], [pn])
        cp("act", h2T[:], ps[:, :].rearrange("p (j t) -> p j t", t=128), [pn], ["h2T"])
        yield
        for g4 in range(4):
            ps, pn = pf4()
            for gi in range(4):
                g_ = g4 * 4 + gi
                for kc in range(8):
                    mm(ps[:, gi * 128:(gi + 1) * 128], wq_bf[:, kc, g_ * 128:(g_ + 1) * 128], h2T[:, kc, :], kc == 0, kc == 7,
                       ["wq_bf", "h2T"], [pn])
            cp("act", qTs[:, g4 * 4:(g4 + 1) * 4, :], ps[:, :].rearrange("p (g t) -> p g t", t=128), [pn], [("qTs", g4)])
            yield
        for g4 in range(4):
            ps, pn = pf4()
            for gi in range(4):
                g_ = g4 * 4 + gi
                mm(ps[:, gi * 128:(gi + 1) * 128], qTs[:, g_, :], skT[:, g_ % 2, :], True, True, [("qTs", g4), "skT"], [pn])
            cp("act", bigA[:, g4 * 512:(g4 + 1) * 512], ps[:, :], [pn], [("bigA", g4)])
            yield
        for gb in range(4):
            gs_ = [gb * 4 + i for i in range(4)]
            scs = {g_: bigA[:, g_ * 128:(g_ + 1) * 128] for g_ in gs_}
            for g_ in gs_:
                S.op("dve", (lambda o, i_: (lambda h: h.max(out=o, in_=i_)))(vals[:, g_, 0:8], scs[g_]), reads=[("bigA", gb)], writes=[("vals", g_, 0)])
            for g_ in gs_:
                S.op("dve", (lambda o, r_, i_: (lambda h: h.match_replace(out=o, in_to_replace=r_, in_values=i_, imm_value=-1e30)))(
                    wk[g_ % 4][:], vals[:, g_, 0:8], scs[g_]), reads=[("bigA", gb), ("vals", g_, 0)], writes=["wk%d" % (g_ % 4)])
            for g_ in gs_:
                S.op("dve", (lambda o, i_: (lambda h: h.max(out=o, in_=i_)))(vals[:, g_, 8:16], wk[g_ % 4][:]), reads=["wk%d" % (g_ % 4)], writes=[("vals", g_, 1)])
            for hf_ in range(2):
                for g_ in gs_:
                    S.op("dve", (lambda o, m_, i_: (lambda h: h.max_index(out=o, in_max=m_, in_values=i_)))(
                        idxu[:, g_, hf_ * 8:(hf_ + 1) * 8], vals[:, g_, hf_ * 8:(hf_ + 1) * 8], scs[g_]),
                        reads=[("bigA", gb), ("vals", g_, hf_)], writes=[("idxu", g_, hf_)])
            yield
        cp("dve", idxf[:], idxu[:], ["idxu"], ["idxf"])
        vv = vals[:].rearrange("p (h s) k -> p h s k", s=2)
        cand = bigA[:].rearrange("p (h i j) -> p h i j", h=8, i=16)
        tt("dve", cand, vv[:, :, 0, :].unsqueeze(3).broadcast_to([128, 8, 16, 16]), vv[:, :, 1, :].unsqueeze(2).broadcast_to([128, 8, 16, 16]),
           ALU.add, ["vals"], ["bigA"])
        yield
        for hb_ in range(2):
            hs_ = [hb_ * 4 + i for i in range(4)]
            cgs = {h_: bigA[:, h_ * 256:(h_ + 1) * 256] for h_ in hs_}
            for h_ in hs_:
                S.op("dve", (lambda o, i_: (lambda h: h.max(out=o, in_=i_)))(tops[:, h_, 0:8], cgs[h_]), reads=["bigA"], writes=[("tops", h_, 0)])
            for h_ in hs_:
                S.op("dve", (lambda o, r_, i_: (lambda h: h.match_replace(out=o, in_to_replace=r_, in_values=i_, imm_value=-1e30)))(
                    wk2[h_ % 4][:], tops[:, h_, 0:8], cgs[h_]), reads=["bigA", ("tops", h_, 0)], writes=["wk2%d" % (h_ % 4)])
            for h_ in hs_:
                S.op("dve", (lambda o, i_: (lambda h: h.max(out=o, in_=i_)))(tops[:, h_, 8:16], wk2[h_ % 4][:]), reads=["wk2%d" % (h_ % 4)], writes=[("tops", h_, 1)])
            for hf_ in range(2):
                for h_ in hs_:
                    S.op("dve", (lambda o, m_, i_: (lambda h: h.max_index(out=o, in_max=m_, in_values=i_)))(
                        topp[:, h_, hf_ * 8:(hf_ + 1) * 8], tops[:, h_, hf_ * 8:(hf_ + 1) * 8], cgs[h_]),
                        reads=["bigA", ("tops", h_, hf_)], writes=[("topp", h_, hf_)])
            yield
        S.op("dve", lambda h: h.tensor_single_scalar(out=hiu[:], in_=topp[:], scalar=4, op=ALU.logical_shift_right), reads=["topp"], writes=["hiu"])
        S.op("dve", lambda h: h.tensor_single_scalar(out=lou[:], in_=topp[:], scalar=15, op=ALU.bitwise_and), reads=["topp"], writes=["lou"])
        cp("dve", hif[:], hiu[:], ["hiu"], ["hif"])
        cp("dve", lof[:], lou[:], ["lou"], ["lof"])
        yield
        idv = idxf[:].rearrange("p (h s) k -> p h s k", s=2)
        eq = bigA[:].rearrange("p (h k i) -> p h k i", h=8, k=16)
        io_b = iota16[:].unsqueeze(1).unsqueeze(1).broadcast_to([128, 8, 16, 16])
        for (sel, seln, src_i, dst, dstn) in ((hif, "hif", 0, i1s, "i1s"), (lof, "lof", 1, i2s, "i2s")):
            tt("dve", eq, sel[:].unsqueeze(3).broadcast_to([128, 8, 16, 16]), io_b, ALU.is_equal, [seln, "iota16", "tops", "topp"], ["bigA"])
            yield
            tt("dve", eq, eq, idv[:, :, src_i, :].unsqueeze(2).broadcast_to([128, 8, 16, 16]), ALU.mult, ["bigA", "idxf"], ["bigA"])
            yield
            S.op("dve", (lambda o: (lambda h: h.tensor_reduce(out=o, in_=eq, axis=AX.X, op=ALU.add)))(dst[:]), reads=["bigA"], writes=[dstn])
            yield
        stt(i1s[:], i1s[:], 128.0, i2s[:], ALU.mult, ALU.add, ["i1s", "i2s"], ["i1s"])
        cp("dve", ei[:], i1s[:].rearrange("p h k -> p (h k)"), ["i1s"], [ein])
        tt("dve", ga[:], tops[:], tops[:, :, 0:1].broadcast_to([128, 8, 16]), ALU.subtract, ["tops"], [gan])
        act(ga[:], ga[:], AF.Exp, [gan], [gan])
        S.op("dve", (lambda g_: (lambda h: h.tensor_reduce(out=gsum[:], in_=g_, axis=AX.X, op=ALU.add)))(ga[:]), reads=[gan], writes=["gsum"])
        S.op("dve", lambda h: h.reciprocal(out=gsum[:], in_=gsum[:]), reads=["gsum"], writes=["gsum"])
        tt("dve", ga[:], ga[:], gsum[:].unsqueeze(2).broadcast_to([128, 8, 16]), ALU.mult, [gan, "gsum"], [gan])
        if t == 0:
            tap("eidx", ei[:], [128, 128], [ein], I32)
            tap("gate", ga[:], [128, 8, 16], [gan])
        yield

    NGRP = 128 // GJ

    def issue_gathers(t, sl, g):
        bs = (t * NGRP + g) % NSET
        for jj in range(GJ):
            j = g * GJ + jj
            S.op("pool", (lambda o, ix: (lambda h: h.indirect_dma_start(out=o, out_offset=None, in_=tab_d[:, :],
                                                                         in_offset=bass.IndirectOffsetOnAxis(ap=ix, axis=0))))(
                guv[bs][jj][:], eidx[sl][:, j:j + 1]), reads=["eidx%d" % sl, "tab"], writes=["guv%d_%d" % (bs, jj)], dma=True,
                swsem="gsem%d_%d" % (bs, jj))

    def consume_a(t, sl, g):
        bs = (t * NGRP + g) % NSET
        for jj in range(GJ):
            stt(junkb[:], guv[bs][jj][:, 0:D], 1.0, h2bf[sl][:], ALU.mult, ALU.mult, ["guv%d_%d" % (bs, jj), "h2bf%d" % sl],
                [("pre4_%d" % bs, jj)], accum=pre4[bs][:, jj:jj + 1])
        act(coef4[bs][:], pre4[bs][:], AF.Gelu, ["pre4_%d" % bs], ["coef4_%d" % bs])

    def consume_b(t, sl, g):
        bs = (t * NGRP + g) % NSET
        j0 = g * GJ
        tt("dve", coef4[bs][:], coef4[bs][:], gate[sl][:].rearrange("p h k -> p (h k)")[:, j0:j0 + GJ], ALU.mult,
           ["coef4_%d" % bs, "gate%d" % sl], ["coef4_%d" % bs])
        for jj in range(GJ):
            act(dg4[bs][:, jj, :], ident[:], AF.Copy, ["ident", "coef4_%d" % bs], [("dg4_%d" % bs, jj)], scale=coef4[bs][:, jj:jj + 1])
        for jj in range(GJ):
            j = j0 + jj
            for nb in range(2):
                mm(psf[4 + nb][:, :], dg4[bs][:, jj, :], guv[bs][jj][:, D + nb * 512:D + (nb + 1) * 512], j == 0, j == 127,
                   [("dg4_%d" % bs, jj), "guv%d_%d" % (bs, jj)], ["psf%d" % (4 + nb)])

    for _ in retr(0, 0):
        pass
    for t in range(NT):
        sl = t % 2
        nxt = retr(t + 1, (t + 1) % 2) if t + 1 < NT else None
        for g in range(NSET - 1):
            issue_gathers(t, sl, g)
        for g in range(NGRP + 1):
            if g < NGRP:
                consume_a(t, sl, g)
            if g >= 1:
                consume_b(t, sl, g - 1)
            if g + NSET - 1 < NGRP:
                issue_gathers(t, sl, g + NSET - 1)
            if nxt is not None:
                for _ in range(2):
                    try:
                        next(nxt)
                    except StopIteration:
                        nxt = None
                        break
        if nxt is not None:
            for _ in nxt:
                pass
        for nb in range(2):
            csl = slice(nb * 512, (nb + 1) * 512)
            tt("dve", otile[:, csl], psf[4 + nb][:, :], bc3[:, 2, csl], ALU.mult, ["psf%d" % (4 + nb), ("bc3", 2)], [("otile", nb)])
            tt("dve", otile[:, csl], otile[:, csl], x1b[sl][:, csl], ALU.add, [("otile", nb), "x1b%d" % sl], [("otile", nb)])
        dma("sp", out_v[t], otile[:], ["otile"], ["out_d"])
    if dbg:
        print("ops", len(S.ops), {e: S.count[e] for e in ENGS})
    S.emit()
    return taps


def host_inputs(inputs, b):
    f = lambda k: np.ascontiguousarray(inputs[k][0], dtype=np.float32)
    fp = lambda v, n: np.ascontiguousarray(v.reshape(n, 128).T)
    m = {}
    m["x"] = np.ascontiguousarray(inputs["x"][b], dtype=np.float32)
    m["c_fp"] = fp(np.asarray(inputs["c"][b], dtype=np.float32), 8)
    m["ada_w"] = f("ada_w")
    m["ada_b"] = f("ada_b").reshape(1, -1)
    m["g1_fp"] = fp(f("norm1_g"), 8)
    m["g2_fp"] = fp(f("norm2_g"), 8)
    m["g2_b"] = np.ascontiguousarray(np.broadcast_to(f("norm2_g")[None, :], (128, D)))
    m["w_in"] = f("w_in")
    m["qkg"] = np.ascontiguousarray(np.stack([np.tile(f("q_norm_g"), 2), np.tile(f("k_norm_g"), 2)], axis=1))
    m["eb"] = EB_CONST
    m["up"] = np.ascontiguousarray(np.concatenate([f("w_decay_up"), f("a_gate_up")], axis=1).transpose(1, 0, 2))
    m["g_up"] = f("g_up")
    m["lnx"] = np.ascontiguousarray(np.broadcast_to(np.stack([f("lnx_g"), f("lnx_b")])[None], (128, 2, 512)))
    m["mu"] = np.ascontiguousarray(np.stack([fp(f("mu_prev"), 14), fp(f("mu_next"), 14)], axis=1))
    m["w0a0"] = np.ascontiguousarray(np.stack([f("w_decay0").reshape(2, 4, 128), f("a_gate0").reshape(2, 4, 128)]).transpose(3, 0, 1, 2))
    m["kkr"] = np.ascontiguousarray(np.stack([fp(f("k_k"), 4), fp(f("k_a"), 4), fp(f("r_k").reshape(-1), 4)], axis=1))
    m["cpk"] = CPK_CONST
    m["w_out"] = f("w_out")
    m["w_query"] = f("peer_w_query")
    m["skT"] = np.ascontiguousarray(np.stack([f("peer_sub_keys1").T, f("peer_sub_keys2").T], axis=1))
    m["iota16"] = np.ascontiguousarray(np.broadcast_to(np.arange(16, dtype=np.float32)[None, :], (128, 16)))
    m["peer_u"] = f("peer_u")
    m["peer_v"] = f("peer_v")
    return m


def _make_cpk():
    r = np.arange(128)[:, None]
    c = np.arange(128)[None, :]
    SL, SU, IL, IU = (r > c), (r < c), (r >= c), (r <= c)
    out = np.zeros((128, 1282), np.float32)
    out[:, 0:640] = np.concatenate([SU, SU, IU, IU, SL], axis=1)
    out[:, 640:1280] = np.concatenate([SL, SL, IL, IL, SU], axis=1)
    out[0:64, 1280] = 1.0
    out[64:128, 1281] = 1.0
    return np.ascontiguousarray(out.astype(ml_dtypes.bfloat16))


CPK_CONST = _make_cpk()


def _make_eb():
    eb = np.zeros((4, 128, 7, 2, 2, 128), np.float64)
    k = np.arange(128)[:, None]
    q = np.arange(128)[None, :]
    for hp in range(4):
        for hh in range(2):
            slope = 2.0 ** (-(2 * hp + hh + 1))
            for pi, dil in enumerate((1, 4)):
                for ty in range(3):
                    if ty == 0:
                        off = q - k
                        own = k < 64
                    else:
                        off = q - (k - 64)
                        own = k >= 0
                    eb[hp, :, pi * 3 + ty, hh, 0, :] = np.where((np.abs(off) <= 64) & own, np.exp(-slope * dil * np.abs(off)), 0.0)
                    if ty == 2:
                        off = q - k
                        own = k >= 64
                    else:
                        off = q - (k + 64)
                        own = k >= 0
                    eb[hp, :, pi * 3 + ty, hh, 1, :] = np.where((np.abs(off) <= 64) & own, np.exp(-slope * dil * np.abs(off)), 0.0)
            off = q - k
            eb[hp, :, 6, hh, 0, :] = np.where(np.abs(off) <= 64, np.exp(-slope * 16 * np.abs(off)), 0.0)
    return np.ascontiguousarray(eb.reshape(4, 128, -1).astype(ml_dtypes.bfloat16))


EB_CONST = _make_eb()


def kernel(**inputs):
    nc = bass.Bass("TRN2", target_bir_lowering=False)
    build(nc)
    in_maps = [host_inputs(inputs, b) for b in range(8)]
    res = run_bass_kernel_spmd(nc, in_maps, core_ids=list(range(8)))
    return np.stack([np.asarray(r["out"], dtype=np.float32) for r in res.results], axis=0)
```
